# Optimizing a Trainium2 kernel written in Bass

```python
import math
import jax, jax.numpy as jnp
from jax import lax
import numpy as np

D_MODEL = 1024
BATCH = 2
SEQ = 8192
DEPTH = 1
DEC_BATCH = 128
DEC_SEQ = 1
PAST_LEN = 2048
PAGE_SIZE = 128

MIX_WIDTH = D_MODEL
R_WIDTH = MIX_WIDTH // 2
A_WIDTH = MIX_WIDTH - R_WIDTH
R_EXPAND = 128
R_HEADS = R_WIDTH // R_EXPAND
R_DK = R_EXPAND
R_DV = R_WIDTH // R_HEADS
A_HEADS = 4
A_DV = A_WIDTH // A_HEADS
A_DK = A_DV // 2
D_FF = -(-8 * D_MODEL // (3 * 256)) * 256
CHUNK = 64
Q_BLOCK = 128
EPS = 1e-6
PROJ_SIZES = (R_HEADS * R_DK, R_HEADS * R_DK, R_WIDTH, R_WIDTH,
              2 * A_HEADS * A_DK, 2 * A_HEADS * A_DK, A_WIDTH)
D_IN = sum(PROJ_SIZES)

kernel_name = "hymba_hgrn2_diffattn_decode_step"


def rmsnorm(x, g):
    xf = x.astype(jnp.float32)
    y = xf * lax.rsqrt(jnp.mean(xf * xf, axis=-1, keepdims=True) + EPS)
    return (y * g.astype(jnp.float32)).astype(x.dtype)


def project(h, w_in_l, lb_l):
    B, T, _ = h.shape
    p = jnp.einsum('btd,de->bte', h, w_in_l)
    offs = np.cumsum(PROJ_SIZES)[:-1].tolist()
    rq, rf, ri, rg, aq, ak, av = jnp.split(p, offs, axis=-1)
    f = lb_l + (1.0 - lb_l) * jax.nn.sigmoid(rf.astype(jnp.float32))
    logf = jnp.log(f).reshape(B, T, R_HEADS, R_DK)
    rk = (1.0 - f).reshape(B, T, R_HEADS, R_DK)
    rq = jax.nn.silu(rq.astype(jnp.float32)).reshape(B, T, R_HEADS, R_DK)
    rv = ri.astype(jnp.float32).reshape(B, T, R_HEADS, R_DV)
    aq = aq.reshape(B, T, A_HEADS, 2, A_DK)
    ak = ak.reshape(B, T, A_HEADS, 2, A_DK)
    av = av.reshape(B, T, A_HEADS, A_DV)
    return rq, rk, rv, logf, rg, aq, ak, av


def hgrn_chunked(q, k, v, logf, s0):
    B, T, H, DK = q.shape
    DV = v.shape[-1]
    n = T // CHUNK

    def to_chunks(a):
        return a.reshape(B, n, CHUNK, H, a.shape[-1]).transpose(1, 0, 3, 2, 4)

    causal = jnp.tril(jnp.ones((CHUNK, CHUNK), dtype=bool))[:, :, None]

    def step(S, blk):
        qb, kb, vb, gb = blk
        b = jnp.cumsum(gb, axis=2)
        o_inter = jnp.einsum('bhtk,bhkv->bhtv', qb * jnp.exp(b), S)
        rel = b[:, :, :, None, :] - b[:, :, None, :, :]
        decay = jnp.exp(jnp.where(causal, rel, -jnp.inf))
        A = jnp.einsum('bhtk,bhsk,bhtsk->bhts', qb, kb, decay)
        o = o_inter + jnp.einsum('bhts,bhsv->bhtv', A, vb)
        b_last = b[:, :, -1:, :]
        S_new = jnp.exp(b_last[:, :, 0, :])[..., None] * S + jnp.einsum(
            'bhsk,bhsv->bhkv', kb * jnp.exp(b_last - b), vb)
        return S_new, o

    S_fin, o = lax.scan(step, s0, (to_chunks(q), to_chunks(k), to_chunks(v), to_chunks(logf)))
    o = o.transpose(1, 0, 3, 2, 4).reshape(B, T, H, DV)
    return o, S_fin


def hgrn_recurrent(q, k, v, logf, s0):
    def step(S, xs):
        qt, kt, vt, gt = xs
        S = jnp.exp(gt)[..., None] * S + kt[..., None] * vt[..., None, :]
        return S, jnp.einsum('bhk,bhkv->bhv', qt, S)

    S_fin, o = lax.scan(step, s0, (q.swapaxes(0, 1), k.swapaxes(0, 1),
                                   v.swapaxes(0, 1), logf.swapaxes(0, 1)))
    return o.swapaxes(0, 1), S_fin


def diff_attend(q, k, v, q_pos, lam, lam_init, subln_l):
    s = jnp.einsum('bqhmd,bkhmd->bhmqk', q.astype(jnp.float32), k.astype(jnp.float32)) * (A_DK ** -0.5)
    mask = jnp.arange(k.shape[1])[None, :] <= q_pos[:, None]
    p = jax.nn.softmax(jnp.where(mask, s, -jnp.inf), axis=-1)
    attn = p[:, :, 0] - lam * p[:, :, 1]
    o = jnp.einsum('bhqk,bkhv->bqhv', attn, v.astype(jnp.float32))
    return rmsnorm(o, subln_l) * (1.0 - lam_init)


def diff_prompt(aq, ak, av, lam, lam_init, subln_l):
    B, T = aq.shape[:2]
    nb = T // Q_BLOCK
    qb = aq.reshape(B, nb, Q_BLOCK, A_HEADS, 2, A_DK).transpose(1, 0, 2, 3, 4, 5)
    pos = jnp.arange(T).reshape(nb, Q_BLOCK)
    o = lax.map(lambda a: diff_attend(a[0], ak, av, a[1], lam, lam_init, subln_l), (qb, pos))
    return o.transpose(1, 0, 2, 3, 4).reshape(B, T, A_HEADS, A_DV)


def diff_sample(aq, ak, av, cache_k_l, cache_v_l, page_table, lam, lam_init, subln_l):
    DB, T = aq.shape[:2]
    past = page_table.shape[1] * PAGE_SIZE
    k_past = cache_k_l[page_table].reshape(DB, past, A_HEADS, 2, A_DK)
    v_past = cache_v_l[page_table].reshape(DB, past, A_HEADS, A_DV)
    k = jnp.concatenate([k_past.astype(ak.dtype), ak], axis=1)
    v = jnp.concatenate([v_past.astype(av.dtype), av], axis=1)
    q_pos = past + jnp.arange(T)
    return diff_attend(aq, k, v, q_pos, lam, lam_init, subln_l)


def merge(o_r, g, o_a, r_gnorm_l, w_out_l, dtype):
    B, T = o_r.shape[:2]
    o_r = rmsnorm(o_r, r_gnorm_l) * jax.nn.silu(g.astype(jnp.float32)).reshape(B, T, R_HEADS, R_DV)
    o = jnp.concatenate([o_r.reshape(B, T, R_WIDTH), o_a.reshape(B, T, A_WIDTH)], axis=-1).astype(dtype)
    return jnp.einsum('bte,ed->btd', o, w_out_l)


def swiglu(h, wg, wu, wd):
    return jnp.einsum('btf,fd->btd', jax.nn.silu(jnp.einsum('btd,df->btf', h, wg)) * jnp.einsum('btd,df->btf', h, wu), wd)


def setup_inputs(seed: int = 0) -> dict:
    key = jax.random.key(seed)
    ks = jax.random.split(key, 24)
    n_pages = PAST_LEN // PAGE_SIZE
    n_phys = (DEC_BATCH * n_pages * 5) // 4
    f32 = jnp.float32
    nrm = lambda k, shape, s: jax.random.normal(k, shape, f32) * s
    x_prompt = nrm(ks[0], (BATCH, SEQ, D_MODEL), 1.0)
    x_sample = nrm(ks[1], (DEC_BATCH, DEC_SEQ, D_MODEL), 1.0)
    cache_k = nrm(ks[2], (DEPTH, n_phys, PAGE_SIZE, A_HEADS, 2, A_DK), 1.0)
    cache_v = nrm(ks[3], (DEPTH, n_phys, PAGE_SIZE, A_HEADS, A_DV), 1.0)
    state_hgrn = nrm(ks[4], (DEPTH, DEC_BATCH, R_HEADS, R_DK, R_DV), 0.5)
    page_table = jax.random.permutation(ks[5], n_phys)[:DEC_BATCH * n_pages].reshape(DEC_BATCH, n_pages).astype(jnp.int32)
    return {
        "x_prompt": x_prompt,
        "x_sample": x_sample,
        "cache_k": cache_k,
        "cache_v": cache_v,
        "state_hgrn": state_hgrn,
        "page_table": page_table,
        "w_in": nrm(ks[6], (DEPTH, D_MODEL, D_IN), D_MODEL ** -0.5),
        "w_out": nrm(ks[7], (DEPTH, MIX_WIDTH, D_MODEL), MIX_WIDTH ** -0.5),
        "lb_param": nrm(ks[8], (DEPTH + 1, R_HEADS * R_DK), 0.1),
        "r_gnorm": 1.0 + nrm(ks[9], (DEPTH, R_DV), 0.02),
        "lam_q1": nrm(ks[10], (DEPTH, A_DK), 0.1),
        "lam_k1": nrm(ks[11], (DEPTH, A_DK), 0.1),
        "lam_q2": nrm(ks[12], (DEPTH, A_DK), 0.1),
        "lam_k2": nrm(ks[13], (DEPTH, A_DK), 0.1),
        "a_subln": 1.0 + nrm(ks[14], (DEPTH, A_DV), 0.02),
        "norm_mix": 1.0 + nrm(ks[15], (DEPTH, D_MODEL), 0.02),
        "norm_ffn": 1.0 + nrm(ks[16], (DEPTH, D_MODEL), 0.02),
        "w_gate": nrm(ks[17], (DEPTH, D_MODEL, D_FF), D_MODEL ** -0.5),
        "w_up": nrm(ks[18], (DEPTH, D_MODEL, D_FF), D_MODEL ** -0.5),
        "w_down": nrm(ks[19], (DEPTH, D_FF, D_MODEL), D_FF ** -0.5),
        "norm_final": 1.0 + nrm(ks[20], (D_MODEL,), 0.02),
    }


def reference(x_prompt, x_sample, cache_k, cache_v, state_hgrn, page_table, w_in, w_out, lb_param,
              r_gnorm, lam_q1, lam_k1, lam_q2, lam_k2, a_subln, norm_mix, norm_ffn, w_gate, w_up,
              w_down, norm_final):
    hp, hs = x_prompt, x_sample
    B = x_prompt.shape[0]
    lb_all = jnp.cumsum(jax.nn.softmax(lb_param.astype(jnp.float32), axis=0), axis=0)
    kp, vp, sp, kss, vss, sss = [], [], [], [], [], []
    for l in range(DEPTH):
        lam_init = 0.8 - 0.6 * math.exp(-0.3 * l)
        lam = (jnp.exp(jnp.sum(lam_q1[l].astype(jnp.float32) * lam_k1[l].astype(jnp.float32)))
               - jnp.exp(jnp.sum(lam_q2[l].astype(jnp.float32) * lam_k2[l].astype(jnp.float32))) + lam_init)
        rq, rk, rv, logf, rg, aq, ak, av = project(rmsnorm(hp, norm_mix[l]), w_in[l], lb_all[l])
        s0 = jnp.zeros((B, R_HEADS, R_DK, R_DV), jnp.float32)
        o_r, s_new = hgrn_chunked(rq, rk, rv, logf, s0)
        o_a = diff_prompt(aq, ak, av, lam, lam_init, a_subln[l])
        hp = hp + merge(o_r, rg, o_a, r_gnorm[l], w_out[l], hp.dtype)
        hp = hp + swiglu(rmsnorm(hp, norm_ffn[l]), w_gate[l], w_up[l], w_down[l])
        kp.append(ak); vp.append(av); sp.append(s_new)
        rq, rk, rv, logf, rg, aq, ak, av = project(rmsnorm(hs, norm_mix[l]), w_in[l], lb_all[l])
        o_r, s_new = hgrn_recurrent(rq, rk, rv, logf, state_hgrn[l].astype(jnp.float32))
        o_a = diff_sample(aq, ak, av, cache_k[l], cache_v[l], page_table, lam, lam_init, a_subln[l])
        hs = hs + merge(o_r, rg, o_a, r_gnorm[l], w_out[l], hs.dtype)
        hs = hs + swiglu(rmsnorm(hs, norm_ffn[l]), w_gate[l], w_up[l], w_down[l])
        kss.append(ak); vss.append(av); sss.append(s_new)
    y_prompt = rmsnorm(hp, norm_final)
    y_sample = rmsnorm(hs, norm_final)
    k_prompt = jnp.stack(kp)
    v_prompt = jnp.stack(vp)
    s_prompt = jnp.stack(sp)
    k_sample = jnp.stack(kss)
    v_sample = jnp.stack(vss)
    s_sample = jnp.stack(sss)
    return (y_prompt, y_sample, k_prompt, v_prompt, s_prompt, k_sample, v_sample, s_sample)
```

```python
import numpy as np
import concourse.bass as bass
import concourse.mybir as mybir
from concourse.bass_utils import run_bass_kernel_spmd

F32 = mybir.dt.float32
BF16 = mybir.dt.bfloat16
I32 = mybir.dt.int32
ALU = mybir.AluOpType
AF = mybir.ActivationFunctionType
AX = mybir.AxisListType

NCORES = 8
D = 1024
T = 8192
DFF = 2816
NFC = DFF // 128
EPS = 1e-6
LAM_INIT = 0.8 - 0.6
TT = 512
NT = T // TT
NS = 16
NPG = 16
import os as _os
NPHYS = int(_os.environ.get('DBG_NPHYS', '2560'))


class Prog:
    ENGS = ("pe", "act", "dve", "pool", "sp")

    def __init__(self):
        self.ops = []
        self.last_write = {}
        self.readers = {}
        self.last_eng = {}
        self.last_dma = {}
        self.bank_last = {}

    def add(self, eng, fn, reads=(), writes=(), kind="c", key=None, banks=()):
        i = len(self.ops)
        deps = set()
        for b in banks:
            bl = self.bank_last.setdefault(b, {})
            for e2, j2 in bl.items():
                if e2 != eng:
                    deps.add(j2)
            bl[eng] = i
        for r in reads:
            w = self.last_write.get(r)
            if w is not None:
                deps.add(w)
        for w_ in writes:
            w = self.last_write.get(w_)
            if w is not None:
                deps.add(w)
            deps.update(self.readers.get(w_, ()))
        self.ops.append(dict(eng=eng, fn=fn, deps=deps, kind=kind, key=key, signal=False))
        for r in reads:
            self.readers.setdefault(r, []).append(i)
        for w_ in writes:
            self.last_write[w_] = i
            self.readers[w_] = []
        if kind == "c":
            self.last_eng[eng] = i
        else:
            self.last_dma[key] = i
        return i

    def dma(self, eng, fn, key, reads=(), writes=()):
        return self.add(eng, fn, reads, writes, kind="dma", key=key)

    def barrier(self):
        deps = set(self.last_eng.values()) | set(self.last_dma.values())
        for e in self.ENGS:
            self.ops.append(dict(eng=e, fn=None, deps=set(deps), kind="bar", key=None, signal=False))
        self.last_write = {}
        self.readers = {}

    def emit(self, nc, block, sems, dma_sems):
        ops = self.ops
        for op in ops:
            nd = set()
            for d in op["deps"]:
                p = ops[d]
                if p["kind"] == "c" and p["eng"] == op["eng"]:
                    if op["eng"] == "pe" and op["kind"] == "c":
                        continue
                nd.add(d)
            op["deps"] = nd
            for d in nd:
                if ops[d]["kind"] == "c":
                    ops[d]["signal"] = True
        cnt = {e: 0 for e in self.ENGS}
        dcnt = {}
        for op in ops:
            if op["kind"] == "c":
                if op["signal"]:
                    cnt[op["eng"]] += 1
                    op["done"] = ("eng_" + op["eng"], cnt[op["eng"]])
            elif op["kind"] in ("dma", "cc"):
                k = op["key"]
                dcnt[k] = dcnt.get(k, 0) + (16 if op["kind"] == "dma" else 1)
                op["done"] = (k, dcnt[k])
        allsem = dict(dma_sems)
        for e in self.ENGS:
            allsem["eng_" + e] = sems[e]
        streams = {e: [] for e in self.ENGS}
        seen = {e: {} for e in self.ENGS}
        for op in ops:
            e = op["eng"]
            waits = {}
            for d in op["deps"]:
                k, v = ops[d]["done"]
                if seen[e].get(k, 0) >= v:
                    continue
                if waits.get(k, 0) < v:
                    waits[k] = v
            seen[e].update(waits)
            streams[e].append((sorted(waits.items(), key=lambda kv: str(kv[0])), op))
        self.n_instr = {e: len(streams[e]) for e in self.ENGS}

        def run(engobj, lst, final=False):
            for waits, op in lst:
                for k, v in waits:
                    engobj.wait_ge(allsem[k], v)
                if op["kind"] == "bar":
                    continue
                ins = op["fn"](engobj)
                if op["kind"] == "dma":
                    ins.then_inc(allsem[op["done"][0]], 16)
                elif op["kind"] == "cc":
                    ins.then_inc(allsem[op["done"][0]])
                elif op["signal"]:
                    ins.then_inc(allsem[op["done"][0]], 1)
            if final:
                for k, v in dcnt.items():
                    engobj.wait_ge(allsem[k], v)

        block.sync(lambda e: run(e, streams["sp"], final=True))
        block.scalar(lambda e: run(e, streams["act"]))
        block.vector(lambda e: run(e, streams["dve"]))
        block.gpsimd(lambda e: run(e, streams["pool"]))
        block.tensor(lambda e: run(e, streams["pe"]))


class Alloc:
    def __init__(self, nc, limit=196608):
        self.nc = nc
        self.off = 16640
        self.limit = limit
        self.n = 0

    def __call__(self, shape, dtype, name=None):
        esz = 2 if dtype == BF16 else 4
        nbytes = esz
        for d in shape[1:]:
            nbytes *= d
        nbytes = (nbytes + 63) // 64 * 64
        self.n += 1
        h = self.nc.alloc_sbuf_tensor_at(name or f"t{self.n}", list(shape), dtype, offset=self.off)
        self.off += nbytes
        assert self.off <= self.limit, (self.off, name)
        return h

    def at(self, off, shape, dtype, name=None):
        self.n += 1
        return self.nc.alloc_sbuf_tensor_at(name or f"t{self.n}", list(shape), dtype, offset=off)

    def mark(self):
        return self.off

    def reset(self, m):
        self.off = m


import os
DBG_NT = int(os.environ.get('DBG_NT', '16'))
DBG_C = int(os.environ.get('DBG_C', '4'))
DBG_CC = int(os.environ.get('DBG_CC', '1'))
DBG_ST = int(os.environ.get('DBG_ST', '9'))
DBG_SUB = int(os.environ.get('DBG_SUB', '9'))
DBG_X = int(os.environ.get('DBG_X', '0'))
DBG_S = int(os.environ.get('DBG_S', '1'))


def build(with_sample=True):
    nc = bass.Bass("TRN2", target_bir_lowering=False)
    P = Prog()
    dti = lambda n, sh, dt=F32: nc.dram_tensor(n, sh, dt, kind="ExternalInput").ap()
    dto = lambda n, sh, dt=F32: nc.dram_tensor(n, sh, dt, kind="ExternalOutput").ap()
    xp = dti("xp", [T, D])
    w_fm = dti("w_fm", [D, 512])
    w_tm = dti("w_tm", [D, 512])
    w_out = dti("w_out", [D, D])
    w_gate = dti("w_gate", [11, 128, 8, 256])
    w_up = dti("w_up", [11, 128, 8, 256])
    w_down = dti("w_down", [DFF, D])
    lbh = dti("lbh", [128, 2])
    rgn = dti("rgn", [1, 128])
    lamv = dti("lamv", [4, 64])
    asub = dti("asub", [1, 128])
    nmix = dti("nmix", [1, D])
    nffn = dti("nffn", [1, D])
    nfin = dti("nfin", [1, D])
    ident_d = dti("ident", [128, 128])
    tri_d = dti("tri", [128, 128])
    mT_d = dti("mT", [128, 64])
    scm_d = dti("scm", [128, 512])
    gidx_d = dti("gidx", [128, 64], I32)
    y_p = dto("y_p", [2048, D])
    k_p = dto("k_p", [T, 128])
    v_p = dto("v_p", [T, 128])
    s_p = dto("s_p", [128, 128])
    srcs = [nc.dram_tensor(f"mix_src{q}", [2048, 256], BF16).ap() for q in range(4)]
    gaths = [nc.dram_tensor(f"mix_gath{q}", [4 * 2048, 256], BF16).ap() for q in range(4)]

    dma_keys = []
    def K(k):
        if k not in dma_keys:
            dma_keys.append(k)
        return k

    A = Alloc(nc)
    ident_f = A([128, 128], F32); ident = A([128, 128], BF16)
    tri_f = A([128, 128], F32); tri = A([128, 128], BF16)
    mT = A([128, 64], F32)
    scm = A([128, 512], F32)
    rgn_bc = A([128, 128], F32)
    asub_bc = A([128, 128], F32)
    nmix_bc = A([128, D], F32)
    lam_t = A([128, 4, 64], F32); lam_pr = A([128, 2, 64], F32); lam_s = A([128, 2], F32)
    lam_e = A([128, 2], F32); nlam = A([128, 1], F32)
    lbt = A([128, 2], F32); lbd = A([128, 1], F32); lb = A([128, 1], F32); oml = A([128, 1], F32)
    noml = A([128, 1], F32)
    epsc = A([128, 1], F32)
    PS = [nc.alloc_psum_tensor(f"ps{i}", [128, 512], F32) for i in range(8)]

    def C(eng, fn, r=(), w=(), b=()):
        return P.add(eng, fn, r, w, banks=b)

    def sigm(out, x, rk, wk, b=(), mulx=False):
        C("act", lambda e: e.activation(out=out, in_=x, func=AF.Exp, scale=-1.0), rk, (wk,), b)
        C("dve", lambda e: e.tensor_scalar(out=out, in0=out, scalar1=1.0, scalar2=None, op0=ALU.add), (wk,), (wk,))
        C("dve", lambda e: e.reciprocal(out=out, in_=out), (wk,), (wk,))
        if mulx:
            C("dve", lambda e: e.tensor_tensor(out=out, in0=out, in1=x, op=ALU.mult), rk + (wk,), (wk,), b)

    P.dma("sp", lambda e: e.dma_start(out=ident_f[:], in_=ident_d), K("c_ident"), (), ("ident_f",))
    P.dma("sp", lambda e: e.dma_start(out=tri_f[:], in_=tri_d), K("c_tri"), (), ("tri_f",))
    P.dma("sp", lambda e: e.dma_start(out=mT[:], in_=mT_d), K("c_mT"), (), ("mT",))
    P.dma("sp", lambda e: e.dma_start(out=scm[:], in_=scm_d), K("c_scm"), (), ("scm",))
    P.dma("sp", lambda e: e.dma_start(out=rgn_bc[:], in_=rgn.to_broadcast([128, 128])), K("c_rgn"), (), ("rgn_bc",))
    P.dma("sp", lambda e: e.dma_start(out=asub_bc[:], in_=asub.to_broadcast([128, 128])), K("c_asub"), (), ("asub_bc",))
    P.dma("sp", lambda e: e.dma_start(out=nmix_bc[:], in_=nmix.to_broadcast([128, D])), K("c_nmix"), (), ("nmix_bc",))
    P.dma("sp", lambda e: e.dma_start(out=lam_t[:], in_=lamv.rearrange("(o a) b -> o a b", o=1).to_broadcast([128, 4, 64])), K("c_lam"), (), ("lam_t",))
    P.dma("sp", lambda e: e.dma_start(out=lbt[:], in_=lbh), K("c_lb"), (), ("lbt",))
    C("dve", lambda e: e.memset(epsc[:], EPS), (), ("epsc",))
    C("act", lambda e: e.activation(out=ident[:], in_=ident_f[:], func=AF.Copy), ("ident_f",), ("ident",))
    C("act", lambda e: e.activation(out=tri[:], in_=tri_f[:], func=AF.Copy), ("tri_f",), ("tri",))
    C("dve", lambda e: e.tensor_scalar(out=asub_bc[:], in0=asub_bc[:], scalar1=1.0 - LAM_INIT, scalar2=None, op0=ALU.mult), ("asub_bc",), ("asub_bc",))
    C("dve", lambda e: e.tensor_tensor(out=lam_pr[:], in0=lam_t[:, 0:4:2, :], in1=lam_t[:, 1:4:2, :], op=ALU.mult), ("lam_t",), ("lam_pr",))
    C("dve", lambda e: e.tensor_reduce(out=lam_s[:], in_=lam_pr[:], axis=AX.X, op=ALU.add), ("lam_pr",), ("lam_s",))
    C("act", lambda e: e.activation(out=lam_e[:], in_=lam_s[:], func=AF.Exp), ("lam_s",), ("lam_e",))
    C("dve", lambda e: e.tensor_tensor(out=nlam[:], in0=lam_e[:, 1:2], in1=lam_e[:, 0:1], op=ALU.subtract), ("lam_e",), ("nlam",))
    C("dve", lambda e: e.tensor_scalar(out=nlam[:], in0=nlam[:], scalar1=-LAM_INIT, scalar2=None, op0=ALU.add), ("nlam",), ("nlam",))
    C("dve", lambda e: e.tensor_tensor(out=lbd[:], in0=lbt[:, 0:1], in1=lbt[:, 1:2], op=ALU.subtract), ("lbt",), ("lbd",))
    sigm(lb[:], lbd[:], ("lbd",), "lb")
    C("dve", lambda e: e.tensor_scalar(out=oml[:], in0=lb[:], scalar1=-1.0, scalar2=1.0, op0=ALU.mult, op1=ALU.add), ("lb",), ("oml",))
    C("dve", lambda e: e.tensor_scalar(out=noml[:], in0=oml[:], scalar1=-1.0, scalar2=None, op0=ALU.mult), ("oml",), ("noml",))

    mark0 = A.mark()
    wfm = A([128, 8, 512], BF16); wtm = A([128, 8, 512], BF16)
    aqT = A([128, T], BF16); akT = A([128, T], BF16)
    av = A([128, 64, 132], BF16)
    xt = [A([128, 4, D], F32)]
    xn = A([128, 4, D], BF16)
    hT = [A([128, 8, TT], BF16) for _ in range(2)]
    junk = A([128, D], BF16)
    ss = A([128, 4], F32); rs = A([128, 4], F32)
    sig = A([128, TT], F32); logf = A([128, TT], F32); bb = A([128, TT], F32)
    ebs = [A([128, TT], F32) for _ in range(2)]; enb = A([128, TT], F32); kk = A([128, TT], F32); qf = A([128, TT], F32)
    qTts = [A([128, TT], BF16) for _ in range(2)]; kTts = [A([128, TT], BF16) for _ in range(2)]; khT = A([128, TT], BF16)
    kh_toks = [A([128, 4, 128], BF16) for _ in range(2)]; v_toks = [A([128, 4, 128], BF16) for _ in range(2)]; g_toks = [A([128, 4, 128], F32) for _ in range(2)]
    kvst = [A([128, 4, 256], F32) for _ in range(2)]
    S = [A([128, 128], F32) for _ in range(2)]
    Sb = [A([128, 128], BF16) for _ in range(3)]
    ATm = [A([128, 64], BF16) for _ in range(2)]
    pT = [[A([128, TT], BF16) for _ in range(2)] for _ in range(2)]
    t1 = A([128, 128], F32); dif = A([128, 128], F32); hn1 = A([128, 128], F32)
    rl = A([128, 4], F32); sq1 = A([128, 2], F32); rs1 = A([128, 2], F32)
    o_st = [A([128, 4, 256], BF16) for _ in range(2)]

    P.dma("pool", lambda e: e.dma_start(out=wfm[:], in_=w_fm.rearrange("(kc p) n -> p kc n", p=128)), K("w_fm"), (), ("wfm",))
    P.dma("pool", lambda e: e.dma_start(out=wtm[:], in_=w_tm.rearrange("(kc p) n -> p kc n", p=128)), K("w_tm"), (), ("wtm",))
    C("pool", lambda e: e.memset(av[:, :, 128:132], 1.0), (), ("av_ones",))
    C("dve", lambda e: e.memset(S[0][:], 0.0), (), (("S", 0),))
    C("pool", lambda e: e.memset(Sb[0][:], 0.0), (), (("Sb", 0),))

    def rstd_ops(ssum, out, n, rkey, wkey, inv):
        rk = tuple(rkey) if isinstance(rkey, (tuple, list)) and rkey and isinstance(rkey[0], tuple) else (rkey,)
        C("act", lambda e: e.activation(out=out, in_=ssum, func=AF.Ln, bias=epsc[:, 0:1], scale=inv), rk + ("epsc",), (wkey,))
        C("act", lambda e: e.activation(out=out, in_=out, func=AF.Exp, scale=-0.5), (wkey,), (wkey,))

    chunk_ctr = [0]
    cur_free = [0, 1]

    def phase_a(j):
        sl = j % 2
        eb, qTt, kTt, kh_tok, v_tok, g_tok = ebs[sl], qTts[sl], kTts[sl], kh_toks[sl], v_toks[sl], g_toks[sl]
        P.dma("sp", lambda e: e.dma_start(out=xt[0][:], in_=xp[j * TT:(j + 1) * TT, :].rearrange("(tb p) d -> p tb d", p=128)),
              K(("xt", 0)), (), (("xt", 0),))
        for tb in range(4):
            C("act", lambda e, tb=tb: e.activation(out=junk[:], in_=xt[0][:, tb, :], func=AF.Square, accum_out=ss[:, tb:tb + 1]),
              (("xt", 0),), ("junk", ("ss", tb)))
        rstd_ops(ss[:], rs[:], 4, [("ss", tb) for tb in range(4)], "rs", 1.0 / D)
        for tb in range(4):
            C("dve", lambda e, tb=tb: e.scalar_tensor_tensor(out=xn[:, tb, :], in0=xt[0][:, tb, :], scalar=rs[:, tb:tb + 1],
                                                             in1=nmix_bc[:], op0=ALU.mult, op1=ALU.mult),
              (("xt", 0), "rs", "nmix_bc"), (("xn", tb),))
        yield
        for dc in range(8):
            bk = cur_free[dc % 2]
            pb = PS[bk]
            for tb in range(4):
                C("pe", lambda e, dc=dc, tb=tb, pb=pb: e.matmul(out=pb[:, tb * 128:(tb + 1) * 128], lhsT=xn[:, tb, dc * 128:(dc + 1) * 128], rhs=ident[:], start=True, stop=True),
                  (("xn", tb), "ident"), (("ps", bk),), (bk,))
            eng = "act" if dc % 2 == 0 else "dve"
            if eng == "act":
                C("act", lambda e, dc=dc, pb=pb: e.activation(out=hT[sl][:, dc, :], in_=pb[:, 0:TT], func=AF.Copy), (("ps", bk),), (("hT", sl, dc),), (bk,))
            else:
                C("dve", lambda e, dc=dc, pb=pb: e.tensor_copy(out=hT[sl][:, dc, :], in_=pb[:, 0:TT]), (("ps", bk),), (("hT", sl, dc),), (bk,))
            if dc % 2 == 1:
                yield
        hkeys = tuple(("hT", sl, dc) for dc in range(8))
        cs = slice(j * TT, (j + 1) * TT)
        for g in range(4):
            bk = cur_free[g % 2]
            ps = PS[bk]
            for dc in range(8):
                C("pe", lambda e, g=g, dc=dc, ps=ps: e.matmul(out=ps[:], lhsT=wfm[:, dc, g * 128:(g + 1) * 128], rhs=hT[sl][:, dc, :], start=(dc == 0), stop=(dc == 7)),
                  hkeys + ("wfm",), (("ps", bk),), (bk,))
            if g == 0:
                sigm(qf[:], ps[:], (("ps", bk),), "qf", (bk,), mulx=True)
            elif g == 1:
                sigm(sig[:], ps[:], (("ps", bk),), "sig", (bk,))
            elif g == 2:
                C("dve", lambda e, ps=ps: e.tensor_copy(out=aqT[:, cs], in_=ps[:]), (("ps", bk),), (("aqT", j),), (bk,))
            else:
                C("dve", lambda e, ps=ps: e.tensor_copy(out=akT[:, cs], in_=ps[:]), (("ps", bk),), (("akT", j),), (bk,))
            yield
        ks = j % 2
        for tb in range(4):
            bk = cur_free[tb % 2]
            ps = PS[bk]
            for dc in range(8):
                C("pe", lambda e, tb=tb, dc=dc, ps=ps: e.matmul(out=ps[:], lhsT=hT[sl][:, dc, tb * 128:(tb + 1) * 128], rhs=wtm[:, dc, :], start=(dc == 0), stop=(dc == 7)),
                  hkeys + ("wtm",), (("ps", bk),), (bk,))
            if DBG_X not in (3, 6):
                C("dve", lambda e, tb=tb, ps=ps: e.tensor_copy(out=v_tok[:, tb, :], in_=ps[:, 0:128]), (("ps", bk),), (("v_tok", sl, tb),), (bk,))
            if DBG_X not in (4, 6):
                sigm(g_tok[:, tb, :], ps[:, 128:256], (("ps", bk),), ("g_tok", sl, tb), (bk,), mulx=True)
            if DBG_X not in (5, 6):
                C("dve", lambda e, tb=tb, ps=ps: e.tensor_scalar(out=kvst[ks][:, tb, :], in0=ps[:, 256:512], scalar1=1.0, scalar2=None, op0=ALU.mult), (("ps", bk),), (("kvst", ks, tb),), (bk,))
            if DBG_X != 1:
                C("pool", lambda e, tb=tb: e.tensor_copy(out=av[:, 4 * j + tb, 0:128], in_=kvst[ks][:, tb, 128:256]), (("kvst", ks, tb),), (("av", 4 * j + tb),))
            yield
        kvk = tuple(("kvst", ks, tb) for tb in range(4))
        if DBG_X != 2:
            P.dma("sp", lambda e: e.dma_start(out=k_p[j * TT:(j + 1) * TT, :].rearrange("(tb p) c -> p tb c", p=128), in_=kvst[ks][:, :, 0:128]), K(("kst", ks)), kvk, ())
            P.dma("sp", lambda e: e.dma_start(out=v_p[j * TT:(j + 1) * TT, :].rearrange("(tb p) c -> p tb c", p=128), in_=kvst[ks][:, :, 128:256]), K(("vst", ks)), kvk, ())
        yield
        C("act", lambda e: e.activation(out=logf[:], in_=sig[:], func=AF.Ln, bias=lb[:, 0:1], scale=oml[:, 0:1]), ("sig", "lb", "oml"), ("logf",))
        C("dve", lambda e: e.tensor_scalar(out=kk[:], in0=sig[:], scalar1=noml[:, 0:1], scalar2=oml[:, 0:1], op0=ALU.mult, op1=ALU.add), ("sig", "noml", "oml"), ("kk",))
        C("dve", lambda e: e.tensor_tensor_scan(out=bb[:], data0=scm[:], data1=logf[:], initial=0.0, op0=ALU.mult, op1=ALU.add), ("scm", "logf"), ("bb",))
        C("act", lambda e: e.activation(out=eb[:], in_=bb[:], func=AF.Exp), ("bb",), (("eb", sl),))
        C("act", lambda e: e.activation(out=enb[:], in_=bb[:], func=AF.Exp, scale=-1.0), ("bb",), ("enb",))
        C("dve", lambda e: e.tensor_tensor(out=qTt[:], in0=qf[:], in1=eb[:], op=ALU.mult), ("qf", ("eb", sl)), (("qTt", sl),))
        C("dve", lambda e: e.tensor_tensor(out=kTt[:], in0=kk[:], in1=enb[:], op=ALU.mult), ("kk", "enb"), (("kTt", sl),))
        for c in range(8):
            C("dve", lambda e, c=c: e.scalar_tensor_tensor(out=khT[:, c * 64:(c + 1) * 64], in0=enb[:, c * 64:(c + 1) * 64], scalar=eb[:, c * 64 + 63:c * 64 + 64],
                                                           in1=kk[:, c * 64:(c + 1) * 64], op0=ALU.mult, op1=ALU.mult), ("enb", ("eb", sl), "kk"), (("khT", c // 2),))
        yield
        pk = PS[7]
        for tb in range(4):
            C("pe", lambda e, tb=tb: e.matmul(out=pk[:, 0:128], lhsT=khT[:, tb * 128:(tb + 1) * 128], rhs=ident[:], start=True, stop=True),
              (("khT", tb), "ident"), ("ps7k",), (7,))
            C("dve", lambda e, tb=tb: e.tensor_copy(out=kh_tok[:, tb, :], in_=pk[:, 0:128]), ("ps7k",), (("kh_tok", sl, tb),), (7,))

    def hgrn_parts(j, c):
        sl = j % 2
        eb, qTt, kTt, kh_tok, v_tok, g_tok = ebs[sl], qTts[sl], kTts[sl], kh_toks[sl], v_toks[sl], g_toks[sl]
        tb, hf = c // 2, c % 2
        pr = slice(64 * hf, 64 * hf + 64)
        cc = slice(c * 64, (c + 1) * 64)
        am = ATm[hf]
        st = {}

        def part1():
            n = chunk_ctr[0]; chunk_ctr[0] += 1
            cur, nxt = n % 2, (n + 1) % 2
            st["n"] = n
            C("pe", lambda e: e.matmul(out=PS[7][pr, 256:320], lhsT=kTt[:, cc], rhs=qTt[:, cc], start=True, stop=True), (("kTt", sl), ("qTt", sl)), (("psAT", hf),), (7,))
            C("dve", lambda e: e.tensor_tensor(out=am[pr, :], in0=PS[7][pr, 256:320], in1=mT[pr, :], op=ALU.mult), (("psAT", hf), "mT"), (("ATm", hf),), (7,))
            C("pe", lambda e: e.matmul(out=PS[7][:, 320:448], lhsT=kh_tok[pr, tb, :], rhs=v_tok[pr, tb, :], start=True, stop=True), (("kh_tok", sl, tb), ("v_tok", sl, tb)), ("psU",), (7,))
            C("dve", lambda e: e.scalar_tensor_tensor(out=S[nxt][:], in0=S[cur][:], scalar=eb[:, c * 64 + 63:c * 64 + 64], in1=PS[7][:, 320:448], op0=ALU.mult, op1=ALU.add),
              (("S", cur), ("eb", sl), "psU"), (("S", nxt),), (7,))
            C("act", lambda e: e.activation(out=Sb[(n + 1) % 3][:], in_=S[nxt][:], func=AF.Copy), (("S", nxt),), (("Sb", (n + 1) % 3),))

        def part2():
            cur = st["n"] % 3
            ops_ = PS[7][pr, 128:256]
            C("pe", lambda e: e.matmul(out=ops_, lhsT=qTt[:, cc], rhs=Sb[cur][:], start=True, stop=False), (("qTt", sl), ("Sb", cur)), (("pso", hf),), (7,))
            C("pe", lambda e: e.matmul(out=ops_, lhsT=am[pr, :], rhs=v_tok[pr, tb, :], start=False, stop=True), (("ATm", hf), ("v_tok", sl, tb)), (("pso", hf),), (7,))
            if hf == 1:
                osl = o_st[j % 2]
                full = PS[7][:, 128:256]
                C("act", lambda e: e.activation(out=junk[:, 0:128], in_=full, func=AF.Square, accum_out=sq1[:, 0:1]), (("pso", 0), ("pso", 1)), ("junk", "sq1h"), (7,))
                rstd_ops(sq1[:, 0:1], rs1[:, 0:1], 1, "sq1h", "rs1h", 1.0 / 128)
                C("dve", lambda e: e.scalar_tensor_tensor(out=hn1[:], in0=full, scalar=rs1[:, 0:1], in1=g_tok[:, tb, :], op0=ALU.mult, op1=ALU.mult),
                  (("pso", 0), ("pso", 1), "rs1h", ("g_tok", sl, tb)), ("hn1",), (7,))
                C("pool", lambda e: e.tensor_tensor(out=osl[:, tb, 0:128], in0=hn1[:], in1=rgn_bc[:], op=ALU.mult), ("hn1", "rgn_bc"), (("o_st", j % 2, "r", tb),))
        return part1, part2

    def attn_pairs(j):
        nkb = 4 * j + 4
        qs_ = slice(j * TT, (j + 1) * TT)
        osl = o_st[j % 2]

        def acc(m, qs):
            i = m * 4 + qs
            return PS[4 + i // 3], (i % 3) * 129, i // 3

        def scores(kb):
            sl = kb % 2
            r = kb - 4 * j
            q0 = 128 * r if r > 0 else 0
            for m in range(2):
                ps = PS[2 * m + sl]
                C("pe", lambda e, m=m, ps=ps, q0=q0: e.matmul(out=ps[:, q0:TT], lhsT=akT[64 * m:64 * m + 64, kb * 128:(kb + 1) * 128],
                                                             rhs=aqT[64 * m:64 * m + 64, j * TT + q0:(j + 1) * TT], start=True, stop=True),
                  (("akT", kb // 4), ("aqT", j)), (("ps", 2 * m + sl),), (2 * m + sl,))

        def exps(kb):
            sl = kb % 2
            r = kb - 4 * j
            q0 = 128 * r if r > 0 else 0
            for m in range(2):
                ps = PS[2 * m + sl]
                C("act", lambda e, m=m, ps=ps, q0=q0: e.activation(out=pT[m][sl][:, q0:TT], in_=ps[:, q0:TT], func=AF.Exp, scale=0.125),
                  (("ps", 2 * m + sl),), (("pT", m, sl),), (2 * m + sl,))
                if r >= 0:
                    C("dve", lambda e, m=m, q0=q0: e.tensor_tensor(out=pT[m][sl][:, q0:q0 + 128], in0=pT[m][sl][:, q0:q0 + 128], in1=tri[:], op=ALU.mult),
                      (("pT", m, sl), "tri"), (("pT", m, sl),))

        def pv(kb):
            sl = kb % 2
            r = kb - 4 * j
            for m in range(2):
                for qs in range(max(r, 0), 4):
                    ps, c0, bank = acc(m, qs)
                    C("pe", lambda e, m=m, qs=qs, ps=ps, c0=c0: e.matmul(out=ps[:, c0:c0 + 129], lhsT=pT[m][sl][:, qs * 128:(qs + 1) * 128], rhs=av[:, kb, 0:129],
                                                                        start=(kb == 0 and c0 == 0), stop=(kb == 4 * j + qs), skip_group_check=True),
                      (("pT", m, sl), ("av", kb), "av_ones"), (("acc", m, qs),), (4 + bank,))
            if r >= 0:
                qs = r
                p0, c00, b0_ = acc(0, qs)
                p1, c01, b1_ = acc(1, qs)
                C("dve", lambda e: e.reciprocal(out=rl[:, 0:1], in_=p0[:, c00 + 128:c00 + 129]), (("acc", 0, qs),), ("rl0",), (4 + b0_,))
                C("dve", lambda e: e.reciprocal(out=rl[:, 1:2], in_=p1[:, c01 + 128:c01 + 129]), (("acc", 1, qs),), ("rl1",), (4 + b1_,))
                C("dve", lambda e: e.tensor_tensor(out=rl[:, 2:3], in0=rl[:, 1:2], in1=nlam[:], op=ALU.mult), ("rl1", "nlam"), ("rl2",))
                C("dve", lambda e: e.tensor_scalar(out=t1[:], in0=p1[:, c01:c01 + 128], scalar1=rl[:, 2:3], scalar2=None, op0=ALU.mult), (("acc", 1, qs), "rl2"), ("t1",), (4 + b1_,))
                C("dve", lambda e: e.scalar_tensor_tensor(out=dif[:], in0=p0[:, c00:c00 + 128], scalar=rl[:, 0:1], in1=t1[:], op0=ALU.mult, op1=ALU.add),
                  (("acc", 0, qs), "rl0", "t1"), ("dif",), (4 + b0_,))
                C("act", lambda e: e.activation(out=junk[:, 128:256], in_=dif[:], func=AF.Square, accum_out=sq1[:, 1:2]), ("dif",), ("junk2", "sq1a"))
                rstd_ops(sq1[:, 1:2], rs1[:, 1:2], 1, "sq1a", "rs1a", 1.0 / 128)
                C("dve", lambda e: e.scalar_tensor_tensor(out=osl[:, qs, 128:256], in0=dif[:], scalar=rs1[:, 1:2], in1=asub_bc[:], op0=ALU.mult, op1=ALU.mult),
                  ("dif", "rs1a", "asub_bc"), (("o_st", j % 2, "a", qs),))

        scores(0)
        for kb in range(nkb):
            cur_free[:] = [(kb + 1) % 2, 2 + (kb + 1) % 2]
            yield
            if kb + 1 < nkb:
                scores(kb + 1)
            exps(kb)
            pv(kb)

    def side_items(j):
        parts = [hgrn_parts(j, c) for c in range(8)]
        hg_items = []
        for c in range(8):
            hg_items.append(parts[c][0])
            if c >= 1:
                hg_items.append(parts[c - 1][1])
        hg_items.append(parts[7][1])
        pa_gen = phase_a(j + 1) if j + 1 < DBG_NT else iter(())
        def pa_step():
            next(pa_gen, None)
        items = []
        pa_done = [False]
        for k in range(max(len(hg_items), 14)):
            if k < len(hg_items):
                items.append(hg_items[k])
            if k < 14:
                items.append(pa_step)
        def drain():
            for _ in pa_gen:
                pass
        items.append(drain)
        return items

    for _ in phase_a(0):
        pass
    for j in range(DBG_NT):
        items = side_items(j)
        nsteps = 4 * j + 4
        k = 0
        for si, _ in enumerate(attn_pairs(j)):
            rem_steps = nsteps - si
            take = -(-(len(items) - k) // rem_steps)
            for _t in range(take):
                items[k](); k += 1
        while k < len(items):
            items[k](); k += 1
        okeys = tuple(("o_st", j % 2, "r", tb) for tb in range(4)) + tuple(("o_st", j % 2, "a", q) for q in range(4))
        P.dma("sp", lambda e, j=j: e.dma_start(out=srcs[j // 4][(j % 4) * TT:(j % 4 + 1) * TT, :].rearrange("(tb p) c -> p tb c", p=128), in_=o_st[j % 2][:]), K(("ost", j % 2)), okeys, (("src", j),))
        if j % 4 == 3 and DBG_CC:
            q = j // 4
            P.add("pool", lambda e, q=q: e.collective_compute("AllGather", ALU.bypass, replica_groups=[[0, 1, 2, 3], [4, 5, 6, 7]], ins=[srcs[q].opt()], outs=[gaths[q].opt()]),
                  tuple(("src", jj) for jj in range(4 * q, 4 * q + 4)), (("gath", q),), kind="cc", key=K(("cc", q)))
    nfin_chunks = chunk_ctr[0]
    P.dma("sp", lambda e: e.dma_start(out=s_p, in_=S[nfin_chunks % 2][:]), K("s_p"), (("S", nfin_chunks % 2),), ())
    P.barrier()
    peak_ab = A.mark()
    A.reset(mark0)
    xres = dti("xres", [2048, D])
    wo = A([128, 8, D], BF16)
    nffn_bc = A([128, D], F32); nfin_bc = A([128, D], F32)
    wg = [A([128, 8, 256], BF16) for _ in range(2)]
    wu = [A([128, 8, 256], BF16) for _ in range(2)]
    mark_s = A.mark()
    gidx = A([128, 64], I32)
    ogh = A([128, 4096], BF16)
    og = ogh[:].rearrange("p (a b c) -> p a b c", a=4, b=4)
    oT = A([128, 8, TT], BF16)
    wd = A([128, NFC, D], BF16)
    hp1 = A([128, 4, D], F32)
    hn = ogh[:].rearrange("p (a d) -> p a d", a=4)
    h2T = oT
    aT = A([128, NFC, TT], BF16)
    sg = [A([128, TT], F32) for _ in range(2)]
    yout = [A([128, D], F32)] * 2
    junkc = A([128, D], BF16)
    ssc = A([128, 4], F32); rsc = A([128, 4], F32)
    ssd = A([128, 4], F32); rsd = A([128, 4], F32)

    P.dma("sp", lambda e: e.dma_start(out=gidx[:], in_=gidx_d), K("c_gidx"), (), ("gidx",))
    P.dma("sp", lambda e: e.dma_start(out=nffn_bc[:], in_=nffn.to_broadcast([128, D])), K("c_nffn"), (), ("nffn_bc",))
    P.dma("sp", lambda e: e.dma_start(out=nfin_bc[:], in_=nfin.to_broadcast([128, D])), K("c_nfin"), (), ("nfin_bc",))
    P.dma("pool", lambda e: e.dma_start(out=wo[:], in_=w_out.rearrange("(kc p) n -> p kc n", p=128)), K("w_o"), (), ("wo",))
    for q4 in range(2):
        P.dma("pool", lambda e, q4=q4: e.dma_start(out=wd[:, q4 * 11:(q4 + 1) * 11, :], in_=w_down[q4 * 1408:(q4 + 1) * 1408, :].rearrange("(kc p) n -> p kc n", p=128)),
              K(("w_d", q4)), (), (("wd", q4),))
    wdk = (("wd", 0), ("wd", 1))
    wctr = [0]

    for tt in range(DBG_C):
        for tb in range(4):
            for r in range(4):
                col = (tt * 4 + tb) * 4 + r
                P.dma("pool", lambda e, tb=tb, r=r, col=col, tt=tt: e.indirect_dma_start(out=og[:, tb, r, :], out_offset=None, in_=gaths[tt],
                                                                                    in_offset=bass.IndirectOffsetOnAxis(ap=gidx[:, col:col + 1], axis=0)),
                      K(("og", tb)), (("gath", tt), "gidx"), (("og", tb, r), ("hn", tb)))
        P.dma("sp", lambda e, tt=tt: e.dma_start(out=hp1[:], in_=xres[tt * TT:(tt + 1) * TT, :].rearrange("(tb p) d -> p tb d", p=128)), K("hp1"), (), tuple(("hp1", tb) for tb in range(4)))
        for ec in range(8):
            half, r = ec // 4, ec % 4
            pb = PS[ec % 4]
            for tb in range(4):
                C("pe", lambda e, tb=tb, r=r, half=half, pb=pb: e.matmul(out=pb[:, tb * 128:(tb + 1) * 128], lhsT=og[:, tb, r, half * 128:(half + 1) * 128], rhs=ident[:], start=True, stop=True),
                  (("og", tb, r), "ident"), (("ps", ec % 4),), (ec % 4,))
            if ec % 2 == 0:
                C("act", lambda e, ec=ec, pb=pb: e.activation(out=oT[:, ec, :], in_=pb[:, 0:TT], func=AF.Copy), (("ps", ec % 4),), (("oT", ec),), (ec % 4,))
            else:
                C("dve", lambda e, ec=ec, pb=pb: e.tensor_copy(out=oT[:, ec, :], in_=pb[:, 0:TT]), (("ps", ec % 4),), (("oT", ec),), (ec % 4,))
        otk = tuple(("oT", ec) for ec in range(8))
        for tb in range(4):
            for nh in range(2):
                ps = PS[(tb * 2 + nh) % 4]
                pk_ = ("ps", (tb * 2 + nh) % 4)
                for ec in range(8):
                    C("pe", lambda e, tb=tb, nh=nh, ec=ec, ps=ps: e.matmul(out=ps[:], lhsT=oT[:, ec, tb * 128:(tb + 1) * 128], rhs=wo[:, ec, nh * 512:(nh + 1) * 512], start=(ec == 0), stop=(ec == 7)),
                      otk + ("wo",), (pk_,), (pk_[1],))
                C("dve", lambda e, tb=tb, nh=nh, ps=ps: e.tensor_tensor(out=hp1[:, tb, nh * 512:(nh + 1) * 512], in0=hp1[:, tb, nh * 512:(nh + 1) * 512], in1=ps[:], op=ALU.add),
                  (pk_, ("hp1", tb)), (("hp1", tb),), (pk_[1],))
        for tb in range(4):
            C("act", lambda e, tb=tb: e.activation(out=junkc[:], in_=hp1[:, tb, :], func=AF.Square, accum_out=ssc[:, tb:tb + 1]), (("hp1", tb),), ("junkc", ("ssc", tb)))
        rstd_ops(ssc[:], rsc[:], 4, [("ssc", tb) for tb in range(4)], "rsc", 1.0 / D)
        for tb in range(4):
            C("dve", lambda e, tb=tb: e.scalar_tensor_tensor(out=hn[:, tb, :], in0=hp1[:, tb, :], scalar=rsc[:, tb:tb + 1], in1=nffn_bc[:], op0=ALU.mult, op1=ALU.mult),
              (("hp1", tb), "rsc", "nffn_bc"), (("hn", tb),))
        for dc in range(8):
            pb = PS[dc % 4]
            for tb in range(4):
                C("pe", lambda e, dc=dc, tb=tb, pb=pb: e.matmul(out=pb[:, tb * 128:(tb + 1) * 128], lhsT=hn[:, tb, dc * 128:(dc + 1) * 128], rhs=ident[:], start=True, stop=True),
                  (("hn", tb), "ident"), (("ps", dc % 4),), (dc % 4,))
            if dc % 2 == 0:
                C("act", lambda e, dc=dc, pb=pb: e.activation(out=h2T[:, dc, :], in_=pb[:, 0:TT], func=AF.Copy), (("ps", dc % 4),), (("oT", dc),), (dc % 4,))
            else:
                C("dve", lambda e, dc=dc, pb=pb: e.tensor_copy(out=h2T[:, dc, :], in_=pb[:, 0:TT]), (("ps", dc % 4),), (("oT", dc),), (dc % 4,))
        h2k = tuple(("oT", dc) for dc in range(8))
        for fg in range(11):
            ws = wctr[0] % 2; wctr[0] += 1
            P.dma("pool", lambda e, fg=fg, ws=ws: e.dma_start(out=wg[ws][:], in_=w_gate[fg]), K(("wg", ws)), (), (("wg", ws),))
            P.dma("pool", lambda e, fg=fg, ws=ws: e.dma_start(out=wu[ws][:], in_=w_up[fg]), K(("wu", ws)), (), (("wu", ws),))
            for fl in range(2):
                fc = fg * 2 + fl
                pg, pu = PS[4 + fl * 2], PS[5 + fl * 2]
                for dc in range(8):
                    C("pe", lambda e, dc=dc, fl=fl, ws=ws, pg=pg: e.matmul(out=pg[:], lhsT=wg[ws][:, dc, fl * 128:(fl + 1) * 128], rhs=h2T[:, dc, :], start=(dc == 0), stop=(dc == 7)),
                      h2k + (("wg", ws),), (("ps", 4 + fl * 2),), (4 + fl * 2,))
                for dc in range(8):
                    C("pe", lambda e, dc=dc, fl=fl, ws=ws, pu=pu: e.matmul(out=pu[:], lhsT=wu[ws][:, dc, fl * 128:(fl + 1) * 128], rhs=h2T[:, dc, :], start=(dc == 0), stop=(dc == 7)),
                      h2k + (("wu", ws),), (("ps", 5 + fl * 2),), (5 + fl * 2,))
                sigm(sg[fl][:], pg[:], (("ps", 4 + fl * 2),), ("sg", fl), (4 + fl * 2,), mulx=True)
                C("dve", lambda e, fl=fl, fc=fc, pu=pu: e.tensor_tensor(out=aT[:, fc, :], in0=sg[fl][:], in1=pu[:], op=ALU.mult), (("sg", fl), ("ps", 5 + fl * 2)), (("aT", fc),), (5 + fl * 2,))
        atk = tuple(("aT", fc) for fc in range(NFC))
        for tb in range(4):
            for nh in range(2):
                ps = PS[(tb * 2 + nh) % 4]
                pk_ = ("ps", (tb * 2 + nh) % 4)
                for fc in range(NFC):
                    C("pe", lambda e, tb=tb, nh=nh, fc=fc, ps=ps: e.matmul(out=ps[:], lhsT=aT[:, fc, tb * 128:(tb + 1) * 128], rhs=wd[:, fc, nh * 512:(nh + 1) * 512], start=(fc == 0), stop=(fc == NFC - 1)),
                      atk + wdk, (pk_,), (pk_[1],))
                C("dve", lambda e, tb=tb, nh=nh, ps=ps: e.tensor_tensor(out=hp1[:, tb, nh * 512:(nh + 1) * 512], in0=hp1[:, tb, nh * 512:(nh + 1) * 512], in1=ps[:], op=ALU.add),
                  (pk_, ("hp1", tb)), (("hp1", tb),), (pk_[1],))
        for tb in range(4):
            C("act", lambda e, tb=tb: e.activation(out=junkc[:], in_=hp1[:, tb, :], func=AF.Square, accum_out=ssd[:, tb:tb + 1]), (("hp1", tb),), ("junkc", ("ssd", tb)))
        rstd_ops(ssd[:], rsd[:], 4, [("ssd", tb) for tb in range(4)], "rsd", 1.0 / D)
        for tb in range(4):
            ys = tb % 2
            C("dve", lambda e, tb=tb, ys=ys: e.scalar_tensor_tensor(out=yout[ys][:], in0=hp1[:, tb, :], scalar=rsd[:, tb:tb + 1], in1=nfin_bc[:], op0=ALU.mult, op1=ALU.mult),
              (("hp1", tb), "rsd", "nfin_bc"), (("yout", 0),))
            P.dma("sp", lambda e, tt=tt, tb=tb, ys=ys: e.dma_start(out=y_p[tt * TT + tb * 128: tt * TT + (tb + 1) * 128, :], in_=yout[ys][:]), K(("yout", 0)), (("yout", 0),), ())
    peak_c = A.mark()
    if not with_sample or not DBG_S:
        return nc, P, A, dma_keys, dict(peak_ab=peak_ab, peak_c=peak_c)
    P.barrier()
    A.reset(mark_s)
    xs_d = dti("xs", [NS, D])
    w_in_d = dti("w_in_s", [D, 3584])
    lbp_d = dti("lbp", [2, 512])
    sel_d = dti("sel", [NS, NS, 128])
    rgc_d = dti("rgn_col", [128, 1]); asc_d = dti("asub_col", [128, 1])
    ptl_d = dti("ptl", [8, 2 * NS], I32)
    sub_d = dti("subi", [128, 1], I32)
    ck_d = dti("cache_k", [NPHYS * 16, 4096])
    cv_d = dti("cache_v", [NPHYS * 16, 4096])
    st_d = dti("state", [NS, 4, 128, 128])
    y_s = dto("y_s", [NS, D]); k_s = dto("k_s", [NS, 512]); v_s = dto("v_s", [NS, 512])
    s_s = dto("s_s", [NS, 4, 128, 128])

    xs_t = A([NS, D], F32); xn_s = A([NS, D], BF16)
    ss_s = A([NS, 2], F32); rs_s = A([NS, 2], F32)
    lbp = A([NS, 2, 512], F32); oml_t = A([NS, 512], F32)
    lbd_t = lbp[:, 0, :]; lb_t = lbp[:, 1, :]
    sel = A([NS, NS, 128], F32)
    rgn_col = A([128, 1], F32); asub_col = A([128, 1], F32)
    ones_f = A([128, 128], F32)
    pt_raw = A([128, 2 * NS], I32); subi = A([128, 1], I32); idx_t = A([128, 2 * NS], I32)
    hT_s = A([128, 8 * NS], BF16)
    m_wS = A.mark()
    wS = [A([128, 8, 512], BF16)] * 2
    p_tok = A([NS, 7, 512], F32)
    m_ft = A.mark()
    f_t = A([NS, 512], F32); kk_t = A([NS, 512], F32); q_t = A([NS, 512], F32); g_t = A([NS, 512], F32)
    sig_t = f_t
    featT = A([128, 4, 64], F32)
    m_Sin = A.mark()
    S_in = A([128, NS, 128], F32); S_out = S_in
    KKbd = A([NS, NS, 128], F32)
    orT = A([128, 64], F32); sq_s = A([128, 64], F32); rstd_bc = A([128, 64], F32); tmp64 = A([128, 64], F32)
    mixT_s = A([128, 128], BF16)
    prod_s = q_t; s_self = A([NS, 8], F32); p_self = A([NS, 8], F32); Pbd = A([NS, NS, 8], F32)
    qb_s = A([128, 512], F32)
    Kb = A([128, 4096], F32); Vb = A([128, 4096], F32)
    Kbs = [Kb, A.at(m_Sin, [128, 4096], F32)]
    Vbbs = [A.at(m_wS, [128, 4096], BF16), A.at(m_ft, [128, 4096], BF16)]
    Kkeys = [("Kb",), ("Kb1", "S_in", "KKbd")]
    Vkeys = [("Vbb0", ("wS", 0)), ("Vbb1", "f_t", "kk_t", "q_t", "g_t")]
    sc_s = A([128, 64], F32); pexp = A([128, 64], BF16); ps8 = A([128, 8], F32)
    OT = A([128, 128], F32); Lc = A([128, 128], F32); psb = A([128, 128], F32); tmpO = A([128, 128], F32); Rl = A([128, 128], F32)
    dif_s = A([128, 64], F32); sqa = A([128, 64], F32); rstd_a = A([128, 64], F32); tmpa = A([128, 64], F32)
    hs1 = A([NS, D], F32); hn_s = A([NS, D], BF16); junks = hn_s; h2T_s = A([128, 8 * NS], BF16)
    wdn = [A([128, 2, D], BF16) for _ in range(2)]
    sgs = A([128, NS], F32); aT_s = A([128, NFC * NS], BF16)
    ys_t = hs1

    P.dma("sp", lambda e: e.dma_start(out=xs_t[:], in_=xs_d), K("s_xs"), (), ("xs_t",))
    P.dma("sp", lambda e: e.dma_start(out=lbp[:], in_=lbp_d.rearrange("(o a) n -> o a n", o=1).to_broadcast([NS, 2, 512])), K("s_lbp"), (), ("lbp",))
    P.dma("sp", lambda e: e.dma_start(out=sel[:], in_=sel_d), K("s_sel"), (), ("sel",))
    P.dma("sp", lambda e: e.dma_start(out=rgn_col[:], in_=rgc_d), K("s_rgc"), (), ("rgn_col",))
    P.dma("sp", lambda e: e.dma_start(out=asub_col[:], in_=asc_d), K("s_asc"), (), ("asub_col",))
    P.dma("sp", lambda e: e.dma_start(out=subi[:], in_=sub_d), K("s_sub"), (), ("subi",))
    P.dma("sp", lambda e: e.dma_start(out=pt_raw[:], in_=bass.AP(tensor=ptl_d.tensor, offset=0, ap=[[2 * NS, 8], [0, 16], [1, 2 * NS]])), K("s_pt"), (), ("pt_raw",))
    C("dve", lambda e: e.memset(ones_f[:], 1.0), (), ("ones_f",))
    C("dve", lambda e: e.tensor_scalar(out=asub_col[:], in0=asub_col[:], scalar1=1.0 - LAM_INIT, scalar2=None, op0=ALU.mult), ("asub_col",), ("asub_col",))
    for cidx in range(2 * NS):
        pass
    C("dve", lambda e: e.scalar_tensor_tensor(out=idx_t[:], in0=pt_raw[:], scalar=16, in1=subi[:, 0:1].to_broadcast([128, 2 * NS]), op0=ALU.mult, op1=ALU.add), ("pt_raw", "subi"), ("idx_t",))
    C("dve", lambda e: e.tensor_tensor(out=lbd_t, in0=lbp[:, 0, :], in1=lbp[:, 1, :], op=ALU.subtract), ("lbp",), ("lbp",))
    sigm(oml_t[:], lbd_t, ("lbp",), "oml_t")
    C("dve", lambda e: e.tensor_copy(out=lb_t, in_=oml_t[:]), ("oml_t", "lbp"), ("lbp",))
    C("dve", lambda e: e.tensor_scalar(out=oml_t[:], in0=oml_t[:], scalar1=-1.0, scalar2=1.0, op0=ALU.mult, op1=ALU.add), ("oml_t", "lbp"), ("oml_t",))
    C("dve", lambda e: e.memset(PS[1][:], 0.0), (), ("psL",), (1,))
    C("dve", lambda e: e.memset(PS[2][:], 0.0), (), ("psO",), (2,))

    def rstd16(ssum, out, rkey, wkey, inv):
        C("act", lambda e: e.activation(out=out, in_=ssum, func=AF.Ln, bias=epsc[0:NS, 0:1], scale=inv), (rkey, "epsc"), (wkey,))
        C("act", lambda e: e.activation(out=out, in_=out, func=AF.Exp, scale=-0.5), (wkey,), (wkey,))

    C("act", lambda e: e.activation(out=junks[:], in_=xs_t[:], func=AF.Square, accum_out=ss_s[:, 0:1]), ("xs_t",), ("hn_s", "ss_s0"))
    rstd16(ss_s[:, 0:1], rs_s[:, 0:1], "ss_s0", "rs_s0", 1.0 / D)
    C("dve", lambda e: e.scalar_tensor_tensor(out=xn_s[:], in0=xs_t[:], scalar=rs_s[:, 0:1], in1=nmix_bc[0:NS, :], op0=ALU.mult, op1=ALU.mult), ("xs_t", "rs_s0", "nmix_bc"), ("xn_s",))
    for dc in range(8):
        C("pe", lambda e, dc=dc: e.matmul(out=PS[7][:, dc * NS:(dc + 1) * NS], lhsT=xn_s[:, dc * 128:(dc + 1) * 128], rhs=ident[0:NS, 0:NS], start=True, stop=True),
          ("xn_s", "ident"), ("ps7s",), (7,))
    C("act", lambda e: e.activation(out=hT_s[:], in_=PS[7][:, 0:8 * NS], func=AF.Copy), ("ps7s",), ("hT_s",), (7,))
    for g in range(7):
        ws = g % 2
        P.dma("pool", lambda e, g=g, ws=ws: e.dma_start(out=wS[ws][:], in_=w_in_d[:, g * 512:(g + 1) * 512].rearrange("(kc p) n -> p kc n", p=128)), K(("wS", 0)), (), (("wS", 0),))
        bk = 5 + g % 2
        for dc in range(8):
            C("pe", lambda e, dc=dc, ws=ws, bk=bk: e.matmul(out=PS[bk][0:NS, :], lhsT=hT_s[:, dc * NS:(dc + 1) * NS], rhs=wS[ws][:, dc, :], start=(dc == 0), stop=(dc == 7)),
              ("hT_s", ("wS", 0)), (("psp", bk),), (bk,))
        C("dve", lambda e, g=g, bk=bk: e.tensor_scalar(out=p_tok[:, g, :], in0=PS[bk][0:NS, :], scalar1=1.0, scalar2=None, op0=ALU.mult), (("psp", bk),), (("p_tok", g),), (bk,))
    P.dma("sp", lambda e: e.dma_start(out=k_s, in_=p_tok[:, 5, :]), K("s_ks"), (("p_tok", 5),), ())
    P.dma("sp", lambda e: e.dma_start(out=v_s, in_=p_tok[:, 6, :]), K("s_vs"), (("p_tok", 6),), ())
    sigm(sig_t[:], p_tok[:, 1, :], (("p_tok", 1),), "f_t")
    C("dve", lambda e: e.tensor_tensor(out=f_t[:], in0=sig_t[:], in1=oml_t[:], op=ALU.mult), ("f_t", "oml_t"), ("f_t",))
    C("dve", lambda e: e.tensor_tensor(out=f_t[:], in0=f_t[:], in1=lb_t, op=ALU.add), ("f_t", "lbp"), ("f_t",))
    C("dve", lambda e: e.tensor_scalar(out=kk_t[:], in0=f_t[:], scalar1=-1.0, scalar2=1.0, op0=ALU.mult, op1=ALU.add), ("f_t",), ("kk_t",))
    sigm(q_t[:], p_tok[:, 0, :], (("p_tok", 0),), "q_t", mulx=True)
    sigm(g_t[:], p_tok[:, 3, :], (("p_tok", 3),), "g_t", mulx=True)
    srcs_fm = [(f_t, "f_t", None), (q_t, "q_t", None), (g_t, "g_t", None), (p_tok, ("p_tok", 6), 6)]
    for xi, (xt_, xk, gsel) in enumerate(srcs_fm):
        for h in range(4):
            src_ap = (xt_[:, h * 128:(h + 1) * 128] if gsel is None else xt_[:, gsel, h * 128:(h + 1) * 128])
            C("pe", lambda e, xi=xi, h=h, src_ap=src_ap: e.matmul(out=PS[7][:, 128 + xi * 64 + h * NS: 128 + xi * 64 + (h + 1) * NS], lhsT=src_ap, rhs=ident_f[0:NS, 0:NS], start=True, stop=True),
              (xk, "ident_f"), ("ps7f",), (7,))
    C("act", lambda e: e.activation(out=featT[:], in_=PS[7][:, 128:384].rearrange("p (a b) -> p a b", a=4), func=AF.Copy), ("ps7f",), ("featT",), (7,))
    fT, qT, gT, avT = featT[:, 0, :], featT[:, 1, :], featT[:, 2, :], featT[:, 3, :]
    for h in range(4):
        P.dma("sp", lambda e, h=h: e.dma_start(out=S_in[:], in_=st_d[:, h, :, :].rearrange("b k v -> k b v")), K("s_sin"), (), ("S_in",))
        C("dve", lambda e, h=h: e.tensor_tensor(out=KKbd[:], in0=kk_t[:, h * 128:(h + 1) * 128].unsqueeze(1).to_broadcast([NS, NS, 128]), in1=sel[:], op=ALU.mult), ("kk_t", "sel"), ("KKbd",))
        for b in range(NS):
            bk = 4 + b % 2
            col = h * NS + b
            C("pe", lambda e, b=b, h=h, bk=bk: e.matmul(out=PS[bk][:, 0:128], lhsT=KKbd[:, b, :], rhs=p_tok[:, 2, h * 128:(h + 1) * 128], start=True, stop=True),
              ("KKbd", ("p_tok", 2)), (("pso", bk),), (bk,))
            C("dve", lambda e, b=b, bk=bk, col=col: e.scalar_tensor_tensor(out=S_out[:, b, :], in0=S_in[:, b, :], scalar=fT[:, col:col + 1], in1=PS[bk][:, 0:128], op0=ALU.mult, op1=ALU.add),
              ("S_in", "featT", ("pso", bk)), ("S_in",), (bk,))
            C("pe", lambda e, b=b, col=col: e.matmul(out=PS[6][:, col:col + 1], lhsT=S_out[:, b, :], rhs=qT[:, col:col + 1], start=True, stop=True),
              ("S_in", "featT"), ("ps6o",), (6,))
        P.dma("sp", lambda e, h=h: e.dma_start(out=s_s[:, h, :, :].rearrange("b k v -> k b v"), in_=S_out[:]), K("s_sout"), ("S_in",), ())
    C("act", lambda e: e.activation(out=orT[:], in_=PS[6][:, 0:64], func=AF.Copy), ("ps6o",), ("orT",), (6,))
    C("dve", lambda e: e.tensor_tensor(out=sq_s[:], in0=orT[:], in1=orT[:], op=ALU.mult), ("orT",), ("sq_s",))
    C("pe", lambda e: e.matmul(out=PS[6][:, 64:128], lhsT=ones_f[:], rhs=sq_s[:], start=True, stop=True), ("ones_f", "sq_s"), ("ps6q",), (6,))
    C("act", lambda e: e.activation(out=rstd_bc[:], in_=PS[6][:, 64:128], func=AF.Ln, bias=epsc[:, 0:1], scale=1.0 / 128), ("ps6q", "epsc"), ("rstd_bc",), (6,))
    C("act", lambda e: e.activation(out=rstd_bc[:], in_=rstd_bc[:], func=AF.Exp, scale=-0.5), ("rstd_bc",), ("rstd_bc",))
    C("dve", lambda e: e.tensor_tensor(out=tmp64[:], in0=orT[:], in1=rstd_bc[:], op=ALU.mult), ("orT", "rstd_bc"), ("tmp64",))
    C("dve", lambda e: e.scalar_tensor_tensor(out=mixT_s[:, 0:64], in0=tmp64[:], scalar=rgn_col[:, 0:1], in1=gT, op0=ALU.mult, op1=ALU.mult), ("tmp64", "rgn_col", "featT"), ("mixT_r",))
    C("dve", lambda e: e.tensor_tensor(out=prod_s[:], in0=p_tok[:, 4, :], in1=p_tok[:, 5, :], op=ALU.mult), (("p_tok", 4), ("p_tok", 5)), ("q_t",))
    C("dve", lambda e: e.tensor_reduce(out=s_self[:], in_=prod_s[:].rearrange("p (a d) -> p a d", d=64), axis=AX.X, op=ALU.add), ("q_t",), ("s_self",))
    C("act", lambda e: e.activation(out=p_self[:], in_=s_self[:], func=AF.Exp, scale=0.125), ("s_self",), ("p_self",))
    C("dve", lambda e: e.tensor_tensor(out=Pbd[:], in0=p_self[:].unsqueeze(1).to_broadcast([NS, NS, 8]), in1=sel[:, :, 0:8], op=ALU.mult), ("p_self", "sel"), ("Pbd",))
    steps = [(b, half) for b in range(NS) for half in range(2)]

    def gathers(i):
        b, half = steps[i]
        ci = 2 * b + half
        kb_, kk_ = Kbs[i % 2], Kkeys[i % 2]
        P.dma("pool", lambda e: e.indirect_dma_start(out=kb_[:], out_offset=None, in_=ck_d, in_offset=bass.IndirectOffsetOnAxis(ap=idx_t[:, ci:ci + 1], axis=0)),
              K(("s_kb", i % 2)), ("idx_t",), kk_)
        P.dma("pool", lambda e: e.indirect_dma_start(out=Vb[:], out_offset=None, in_=cv_d, in_offset=bass.IndirectOffsetOnAxis(ap=idx_t[:, ci:ci + 1], axis=0)),
              K("s_vb"), ("idx_t",), ("Vb",))

    gathers(0)
    for i, (b, half) in enumerate(steps):
        kb_, kk_ = Kbs[i % 2], Kkeys[i % 2]
        vb_, vk_ = Vbbs[i % 2], Vkeys[i % 2]
        kb3 = kb_[:].rearrange("p (j c) -> p j c", j=8)
        vb3 = vb_[:].rearrange("p (j c) -> p j c", j=8)
        if half == 0:
            C("pe", lambda e, b=b: e.matmul(out=PS[0][:], lhsT=sel[:, b, :], rhs=p_tok[:, 4, :], start=True, stop=True), ("sel", ("p_tok", 4)), ("ps0q",), (0,))
            C("act", lambda e: e.activation(out=qb_s[:], in_=PS[0][:], func=AF.Copy), ("ps0q",), ("qb_s",), (0,))
        C("act", lambda e, vb_=vb_: e.activation(out=vb_[:], in_=Vb[:], func=AF.Copy), ("Vb",), vk_)
        if i + 1 < len(steps):
            gathers(i + 1)
        C("dve", lambda e, kb3=kb3: e.tensor_tensor(out=kb3, in0=kb3, in1=qb_s[:].unsqueeze(1).to_broadcast([128, 8, 512]), op=ALU.mult), kk_ + ("qb_s",), kk_)
        C("dve", lambda e, kb_=kb_: e.tensor_reduce(out=sc_s[:], in_=kb_[:].rearrange("p (a d) -> p a d", d=64), axis=AX.X, op=ALU.add), kk_, ("sc_s",))
        C("act", lambda e: e.activation(out=pexp[:], in_=sc_s[:], func=AF.Exp, scale=0.125), ("sc_s",), ("pexp",))
        C("dve", lambda e: e.tensor_reduce(out=ps8[:], in_=pexp[:].rearrange("p (j c) -> p c j", j=8), axis=AX.X, op=ALU.add), ("pexp",), ("ps8",))
        C("pe", lambda e, b=b, half=half: e.matmul(out=PS[1][:, b * 8:(b + 1) * 8], lhsT=ones_f[:], rhs=ps8[:], start=False, stop=(half == 1), skip_group_check=True),
          ("ones_f", "ps8", "psL"), ("psL",), (1,))
        for h in range(4):
            for jk in range(8):
                C("pe", lambda e, b=b, h=h, jk=jk, half=half, vb3=vb3: e.matmul(out=PS[2][:, b * 8 + 2 * h: b * 8 + 2 * h + 2], lhsT=vb3[:, jk, h * 128:(h + 1) * 128],
                                                                          rhs=pexp[:, jk * 8 + 2 * h: jk * 8 + 2 * h + 2], start=False, stop=(half == 1 and jk == 7), skip_group_check=True),
                  vk_ + ("pexp", "psO"), ("psO",), (2,))
    C("act", lambda e: e.activation(out=OT[:], in_=PS[2][:, 0:128], func=AF.Copy), ("psO",), ("OT",), (2,))
    C("act", lambda e: e.activation(out=Lc[:], in_=PS[1][:, 0:128], func=AF.Copy), ("psL",), ("Lc",), (1,))
    C("pe", lambda e: e.matmul(out=PS[3][:, 0:128], lhsT=ones_f[0:NS, :], rhs=Pbd[:].rearrange("p b c -> p (b c)"), start=True, stop=True), ("ones_f", "Pbd"), ("ps3p",), (3,))
    C("act", lambda e: e.activation(out=psb[:], in_=PS[3][:, 0:128], func=AF.Copy), ("ps3p",), ("psb",), (3,))
    for m_ in range(2):
        C("dve", lambda e, m_=m_: e.tensor_tensor(out=tmpO[:, m_:128:2].rearrange("p (b h) -> p b h", h=4), in0=psb[:, m_:128:2].rearrange("p (b h) -> p b h", h=4),
                                                 in1=avT.rearrange("p (h b) -> p b h", h=4), op=ALU.mult), ("psb", "featT"), (("tmpO", m_),))
    C("dve", lambda e: e.tensor_tensor(out=OT[:], in0=OT[:], in1=tmpO[:], op=ALU.add), ("OT", ("tmpO", 0), ("tmpO", 1)), ("OT",))
    C("dve", lambda e: e.tensor_tensor(out=Lc[:], in0=Lc[:], in1=psb[:], op=ALU.add), ("Lc", "psb"), ("Lc",))
    C("dve", lambda e: e.reciprocal(out=Rl[:], in_=Lc[:]), ("Lc",), ("Rl",))
    C("dve", lambda e: e.tensor_tensor(out=OT[:], in0=OT[:], in1=Rl[:], op=ALU.mult), ("OT", "Rl"), ("OT",))
    C("dve", lambda e: e.scalar_tensor_tensor(out=dif_s[:], in0=OT[:, 1:128:2], scalar=nlam[:, 0:1], in1=OT[:, 0:128:2], op0=ALU.mult, op1=ALU.add), ("OT", "nlam"), ("dif_s",))
    C("dve", lambda e: e.tensor_tensor(out=sqa[:], in0=dif_s[:], in1=dif_s[:], op=ALU.mult), ("dif_s",), ("sqa",))
    C("pe", lambda e: e.matmul(out=PS[3][:, 128:192], lhsT=ones_f[:], rhs=sqa[:], start=True, stop=True), ("ones_f", "sqa"), ("ps3q",), (3,))
    C("act", lambda e: e.activation(out=rstd_a[:], in_=PS[3][:, 128:192], func=AF.Ln, bias=epsc[:, 0:1], scale=1.0 / 128), ("ps3q", "epsc"), ("rstd_a",), (3,))
    C("act", lambda e: e.activation(out=rstd_a[:], in_=rstd_a[:], func=AF.Exp, scale=-0.5), ("rstd_a",), ("rstd_a",))
    C("dve", lambda e: e.tensor_tensor(out=tmpa[:], in0=dif_s[:], in1=rstd_a[:], op=ALU.mult), ("dif_s", "rstd_a"), ("tmpa",))
    C("dve", lambda e: e.tensor_scalar(out=mixT_s[:, 64:128].rearrange("p (h b) -> p b h", h=4), in0=tmpa[:].rearrange("p (b h) -> p b h", h=4), scalar1=asub_col[:, 0:1], scalar2=None, op0=ALU.mult),
      ("tmpa", "asub_col"), ("mixT_a",))
    for nh in range(2):
        for ec in range(8):
            C("pe", lambda e, nh=nh, ec=ec: e.matmul(out=PS[4 + nh][0:NS, :], lhsT=mixT_s[:, ec * NS:(ec + 1) * NS], rhs=wo[:, ec, nh * 512:(nh + 1) * 512], start=(ec == 0), stop=(ec == 7)),
              ("mixT_r", "mixT_a", "wo"), (("psy", nh),), (4 + nh,))
        C("dve", lambda e, nh=nh: e.tensor_tensor(out=hs1[:, nh * 512:(nh + 1) * 512], in0=xs_t[:, nh * 512:(nh + 1) * 512], in1=PS[4 + nh][0:NS, :], op=ALU.add), ("xs_t", ("psy", nh)), (("hs1", nh),), (4 + nh,))
    C("act", lambda e: e.activation(out=junks[:], in_=hs1[:], func=AF.Square, accum_out=ss_s[:, 1:2]), (("hs1", 0), ("hs1", 1)), ("hn_s", "ss_s1"))
    rstd16(ss_s[:, 1:2], rs_s[:, 1:2], "ss_s1", "rs_s1", 1.0 / D)
    C("dve", lambda e: e.scalar_tensor_tensor(out=hn_s[:], in0=hs1[:], scalar=rs_s[:, 1:2], in1=nffn_bc[0:NS, :], op0=ALU.mult, op1=ALU.mult), (("hs1", 0), ("hs1", 1), "rs_s1", "nffn_bc"), ("hn_s",))
    for dc in range(8):
        C("pe", lambda e, dc=dc: e.matmul(out=PS[7][:, dc * NS:(dc + 1) * NS], lhsT=hn_s[:, dc * 128:(dc + 1) * 128], rhs=ident[0:NS, 0:NS], start=True, stop=True),
          ("hn_s", "ident"), ("ps7s",), (7,))
    C("act", lambda e: e.activation(out=h2T_s[:], in_=PS[7][:, 0:8 * NS], func=AF.Copy), ("ps7s",), ("h2T_s",), (7,))
    for fg in range(11):
        ws = wctr[0] % 2; wctr[0] += 1
        P.dma("pool", lambda e, fg=fg, ws=ws: e.dma_start(out=wg[ws][:], in_=w_gate[fg]), K(("wg", ws)), (), (("wg", ws),))
        P.dma("pool", lambda e, fg=fg, ws=ws: e.dma_start(out=wu[ws][:], in_=w_up[fg]), K(("wu", ws)), (), (("wu", ws),))
        P.dma("pool", lambda e, fg=fg, ws=ws: e.dma_start(out=wdn[ws][:], in_=w_down[fg * 256:(fg + 1) * 256, :].rearrange("(kc p) n -> p kc n", p=128)), K(("wdn", ws)), (), (("wdn", ws),))
        for fl in range(2):
            fc = fg * 2 + fl
            for dc in range(8):
                C("pe", lambda e, dc=dc, fl=fl, ws=ws: e.matmul(out=PS[6][:, 0:NS], lhsT=wg[ws][:, dc, fl * 128:(fl + 1) * 128], rhs=h2T_s[:, dc * NS:(dc + 1) * NS], start=(dc == 0), stop=(dc == 7)),
                  ("h2T_s", ("wg", ws)), ("ps6g",), (6,))
            sigm(sgs[:], PS[6][:, 0:NS], ("ps6g",), "sgs", (6,), mulx=True)
            for dc in range(8):
                C("pe", lambda e, dc=dc, fl=fl, ws=ws: e.matmul(out=PS[6][:, NS:2 * NS], lhsT=wu[ws][:, dc, fl * 128:(fl + 1) * 128], rhs=h2T_s[:, dc * NS:(dc + 1) * NS], start=(dc == 0), stop=(dc == 7)),
                  ("h2T_s", ("wu", ws)), ("ps6u",), (6,))
            C("dve", lambda e, fc=fc: e.tensor_tensor(out=aT_s[:, fc * NS:(fc + 1) * NS], in0=sgs[:], in1=PS[6][:, NS:2 * NS], op=ALU.mult), ("sgs", "ps6u"), (("aT_s", fc),), (6,))
            for nh in range(2):
                C("pe", lambda e, fc=fc, fl=fl, nh=nh, ws=ws: e.matmul(out=PS[4 + nh][0:NS, :], lhsT=aT_s[:, fc * NS:(fc + 1) * NS], rhs=wdn[ws][:, fl, nh * 512:(nh + 1) * 512], start=(fc == 0), stop=(fc == NFC - 1)),
                  (("aT_s", fc), ("wdn", ws)), (("psy", nh),), (4 + nh,))
    for nh in range(2):
        C("dve", lambda e, nh=nh: e.tensor_tensor(out=hs1[:, nh * 512:(nh + 1) * 512], in0=hs1[:, nh * 512:(nh + 1) * 512], in1=PS[4 + nh][0:NS, :], op=ALU.add), (("hs1", nh), ("psy", nh)), (("hs1", nh),), (4 + nh,))
    C("act", lambda e: e.activation(out=junks[:], in_=hs1[:], func=AF.Square, accum_out=ss_s[:, 0:1]), (("hs1", 0), ("hs1", 1)), ("hn_s", "ss_s0"))
    rstd16(ss_s[:, 0:1], rs_s[:, 0:1], "ss_s0", "rs_s0", 1.0 / D)
    C("dve", lambda e: e.scalar_tensor_tensor(out=ys_t[:], in0=hs1[:], scalar=rs_s[:, 0:1], in1=nfin_bc[0:NS, :], op0=ALU.mult, op1=ALU.mult), (("hs1", 0), ("hs1", 1), "rs_s0", "nfin_bc"), (("hs1", 0), ("hs1", 1)))
    P.dma("sp", lambda e: e.dma_start(out=y_s, in_=ys_t[:]), K("s_ys"), (("hs1", 0), ("hs1", 1)), ())
    peak_s = A.mark()
    return nc, P, A, dma_keys, dict(peak_ab=peak_ab, peak_c=peak_c, peak_s=peak_s)


_CACHE = {}


def _get_prog():
    if "nc" in _CACHE:
        return _CACHE["nc"]
    nc, P, A, keys, info = build()
    import contextlib
    with contextlib.ExitStack() as st:
        sems = {e: st.enter_context(nc.semaphore("eng_" + e)) for e in Prog.ENGS}
        dsems = {k: st.enter_context(nc.semaphore("d%d" % i)) for i, k in enumerate(keys)}
        block = st.enter_context(nc.Block())
        P.emit(nc, block, sems, dsems)
    _CACHE["nc"] = nc
    _CACHE["info"] = (info, P.n_instr, len(keys))
    return nc


def kernel(x_prompt, x_sample, cache_k, cache_v, state_hgrn, page_table, w_in, w_out, lb_param, r_gnorm,
           lam_q1, lam_k1, lam_q2, lam_k2, a_subln, norm_mix, norm_ffn, w_gate, w_up, w_down, norm_final):
    f = lambda a: np.ascontiguousarray(np.asarray(a, dtype=np.float32))
    x_prompt = f(x_prompt); w_in0 = f(w_in)[0]
    nc = _get_prog()
    ident = np.eye(128, dtype=np.float32)
    tri = np.triu(np.ones((128, 128), np.float32))
    mT = np.tile(np.triu(np.ones((64, 64), np.float32)), (2, 1))
    scm = np.ones((128, 512), np.float32); scm[:, ::64] = 0.0
    lamv = np.stack([f(lam_q1)[0], f(lam_k1)[0], f(lam_q2)[0], f(lam_k2)[0]])
    in_maps = []
    relay = lambda w: np.ascontiguousarray(f(w)[0].reshape(8, 128, 11, 256).transpose(2, 1, 0, 3))
    wg_l, wu_l = relay(w_gate), relay(w_up)
    x_sample = f(x_sample); page_table = np.asarray(page_table, dtype=np.int32); state_hgrn = f(state_hgrn)
    ck2 = f(cache_k)[0].reshape(NPHYS * 16, 4096)
    cv2 = f(cache_v)[0].reshape(NPHYS * 16, 4096)
    sel = np.zeros((NS, NS, 128), np.float32)
    for b in range(NS):
        sel[b, b, :] = 1.0
    subi = (np.arange(128) % 16).astype(np.int32).reshape(128, 1)
    for c in range(NCORES):
        s, h = c // 4, c % 4
        col = lambda g: w_in0[:, g * 512 + h * 128: g * 512 + (h + 1) * 128]
        w_fm = np.ascontiguousarray(np.concatenate([col(0), col(1), col(4), col(5)], axis=1))
        w_tm = np.ascontiguousarray(np.concatenate([col(2), col(3), col(5), col(6)], axis=1))
        gidx = np.zeros((128, 64), np.int32)
        for tt in range(4):
            for tb in range(4):
                for r in range(4):
                    gidx[:, (tt * 4 + tb) * 4 + r] = r * 2048 + 512 * h + tb * 128 + np.arange(128)
        in_maps.append(dict(
            xp=x_prompt[s], w_fm=w_fm, w_tm=w_tm, w_out=f(w_out)[0], w_gate=wg_l, w_up=wu_l, w_down=f(w_down)[0],
            lbh=np.ascontiguousarray(f(lb_param)[:, h * 128:(h + 1) * 128].T), rgn=f(r_gnorm), lamv=lamv, asub=f(a_subln),
            nmix=f(norm_mix), nffn=f(norm_ffn), nfin=f(norm_final).reshape(1, D), ident=ident, tri=tri, mT=mT, scm=scm, gidx=gidx,
            xs=np.ascontiguousarray(x_sample[NS * c:NS * (c + 1), 0, :]), w_in_s=w_in0, lbp=f(lb_param), sel=sel,
            rgn_col=np.ascontiguousarray(f(r_gnorm)[0].reshape(128, 1)), asub_col=np.ascontiguousarray(f(a_subln)[0].reshape(128, 1)),
            ptl=np.ascontiguousarray(page_table[NS * c:NS * (c + 1)].reshape(NS, 2, 8).transpose(2, 0, 1).reshape(8, 2 * NS)),
            subi=subi, cache_k=ck2, cache_v=cv2, state=np.ascontiguousarray(state_hgrn[0, NS * c:NS * (c + 1)]),
            xres=np.ascontiguousarray(np.concatenate([x_prompt[s, 2048 * q + 512 * h: 2048 * q + 512 * (h + 1)] for q in range(4)], axis=0)),
        ))
    res = run_bass_kernel_spmd(nc, in_maps, core_ids=list(range(NCORES)))
    R = res.results
    y_prompt = np.zeros((2, T, D), np.float32)
    k_prompt = np.zeros((1, 2, T, 4, 2, 64), np.float32)
    v_prompt = np.zeros((1, 2, T, 4, 128), np.float32)
    s_prompt = np.zeros((1, 2, 4, 128, 128), np.float32)
    for c in range(NCORES):
        s, h = c // 4, c % 4
        for q in range(4):
            y_prompt[s, 2048 * q + 512 * h: 2048 * q + 512 * (h + 1)] = R[c]["y_p"][512 * q: 512 * (q + 1)]
        k_prompt[0, s, :, h] = R[c]["k_p"].reshape(T, 2, 64)
        v_prompt[0, s, :, h] = R[c]["v_p"]
        s_prompt[0, s, h] = R[c]["s_p"]
    y_sample = np.zeros((128, 1, D), np.float32)
    k_sample = np.zeros((1, 128, 1, 4, 2, 64), np.float32)
    v_sample = np.zeros((1, 128, 1, 4, 128), np.float32)
    s_sample = np.zeros((1, 128, 4, 128, 128), np.float32)
    for c in range(NCORES):
        if "y_s" not in R[c]:
            break
        y_sample[NS * c:NS * (c + 1), 0] = R[c]["y_s"]
        k_sample[0, NS * c:NS * (c + 1), 0] = R[c]["k_s"].reshape(NS, 4, 2, 64)
        v_sample[0, NS * c:NS * (c + 1), 0] = R[c]["v_s"].reshape(NS, 4, 128)
        s_sample[0, NS * c:NS * (c + 1)] = R[c]["s_s"]
    return (y_prompt, y_sample, k_prompt, v_prompt, s_prompt, k_sample, v_sample, s_sample)
```

```python
import numpy as np
import concourse.bass as bass
import concourse.mybir as mybir
from concourse.bass_utils import run_bass_kernel_spmd

F32 = mybir.dt.float32
BF16 = mybir.dt.bfloat16
I32 = mybir.dt.int32
ALU = mybir.AluOpType
AF = mybir.ActivationFunctionType
AX = mybir.AxisListType

NCORES = 8
D = 1024
T = 8192
DFF = 2816
NFC = DFF // 128
EPS = 1e-6
LAM_INIT = 0.8 - 0.6
TT = 512
NT = T // TT
NS = 16
NPG = 16
import os as _os
NPHYS = int(_os.environ.get('DBG_NPHYS', '2560'))


class Prog:
    ENGS = ("pe", "act", "dve", "pool", "sp")

    def __init__(self):
        self.ops = []
        self.last_write = {}
        self.readers = {}
        self.last_eng = {}
        self.last_dma = {}
        self.bank_last = {}

    def add(self, eng, fn, reads=(), writes=(), kind="c", key=None, banks=()):
        i = len(self.ops)
        deps = set()
        for b in banks:
            bl = self.bank_last.setdefault(b, {})
            for e2, j2 in bl.items():
                if e2 != eng:
                    deps.add(j2)
            bl[eng] = i
        for r in reads:
            w = self.last_write.get(r)
            if w is not None:
                deps.add(w)
        for w_ in writes:
            w = self.last_write.get(w_)
            if w is not None:
                deps.add(w)
            deps.update(self.readers.get(w_, ()))
        self.ops.append(dict(eng=eng, fn=fn, deps=deps, kind=kind, key=key, signal=False))
        for r in reads:
            self.readers.setdefault(r, []).append(i)
        for w_ in writes:
            self.last_write[w_] = i
            self.readers[w_] = []
        if kind == "c":
            self.last_eng[eng] = i
        else:
            self.last_dma[key] = i
        return i

    def dma(self, eng, fn, key, reads=(), writes=()):
        return self.add(eng, fn, reads, writes, kind="dma", key=key)

    def barrier(self):
        deps = set(self.last_eng.values()) | set(self.last_dma.values())
        for e in self.ENGS:
            self.ops.append(dict(eng=e, fn=None, deps=set(deps), kind="bar", key=None, signal=False))
        self.last_write = {}
        self.readers = {}

    def emit(self, nc, block, sems, dma_sems):
        ops = self.ops
        for op in ops:
            nd = set()
            for d in op["deps"]:
                p = ops[d]
                if p["kind"] == "c" and p["eng"] == op["eng"]:
                    if op["eng"] == "pe" and op["kind"] == "c":
                        continue
                nd.add(d)
            op["deps"] = nd
            for d in nd:
                if ops[d]["kind"] == "c":
                    ops[d]["signal"] = True
        cnt = {e: 0 for e in self.ENGS}
        dcnt = {}
        for op in ops:
            if op["kind"] == "c":
                if op["signal"]:
                    cnt[op["eng"]] += 1
                    op["done"] = ("eng_" + op["eng"], cnt[op["eng"]])
            elif op["kind"] in ("dma", "cc"):
                k = op["key"]
                dcnt[k] = dcnt.get(k, 0) + (16 if op["kind"] == "dma" else 1)
                op["done"] = (k, dcnt[k])
        allsem = dict(dma_sems)
        for e in self.ENGS:
            allsem["eng_" + e] = sems[e]
        streams = {e: [] for e in self.ENGS}
        seen = {e: {} for e in self.ENGS}
        for op in ops:
            e = op["eng"]
            waits = {}
            for d in op["deps"]:
                k, v = ops[d]["done"]
                if seen[e].get(k, 0) >= v:
                    continue
                if waits.get(k, 0) < v:
                    waits[k] = v
            seen[e].update(waits)
            streams[e].append((sorted(waits.items(), key=lambda kv: str(kv[0])), op))
        self.n_instr = {e: len(streams[e]) for e in self.ENGS}

        def run(engobj, lst, final=False):
            for waits, op in lst:
                for k, v in waits:
                    engobj.wait_ge(allsem[k], v)
                if op["kind"] == "bar":
                    continue
                ins = op["fn"](engobj)
                if op["kind"] == "dma":
                    ins.then_inc(allsem[op["done"][0]], 16)
                elif op["kind"] == "cc":
                    ins.then_inc(allsem[op["done"][0]])
                elif op["signal"]:
                    ins.then_inc(allsem[op["done"][0]], 1)
            if final:
                for k, v in dcnt.items():
                    engobj.wait_ge(allsem[k], v)

        block.sync(lambda e: run(e, streams["sp"], final=True))
        block.scalar(lambda e: run(e, streams["act"]))
        block.vector(lambda e: run(e, streams["dve"]))
        block.gpsimd(lambda e: run(e, streams["pool"]))
        block.tensor(lambda e: run(e, streams["pe"]))


class Alloc:
    def __init__(self, nc, limit=196608):
        self.nc = nc
        self.off = 16640
        self.limit = limit
        self.n = 0

    def __call__(self, shape, dtype, name=None):
        esz = 2 if dtype == BF16 else 4
        nbytes = esz
        for d in shape[1:]:
            nbytes *= d
        nbytes = (nbytes + 63) // 64 * 64
        self.n += 1
        h = self.nc.alloc_sbuf_tensor_at(name or f"t{self.n}", list(shape), dtype, offset=self.off)
        self.off += nbytes
        assert self.off <= self.limit, (self.off, name)
        return h

    def at(self, off, shape, dtype, name=None):
        self.n += 1
        return self.nc.alloc_sbuf_tensor_at(name or f"t{self.n}", list(shape), dtype, offset=off)

    def mark(self):
        return self.off

    def reset(self, m):
        self.off = m


import os
DBG_NT = int(os.environ.get('DBG_NT', '16'))
DBG_C = int(os.environ.get('DBG_C', '4'))
DBG_CC = int(os.environ.get('DBG_CC', '1'))
DBG_ST = int(os.environ.get('DBG_ST', '9'))
DBG_SUB = int(os.environ.get('DBG_SUB', '9'))
DBG_X = int(os.environ.get('DBG_X', '0'))
DBG_S = int(os.environ.get('DBG_S', '1'))


def build(with_sample=True):
    nc = bass.Bass("TRN2", target_bir_lowering=False)
    P = Prog()
    dti = lambda n, sh, dt=F32: nc.dram_tensor(n, sh, dt, kind="ExternalInput").ap()
    dto = lambda n, sh, dt=F32: nc.dram_tensor(n, sh, dt, kind="ExternalOutput").ap()
    xp = dti("xp", [T, D])
    w_fm = dti("w_fm", [D, 512])
    w_tm = dti("w_tm", [D, 512])
    w_out = dti("w_out", [D, D])
    w_gate = dti("w_gate", [11, 128, 8, 256])
    w_up = dti("w_up", [11, 128, 8, 256])
    w_down = dti("w_down", [DFF, D])
    lbh = dti("lbh", [128, 2])
    rgn = dti("rgn", [1, 128])
    lamv = dti("lamv", [4, 64])
    asub = dti("asub", [1, 128])
    nmix = dti("nmix", [1, D])
    nffn = dti("nffn", [1, D])
    nfin = dti("nfin", [1, D])
    ident_d = dti("ident", [128, 128])
    tri_d = dti("tri", [128, 128])
    mT_d = dti("mT", [128, 64])
    scm_d = dti("scm", [128, 512])
    gidx_d = dti("gidx", [128, 64], I32)
    y_p = dto("y_p", [2048, D])
    k_p = dto("k_p", [T, 128])
    v_p = dto("v_p", [T, 128])
    s_p = dto("s_p", [128, 128])
    srcs = [nc.dram_tensor(f"mix_src{q}", [2048, 256], BF16).ap() for q in range(4)]
    gaths = [nc.dram_tensor(f"mix_gath{q}", [4 * 2048, 256], BF16).ap() for q in range(4)]

    dma_keys = []
    def K(k):
        if k not in dma_keys:
            dma_keys.append(k)
        return k

    A = Alloc(nc)
    ident_f = A([128, 128], F32); ident = A([128, 128], BF16)
    tri_f = A([128, 128], F32); tri = A([128, 128], BF16)
    mT = A([128, 64], F32)
    scm = A([128, 512], F32)
    rgn_bc = A([128, 128], F32)
    asub_bc = A([128, 128], F32)
    nmix_bc = A([128, D], F32)
    lam_t = A([128, 4, 64], F32); lam_pr = A([128, 2, 64], F32); lam_s = A([128, 2], F32)
    lam_e = A([128, 2], F32); nlam = A([128, 1], F32)
    lbt = A([128, 2], F32); lbd = A([128, 1], F32); lb = A([128, 1], F32); oml = A([128, 1], F32)
    noml = A([128, 1], F32)
    epsc = A([128, 1], F32)
    PS = [nc.alloc_psum_tensor(f"ps{i}", [128, 512], F32) for i in range(8)]

    def C(eng, fn, r=(), w=(), b=()):
        return P.add(eng, fn, r, w, banks=b)

    P.dma("sp", lambda e: e.dma_start(out=ident_f[:], in_=ident_d), K("c_ident"), (), ("ident_f",))
    P.dma("sp", lambda e: e.dma_start(out=tri_f[:], in_=tri_d), K("c_tri"), (), ("tri_f",))
    P.dma("sp", lambda e: e.dma_start(out=mT[:], in_=mT_d), K("c_mT"), (), ("mT",))
    P.dma("sp", lambda e: e.dma_start(out=scm[:], in_=scm_d), K("c_scm"), (), ("scm",))
    P.dma("sp", lambda e: e.dma_start(out=rgn_bc[:], in_=rgn.to_broadcast([128, 128])), K("c_rgn"), (), ("rgn_bc",))
    P.dma("sp", lambda e: e.dma_start(out=asub_bc[:], in_=asub.to_broadcast([128, 128])), K("c_asub"), (), ("asub_bc",))
    P.dma("sp", lambda e: e.dma_start(out=nmix_bc[:], in_=nmix.to_broadcast([128, D])), K("c_nmix"), (), ("nmix_bc",))
    P.dma("sp", lambda e: e.dma_start(out=lam_t[:], in_=lamv.rearrange("(o a) b -> o a b", o=1).to_broadcast([128, 4, 64])), K("c_lam"), (), ("lam_t",))
    P.dma("sp", lambda e: e.dma_start(out=lbt[:], in_=lbh), K("c_lb"), (), ("lbt",))
    C("dve", lambda e: e.memset(epsc[:], EPS), (), ("epsc",))
    C("act", lambda e: e.activation(out=ident[:], in_=ident_f[:], func=AF.Copy), ("ident_f",), ("ident",))
    C("act", lambda e: e.activation(out=tri[:], in_=tri_f[:], func=AF.Copy), ("tri_f",), ("tri",))
    C("dve", lambda e: e.tensor_scalar(out=asub_bc[:], in0=asub_bc[:], scalar1=1.0 - LAM_INIT, scalar2=None, op0=ALU.mult), ("asub_bc",), ("asub_bc",))
    C("dve", lambda e: e.tensor_tensor(out=lam_pr[:], in0=lam_t[:, 0:4:2, :], in1=lam_t[:, 1:4:2, :], op=ALU.mult), ("lam_t",), ("lam_pr",))
    C("dve", lambda e: e.tensor_reduce(out=lam_s[:], in_=lam_pr[:], axis=AX.X, op=ALU.add), ("lam_pr",), ("lam_s",))
    C("act", lambda e: e.activation(out=lam_e[:], in_=lam_s[:], func=AF.Exp), ("lam_s",), ("lam_e",))
    C("dve", lambda e: e.tensor_tensor(out=nlam[:], in0=lam_e[:, 1:2], in1=lam_e[:, 0:1], op=ALU.subtract), ("lam_e",), ("nlam",))
    C("dve", lambda e: e.tensor_scalar(out=nlam[:], in0=nlam[:], scalar1=-LAM_INIT, scalar2=None, op0=ALU.add), ("nlam",), ("nlam",))
    C("dve", lambda e: e.tensor_tensor(out=lbd[:], in0=lbt[:, 0:1], in1=lbt[:, 1:2], op=ALU.subtract), ("lbt",), ("lbd",))
    C("act", lambda e: e.activation(out=lb[:], in_=lbd[:], func=AF.Sigmoid), ("lbd",), ("lb",))
    C("act", lambda e: e.activation(out=oml[:], in_=lbd[:], func=AF.Sigmoid, scale=-1.0), ("lbd",), ("oml",))
    C("dve", lambda e: e.tensor_scalar(out=noml[:], in0=oml[:], scalar1=-1.0, scalar2=None, op0=ALU.mult), ("oml",), ("noml",))

    mark0 = A.mark()
    wfm = A([128, 8, 512], BF16); wtm = A([128, 8, 512], BF16)
    aqT = A([128, T], BF16); akT = A([128, T], BF16)
    av = A([128, 64, 132], BF16)
    xt = [A([128, 4, D], F32)]
    xn = A([128, 4, D], BF16)
    hT = [A([128, 8, TT], BF16) for _ in range(2)]
    junk = A([128, D], BF16)
    ss = A([128, 4], F32); rs = A([128, 4], F32)
    sig = A([128, TT], F32); logf = A([128, TT], F32); bb = A([128, TT], F32)
    ebs = [A([128, TT], F32) for _ in range(2)]; enb = A([128, TT], F32); kk = A([128, TT], F32); qf = A([128, TT], F32)
    qTts = [A([128, TT], BF16) for _ in range(2)]; kTts = [A([128, TT], BF16) for _ in range(2)]; khT = A([128, TT], BF16)
    kh_toks = [A([128, 4, 128], BF16) for _ in range(2)]; v_toks = [A([128, 4, 128], BF16) for _ in range(2)]; g_toks = [A([128, 4, 128], F32) for _ in range(2)]
    kvst = [A([128, 4, 256], F32) for _ in range(2)]
    S = [A([128, 128], F32) for _ in range(2)]
    Sb = [A([128, 128], BF16) for _ in range(3)]
    ATm = [A([128, 64], BF16) for _ in range(2)]
    pT = [[A([128, TT], BF16) for _ in range(2)] for _ in range(2)]
    t1 = A([128, 128], F32); dif = A([128, 128], F32); hn1 = A([128, 128], F32)
    rl = A([128, 4], F32); sq1 = A([128, 2], F32); rs1 = A([128, 2], F32)
    o_st = [A([128, 4, 256], BF16) for _ in range(2)]

    P.dma("pool", lambda e: e.dma_start(out=wfm[:], in_=w_fm.rearrange("(kc p) n -> p kc n", p=128)), K("w_fm"), (), ("wfm",))
    P.dma("pool", lambda e: e.dma_start(out=wtm[:], in_=w_tm.rearrange("(kc p) n -> p kc n", p=128)), K("w_tm"), (), ("wtm",))
    C("pool", lambda e: e.memset(av[:, :, 128:132], 1.0), (), ("av_ones",))
    C("dve", lambda e: e.memset(S[0][:], 0.0), (), (("S", 0),))
    C("pool", lambda e: e.memset(Sb[0][:], 0.0), (), (("Sb", 0),))

    def rstd_ops(ssum, out, n, rkey, wkey, inv):
        rk = tuple(rkey) if isinstance(rkey, (tuple, list)) and rkey and isinstance(rkey[0], tuple) else (rkey,)
        C("act", lambda e: e.activation(out=out, in_=ssum, func=AF.Ln, bias=epsc[:, 0:1], scale=inv), rk + ("epsc",), (wkey,))
        C("act", lambda e: e.activation(out=out, in_=out, func=AF.Exp, scale=-0.5), (wkey,), (wkey,))

    chunk_ctr = [0]
    cur_free = [0, 1]

    def phase_a(j):
        sl = j % 2
        eb, qTt, kTt, kh_tok, v_tok, g_tok = ebs[sl], qTts[sl], kTts[sl], kh_toks[sl], v_toks[sl], g_toks[sl]
        P.dma("sp", lambda e: e.dma_start(out=xt[0][:], in_=xp[j * TT:(j + 1) * TT, :].rearrange("(tb p) d -> p tb d", p=128)),
              K(("xt", 0)), (), (("xt", 0),))
        for tb in range(4):
            C("act", lambda e, tb=tb: e.activation(out=junk[:], in_=xt[0][:, tb, :], func=AF.Square, accum_out=ss[:, tb:tb + 1]),
              (("xt", 0),), ("junk", ("ss", tb)))
        rstd_ops(ss[:], rs[:], 4, [("ss", tb) for tb in range(4)], "rs", 1.0 / D)
        for tb in range(4):
            C("dve", lambda e, tb=tb: e.scalar_tensor_tensor(out=xn[:, tb, :], in0=xt[0][:, tb, :], scalar=rs[:, tb:tb + 1],
                                                             in1=nmix_bc[:], op0=ALU.mult, op1=ALU.mult),
              (("xt", 0), "rs", "nmix_bc"), (("xn", tb),))
        yield
        for dc in range(8):
            bk = cur_free[dc % 2]
            pb = PS[bk]
            for tb in range(4):
                C("pe", lambda e, dc=dc, tb=tb, pb=pb: e.matmul(out=pb[:, tb * 128:(tb + 1) * 128], lhsT=xn[:, tb, dc * 128:(dc + 1) * 128], rhs=ident[:], start=True, stop=True),
                  (("xn", tb), "ident"), (("ps", bk),), (bk,))
            eng = "act" if dc % 2 == 0 else "dve"
            if eng == "act":
                C("act", lambda e, dc=dc, pb=pb: e.activation(out=hT[sl][:, dc, :], in_=pb[:, 0:TT], func=AF.Copy), (("ps", bk),), (("hT", sl, dc),), (bk,))
            else:
                C("dve", lambda e, dc=dc, pb=pb: e.tensor_copy(out=hT[sl][:, dc, :], in_=pb[:, 0:TT]), (("ps", bk),), (("hT", sl, dc),), (bk,))
            if dc % 2 == 1:
                yield
        hkeys = tuple(("hT", sl, dc) for dc in range(8))
        cs = slice(j * TT, (j + 1) * TT)
        for g in range(4):
            bk = cur_free[g % 2]
            ps = PS[bk]
            for dc in range(8):
                C("pe", lambda e, g=g, dc=dc, ps=ps: e.matmul(out=ps[:], lhsT=wfm[:, dc, g * 128:(g + 1) * 128], rhs=hT[sl][:, dc, :], start=(dc == 0), stop=(dc == 7)),
                  hkeys + ("wfm",), (("ps", bk),), (bk,))
            if g == 0:
                C("act", lambda e, ps=ps: e.activation(out=qf[:], in_=ps[:], func=AF.Silu), (("ps", bk),), ("qf",), (bk,))
            elif g == 1:
                C("act", lambda e, ps=ps: e.activation(out=sig[:], in_=ps[:], func=AF.Sigmoid), (("ps", bk),), ("sig",), (bk,))
            elif g == 2:
                C("dve", lambda e, ps=ps: e.tensor_copy(out=aqT[:, cs], in_=ps[:]), (("ps", bk),), (("aqT", j),), (bk,))
            else:
                C("dve", lambda e, ps=ps: e.tensor_copy(out=akT[:, cs], in_=ps[:]), (("ps", bk),), (("akT", j),), (bk,))
            yield
        ks = j % 2
        for tb in range(4):
            bk = cur_free[tb % 2]
            ps = PS[bk]
            for dc in range(8):
                C("pe", lambda e, tb=tb, dc=dc, ps=ps: e.matmul(out=ps[:], lhsT=hT[sl][:, dc, tb * 128:(tb + 1) * 128], rhs=wtm[:, dc, :], start=(dc == 0), stop=(dc == 7)),
                  hkeys + ("wtm",), (("ps", bk),), (bk,))
            if DBG_X not in (3, 6):
                C("dve", lambda e, tb=tb, ps=ps: e.tensor_copy(out=v_tok[:, tb, :], in_=ps[:, 0:128]), (("ps", bk),), (("v_tok", sl, tb),), (bk,))
            if DBG_X not in (4, 6):
                C("act", lambda e, tb=tb, ps=ps: e.activation(out=g_tok[:, tb, :], in_=ps[:, 128:256], func=AF.Silu), (("ps", bk),), (("g_tok", sl, tb),), (bk,))
            if DBG_X not in (5, 6):
                C("dve", lambda e, tb=tb, ps=ps: e.tensor_scalar(out=kvst[ks][:, tb, :], in0=ps[:, 256:512], scalar1=1.0, scalar2=None, op0=ALU.mult), (("ps", bk),), (("kvst", ks, tb),), (bk,))
            C("dve", lambda e, tb=tb, ps=ps: e.tensor_copy(out=av[:, 4 * j + tb, 0:128], in_=ps[:, 384:512]), (("ps", bk),), (("av", 4 * j + tb),), (bk,))
            C("dve", lambda e, tb=tb: e.tensor_tensor(out=g_tok[:, tb, :], in0=g_tok[:, tb, :], in1=rgn_bc[:], op=ALU.mult), (("g_tok", sl, tb), "rgn_bc"), (("g_tok", sl, tb),))
            yield
        kvk = tuple(("kvst", ks, tb) for tb in range(4))
        if DBG_X != 2:
            P.dma("sp", lambda e: e.dma_start(out=k_p[j * TT:(j + 1) * TT, :].rearrange("(tb p) c -> p tb c", p=128), in_=kvst[ks][:, :, 0:128]), K(("kst", ks)), kvk, ())
            P.dma("sp", lambda e: e.dma_start(out=v_p[j * TT:(j + 1) * TT, :].rearrange("(tb p) c -> p tb c", p=128), in_=kvst[ks][:, :, 128:256]), K(("vst", ks)), kvk, ())
        yield
        C("act", lambda e: e.activation(out=logf[:], in_=sig[:], func=AF.Ln, bias=lb[:, 0:1], scale=oml[:, 0:1]), ("sig", "lb", "oml"), ("logf",))
        C("dve", lambda e: e.tensor_scalar(out=kk[:], in0=sig[:], scalar1=noml[:, 0:1], scalar2=oml[:, 0:1], op0=ALU.mult, op1=ALU.add), ("sig", "noml", "oml"), ("kk",))
        C("dve", lambda e: e.tensor_tensor_scan(out=bb[:], data0=scm[:], data1=logf[:], initial=0.0, op0=ALU.mult, op1=ALU.add), ("scm", "logf"), ("bb",))
        C("act", lambda e: e.activation(out=eb[:], in_=bb[:], func=AF.Exp), ("bb",), (("eb", sl),))
        C("act", lambda e: e.activation(out=enb[:], in_=bb[:], func=AF.Exp, scale=-1.0), ("bb",), ("enb",))
        C("dve", lambda e: e.tensor_tensor(out=qTt[:], in0=qf[:], in1=eb[:], op=ALU.mult), ("qf", ("eb", sl)), (("qTt", sl),))
        C("dve", lambda e: e.tensor_tensor(out=kTt[:], in0=kk[:], in1=enb[:], op=ALU.mult), ("kk", "enb"), (("kTt", sl),))
        for c in range(8):
            C("dve", lambda e, c=c: e.scalar_tensor_tensor(out=khT[:, c * 64:(c + 1) * 64], in0=enb[:, c * 64:(c + 1) * 64], scalar=eb[:, c * 64 + 63:c * 64 + 64],
                                                           in1=kk[:, c * 64:(c + 1) * 64], op0=ALU.mult, op1=ALU.mult), ("enb", ("eb", sl), "kk"), (("khT", c // 2),))
        yield
        pk = PS[7]
        for tb in range(4):
            C("pe", lambda e, tb=tb: e.matmul(out=pk[:, 0:128], lhsT=khT[:, tb * 128:(tb + 1) * 128], rhs=ident[:], start=True, stop=True),
              (("khT", tb), "ident"), ("ps7k",), (7,))
            C("dve", lambda e, tb=tb: e.tensor_copy(out=kh_tok[:, tb, :], in_=pk[:, 0:128]), ("ps7k",), (("kh_tok", sl, tb),), (7,))

    def hgrn_parts(j, c):
        sl = j % 2
        eb, qTt, kTt, kh_tok, v_tok, g_tok = ebs[sl], qTts[sl], kTts[sl], kh_toks[sl], v_toks[sl], g_toks[sl]
        tb, hf = c // 2, c % 2
        pr = slice(64 * hf, 64 * hf + 64)
        cc = slice(c * 64, (c + 1) * 64)
        am = ATm[hf]
        st = {}

        def part1():
            n = chunk_ctr[0]; chunk_ctr[0] += 1
            cur, nxt = n % 2, (n + 1) % 2
            st["n"] = n
            C("pe", lambda e: e.matmul(out=PS[7][pr, 256:320], lhsT=kTt[:, cc], rhs=qTt[:, cc], start=True, stop=True), (("kTt", sl), ("qTt", sl)), (("psAT", hf),), (7,))
            C("dve", lambda e: e.tensor_tensor(out=am[pr, :], in0=PS[7][pr, 256:320], in1=mT[pr, :], op=ALU.mult), (("psAT", hf), "mT"), (("ATm", hf),), (7,))
            C("pe", lambda e: e.matmul(out=PS[7][:, 320:448], lhsT=kh_tok[pr, tb, :], rhs=v_tok[pr, tb, :], start=True, stop=True), (("kh_tok", sl, tb), ("v_tok", sl, tb)), ("psU",), (7,))
            C("dve", lambda e: e.scalar_tensor_tensor(out=S[nxt][:], in0=S[cur][:], scalar=eb[:, c * 64 + 63:c * 64 + 64], in1=PS[7][:, 320:448], op0=ALU.mult, op1=ALU.add),
              (("S", cur), ("eb", sl), "psU"), (("S", nxt),), (7,))
            C("act", lambda e: e.activation(out=Sb[(n + 1) % 3][:], in_=S[nxt][:], func=AF.Copy), (("S", nxt),), (("Sb", (n + 1) % 3),))

        def part2():
            cur = st["n"] % 3
            ops_ = PS[7][pr, 128:256]
            C("pe", lambda e: e.matmul(out=ops_, lhsT=qTt[:, cc], rhs=Sb[cur][:], start=True, stop=False), (("qTt", sl), ("Sb", cur)), (("pso", hf),), (7,))
            C("pe", lambda e: e.matmul(out=ops_, lhsT=am[pr, :], rhs=v_tok[pr, tb, :], start=False, stop=True), (("ATm", hf), ("v_tok", sl, tb)), (("pso", hf),), (7,))
            if hf == 1:
                osl = o_st[j % 2]
                full = PS[7][:, 128:256]
                C("act", lambda e: e.activation(out=junk[:, 0:128], in_=full, func=AF.Square, accum_out=sq1[:, 0:1]), (("pso", 0), ("pso", 1)), ("junk", "sq1h"), (7,))
                rstd_ops(sq1[:, 0:1], rs1[:, 0:1], 1, "sq1h", "rs1h", 1.0 / 128)
                C("dve", lambda e: e.scalar_tensor_tensor(out=osl[:, tb, 0:128], in0=full, scalar=rs1[:, 0:1], in1=g_tok[:, tb, :], op0=ALU.mult, op1=ALU.mult),
                  (("pso", 0), ("pso", 1), "rs1h", ("g_tok", sl, tb)), (("o_st", j % 2, "r", tb),), (7,))
        return part1, part2

    def attn_pairs(j):
        nkb = 4 * j + 4
        qs_ = slice(j * TT, (j + 1) * TT)
        osl = o_st[j % 2]

        def acc(m, qs):
            i = m * 4 + qs
            return PS[4 + i // 3], (i % 3) * 129, i // 3

        def scores(kb):
            sl = kb % 2
            r = kb - 4 * j
            q0 = 128 * r if r > 0 else 0
            for m in range(2):
                ps = PS[2 * m + sl]
                C("pe", lambda e, m=m, ps=ps, q0=q0: e.matmul(out=ps[:, q0:TT], lhsT=akT[64 * m:64 * m + 64, kb * 128:(kb + 1) * 128],
                                                             rhs=aqT[64 * m:64 * m + 64, j * TT + q0:(j + 1) * TT], start=True, stop=True),
                  (("akT", kb // 4), ("aqT", j)), (("ps", 2 * m + sl),), (2 * m + sl,))

        def exps(kb):
            sl = kb % 2
            r = kb - 4 * j
            q0 = 128 * r if r > 0 else 0
            for m in range(2):
                ps = PS[2 * m + sl]
                C("act", lambda e, m=m, ps=ps, q0=q0: e.activation(out=pT[m][sl][:, q0:TT], in_=ps[:, q0:TT], func=AF.Exp, scale=0.125),
                  (("ps", 2 * m + sl),), (("pT", m, sl),), (2 * m + sl,))
                if r >= 0:
                    C("dve", lambda e, m=m, q0=q0: e.tensor_tensor(out=pT[m][sl][:, q0:q0 + 128], in0=pT[m][sl][:, q0:q0 + 128], in1=tri[:], op=ALU.mult),
                      (("pT", m, sl), "tri"), (("pT", m, sl),))

        def pv(kb):
            sl = kb % 2
            r = kb - 4 * j
            for m in range(2):
                for qs in range(max(r, 0), 4):
                    ps, c0, bank = acc(m, qs)
                    C("pe", lambda e, m=m, qs=qs, ps=ps, c0=c0: e.matmul(out=ps[:, c0:c0 + 129], lhsT=pT[m][sl][:, qs * 128:(qs + 1) * 128], rhs=av[:, kb, 0:129],
                                                                        start=(kb == 0 and c0 == 0), stop=(kb == 4 * j + qs), skip_group_check=True),
                      (("pT", m, sl), ("av", kb), "av_ones"), (("acc", m, qs),), (4 + bank,))
            if r >= 0:
                qs = r
                p0, c00, b0_ = acc(0, qs)
                p1, c01, b1_ = acc(1, qs)
                C("dve", lambda e: e.reciprocal(out=rl[:, 0:1], in_=p0[:, c00 + 128:c00 + 129]), (("acc", 0, qs),), ("rl0",), (4 + b0_,))
                C("dve", lambda e: e.reciprocal(out=rl[:, 1:2], in_=p1[:, c01 + 128:c01 + 129]), (("acc", 1, qs),), ("rl1",), (4 + b1_,))
                C("dve", lambda e: e.tensor_tensor(out=rl[:, 2:3], in0=rl[:, 1:2], in1=nlam[:], op=ALU.mult), ("rl1", "nlam"), ("rl2",))
                C("dve", lambda e: e.tensor_scalar(out=t1[:], in0=p1[:, c01:c01 + 128], scalar1=rl[:, 2:3], scalar2=None, op0=ALU.mult), (("acc", 1, qs), "rl2"), ("t1",), (4 + b1_,))
                C("dve", lambda e: e.scalar_tensor_tensor(out=dif[:], in0=p0[:, c00:c00 + 128], scalar=rl[:, 0:1], in1=t1[:], op0=ALU.mult, op1=ALU.add),
                  (("acc", 0, qs), "rl0", "t1"), ("dif",), (4 + b0_,))
                C("act", lambda e: e.activation(out=junk[:, 128:256], in_=dif[:], func=AF.Square, accum_out=sq1[:, 1:2]), ("dif",), ("junk2", "sq1a"))
                rstd_ops(sq1[:, 1:2], rs1[:, 1:2], 1, "sq1a", "rs1a", 1.0 / 128)
                C("dve", lambda e: e.scalar_tensor_tensor(out=osl[:, qs, 128:256], in0=dif[:], scalar=rs1[:, 1:2], in1=asub_bc[:], op0=ALU.mult, op1=ALU.mult),
                  ("dif", "rs1a", "asub_bc"), (("o_st", j % 2, "a", qs),))

        scores(0)
        for kb in range(nkb):
            cur_free[:] = [(kb + 1) % 2, 2 + (kb + 1) % 2]
            yield
            if kb + 1 < nkb:
                scores(kb + 1)
            exps(kb)
            pv(kb)

    def side_items(j):
        parts = [hgrn_parts(j, c) for c in range(8)]
        hg_items = []
        for c in range(8):
            hg_items.append(parts[c][0])
            if c >= 1:
                hg_items.append(parts[c - 1][1])
        hg_items.append(parts[7][1])
        pa_gen = phase_a(j + 1) if j + 1 < DBG_NT else iter(())
        def pa_step():
            next(pa_gen, None)
        items = []
        pa_done = [False]
        for k in range(max(len(hg_items), 14)):
            if k < len(hg_items):
                items.append(hg_items[k])
            if k < 14:
                items.append(pa_step)
        def drain():
            for _ in pa_gen:
                pass
        items.append(drain)
        return items

    for _ in phase_a(0):
        pass
    for j in range(DBG_NT):
        items = side_items(j)
        nsteps = 4 * j + 4
        k = 0
        for si, _ in enumerate(attn_pairs(j)):
            rem_steps = nsteps - si
            take = -(-(len(items) - k) // rem_steps)
            for _t in range(take):
                items[k](); k += 1
        while k < len(items):
            items[k](); k += 1
        okeys = tuple(("o_st", j % 2, "r", tb) for tb in range(4)) + tuple(("o_st", j % 2, "a", q) for q in range(4))
        P.dma("sp", lambda e, j=j: e.dma_start(out=srcs[j // 4][(j % 4) * TT:(j % 4 + 1) * TT, :].rearrange("(tb p) c -> p tb c", p=128), in_=o_st[j % 2][:]), K(("ost", j % 2)), okeys, (("src", j),))
        if j % 4 == 3 and DBG_CC:
            q = j // 4
            P.add("pool", lambda e, q=q: e.collective_compute("AllGather", ALU.bypass, replica_groups=[[0, 1, 2, 3], [4, 5, 6, 7]], ins=[srcs[q].opt()], outs=[gaths[q].opt()]),
                  tuple(("src", jj) for jj in range(4 * q, 4 * q + 4)), (("gath", q),), kind="cc", key=K(("cc", q)))
    nfin_chunks = chunk_ctr[0]
    P.dma("sp", lambda e: e.dma_start(out=s_p, in_=S[nfin_chunks % 2][:]), K("s_p"), (("S", nfin_chunks % 2),), ())
    P.barrier()
    peak_ab = A.mark()
    A.reset(mark0)
    xres = dti("xres", [2048, D])
    wo = A([128, 8, D], BF16)
    nffn_bc = A([128, D], F32); nfin_bc = A([128, D], F32)
    wg = [A([128, 8, 256], BF16) for _ in range(2)]
    wu = [A([128, 8, 256], BF16) for _ in range(2)]
    mark_s = A.mark()
    gidx = A([128, 64], I32)
    ogh = A([128, 4096], BF16)
    og = ogh[:].rearrange("p (a b c) -> p a b c", a=4, b=4)
    oT = A([128, 8, TT], BF16)
    wd = A([128, NFC, D], BF16)
    hp1 = A([128, 4, D], F32)
    hn = ogh[:].rearrange("p (a d) -> p a d", a=4)
    h2T = oT
    aT = A([128, NFC, TT], BF16)
    sg = [A([128, TT], F32) for _ in range(2)]
    yout = [A([128, D], F32)] * 2
    junkc = A([128, D], BF16)
    ssc = A([128, 4], F32); rsc = A([128, 4], F32)
    ssd = A([128, 4], F32); rsd = A([128, 4], F32)

    P.dma("sp", lambda e: e.dma_start(out=gidx[:], in_=gidx_d), K("c_gidx"), (), ("gidx",))
    P.dma("sp", lambda e: e.dma_start(out=nffn_bc[:], in_=nffn.to_broadcast([128, D])), K("c_nffn"), (), ("nffn_bc",))
    P.dma("sp", lambda e: e.dma_start(out=nfin_bc[:], in_=nfin.to_broadcast([128, D])), K("c_nfin"), (), ("nfin_bc",))
    P.dma("pool", lambda e: e.dma_start(out=wo[:], in_=w_out.rearrange("(kc p) n -> p kc n", p=128)), K("w_o"), (), ("wo",))
    for q4 in range(2):
        P.dma("pool", lambda e, q4=q4: e.dma_start(out=wd[:, q4 * 11:(q4 + 1) * 11, :], in_=w_down[q4 * 1408:(q4 + 1) * 1408, :].rearrange("(kc p) n -> p kc n", p=128)),
              K(("w_d", q4)), (), (("wd", q4),))
    wdk = (("wd", 0), ("wd", 1))
    wctr = [0]

    for tt in range(DBG_C):
        for tb in range(4):
            for r in range(4):
                col = (tt * 4 + tb) * 4 + r
                P.dma("pool", lambda e, tb=tb, r=r, col=col, tt=tt: e.indirect_dma_start(out=og[:, tb, r, :], out_offset=None, in_=gaths[tt],
                                                                                    in_offset=bass.IndirectOffsetOnAxis(ap=gidx[:, col:col + 1], axis=0)),
                      K(("og", tb)), (("gath", tt), "gidx"), (("og", tb, r), ("hn", tb)))
        P.dma("sp", lambda e, tt=tt: e.dma_start(out=hp1[:], in_=xres[tt * TT:(tt + 1) * TT, :].rearrange("(tb p) d -> p tb d", p=128)), K("hp1"), (), tuple(("hp1", tb) for tb in range(4)))
        for ec in range(8):
            half, r = ec // 4, ec % 4
            pb = PS[ec % 4]
            for tb in range(4):
                C("pe", lambda e, tb=tb, r=r, half=half, pb=pb: e.matmul(out=pb[:, tb * 128:(tb + 1) * 128], lhsT=og[:, tb, r, half * 128:(half + 1) * 128], rhs=ident[:], start=True, stop=True),
                  (("og", tb, r), "ident"), (("ps", ec % 4),), (ec % 4,))
            if ec % 2 == 0:
                C("act", lambda e, ec=ec, pb=pb: e.activation(out=oT[:, ec, :], in_=pb[:, 0:TT], func=AF.Copy), (("ps", ec % 4),), (("oT", ec),), (ec % 4,))
            else:
                C("dve", lambda e, ec=ec, pb=pb: e.tensor_copy(out=oT[:, ec, :], in_=pb[:, 0:TT]), (("ps", ec % 4),), (("oT", ec),), (ec % 4,))
        otk = tuple(("oT", ec) for ec in range(8))
        for tb in range(4):
            for nh in range(2):
                ps = PS[(tb * 2 + nh) % 4]
                pk_ = ("ps", (tb * 2 + nh) % 4)
                for ec in range(8):
                    C("pe", lambda e, tb=tb, nh=nh, ec=ec, ps=ps: e.matmul(out=ps[:], lhsT=oT[:, ec, tb * 128:(tb + 1) * 128], rhs=wo[:, ec, nh * 512:(nh + 1) * 512], start=(ec == 0), stop=(ec == 7)),
                      otk + ("wo",), (pk_,), (pk_[1],))
                C("dve", lambda e, tb=tb, nh=nh, ps=ps: e.tensor_tensor(out=hp1[:, tb, nh * 512:(nh + 1) * 512], in0=hp1[:, tb, nh * 512:(nh + 1) * 512], in1=ps[:], op=ALU.add),
                  (pk_, ("hp1", tb)), (("hp1", tb),), (pk_[1],))
        for tb in range(4):
            C("act", lambda e, tb=tb: e.activation(out=junkc[:], in_=hp1[:, tb, :], func=AF.Square, accum_out=ssc[:, tb:tb + 1]), (("hp1", tb),), ("junkc", ("ssc", tb)))
        rstd_ops(ssc[:], rsc[:], 4, [("ssc", tb) for tb in range(4)], "rsc", 1.0 / D)
        for tb in range(4):
            C("dve", lambda e, tb=tb: e.scalar_tensor_tensor(out=hn[:, tb, :], in0=hp1[:, tb, :], scalar=rsc[:, tb:tb + 1], in1=nffn_bc[:], op0=ALU.mult, op1=ALU.mult),
              (("hp1", tb), "rsc", "nffn_bc"), (("hn", tb),))
        for dc in range(8):
            pb = PS[dc % 4]
            for tb in range(4):
                C("pe", lambda e, dc=dc, tb=tb, pb=pb: e.matmul(out=pb[:, tb * 128:(tb + 1) * 128], lhsT=hn[:, tb, dc * 128:(dc + 1) * 128], rhs=ident[:], start=True, stop=True),
                  (("hn", tb), "ident"), (("ps", dc % 4),), (dc % 4,))
            if dc % 2 == 0:
                C("act", lambda e, dc=dc, pb=pb: e.activation(out=h2T[:, dc, :], in_=pb[:, 0:TT], func=AF.Copy), (("ps", dc % 4),), (("oT", dc),), (dc % 4,))
            else:
                C("dve", lambda e, dc=dc, pb=pb: e.tensor_copy(out=h2T[:, dc, :], in_=pb[:, 0:TT]), (("ps", dc % 4),), (("oT", dc),), (dc % 4,))
        h2k = tuple(("oT", dc) for dc in range(8))
        for fg in range(11):
            ws = wctr[0] % 2; wctr[0] += 1
            P.dma("pool", lambda e, fg=fg, ws=ws: e.dma_start(out=wg[ws][:], in_=w_gate[fg]), K(("wg", ws)), (), (("wg", ws),))
            P.dma("pool", lambda e, fg=fg, ws=ws: e.dma_start(out=wu[ws][:], in_=w_up[fg]), K(("wu", ws)), (), (("wu", ws),))
            for fl in range(2):
                fc = fg * 2 + fl
                pg, pu = PS[4 + fl * 2], PS[5 + fl * 2]
                for dc in range(8):
                    C("pe", lambda e, dc=dc, fl=fl, ws=ws, pg=pg: e.matmul(out=pg[:], lhsT=wg[ws][:, dc, fl * 128:(fl + 1) * 128], rhs=h2T[:, dc, :], start=(dc == 0), stop=(dc == 7)),
                      h2k + (("wg", ws),), (("ps", 4 + fl * 2),), (4 + fl * 2,))
                for dc in range(8):
                    C("pe", lambda e, dc=dc, fl=fl, ws=ws, pu=pu: e.matmul(out=pu[:], lhsT=wu[ws][:, dc, fl * 128:(fl + 1) * 128], rhs=h2T[:, dc, :], start=(dc == 0), stop=(dc == 7)),
                      h2k + (("wu", ws),), (("ps", 5 + fl * 2),), (5 + fl * 2,))
                C("act", lambda e, fl=fl, pg=pg: e.activation(out=sg[fl][:], in_=pg[:], func=AF.Silu), (("ps", 4 + fl * 2),), (("sg", fl),), (4 + fl * 2,))
                C("dve", lambda e, fl=fl, fc=fc, pu=pu: e.tensor_tensor(out=aT[:, fc, :], in0=sg[fl][:], in1=pu[:], op=ALU.mult), (("sg", fl), ("ps", 5 + fl * 2)), (("aT", fc),), (5 + fl * 2,))
        atk = tuple(("aT", fc) for fc in range(NFC))
        for tb in range(4):
            for nh in range(2):
                ps = PS[(tb * 2 + nh) % 4]
                pk_ = ("ps", (tb * 2 + nh) % 4)
                for fc in range(NFC):
                    C("pe", lambda e, tb=tb, nh=nh, fc=fc, ps=ps: e.matmul(out=ps[:], lhsT=aT[:, fc, tb * 128:(tb + 1) * 128], rhs=wd[:, fc, nh * 512:(nh + 1) * 512], start=(fc == 0), stop=(fc == NFC - 1)),
                      atk + wdk, (pk_,), (pk_[1],))
                C("dve", lambda e, tb=tb, nh=nh, ps=ps: e.tensor_tensor(out=hp1[:, tb, nh * 512:(nh + 1) * 512], in0=hp1[:, tb, nh * 512:(nh + 1) * 512], in1=ps[:], op=ALU.add),
                  (pk_, ("hp1", tb)), (("hp1", tb),), (pk_[1],))
        for tb in range(4):
            C("act", lambda e, tb=tb: e.activation(out=junkc[:], in_=hp1[:, tb, :], func=AF.Square, accum_out=ssd[:, tb:tb + 1]), (("hp1", tb),), ("junkc", ("ssd", tb)))
        rstd_ops(ssd[:], rsd[:], 4, [("ssd", tb) for tb in range(4)], "rsd", 1.0 / D)
        for tb in range(4):
            ys = tb % 2
            C("dve", lambda e, tb=tb, ys=ys: e.scalar_tensor_tensor(out=yout[ys][:], in0=hp1[:, tb, :], scalar=rsd[:, tb:tb + 1], in1=nfin_bc[:], op0=ALU.mult, op1=ALU.mult),
              (("hp1", tb), "rsd", "nfin_bc"), (("yout", 0),))
            P.dma("sp", lambda e, tt=tt, tb=tb, ys=ys: e.dma_start(out=y_p[tt * TT + tb * 128: tt * TT + (tb + 1) * 128, :], in_=yout[ys][:]), K(("yout", 0)), (("yout", 0),), ())
    peak_c = A.mark()
    if not with_sample or not DBG_S:
        return nc, P, A, dma_keys, dict(peak_ab=peak_ab, peak_c=peak_c)
    P.barrier()
    A.reset(mark_s)
    xs_d = dti("xs", [NS, D])
    w_in_d = dti("w_in_s", [D, 3584])
    lbp_d = dti("lbp", [2, 512])
    sel_d = dti("sel", [NS, NS, 128])
    rgc_d = dti("rgn_col", [128, 1]); asc_d = dti("asub_col", [128, 1])
    ptl_d = dti("ptl", [8, 2 * NS], I32)
    sub_d = dti("subi", [128, 1], I32)
    ck_d = dti("cache_k", [NPHYS * 16, 4096])
    cv_d = dti("cache_v", [NPHYS * 16, 4096])
    st_d = dti("state", [NS, 4, 128, 128])
    y_s = dto("y_s", [NS, D]); k_s = dto("k_s", [NS, 512]); v_s = dto("v_s", [NS, 512])
    s_s = dto("s_s", [NS, 4, 128, 128])

    xs_t = A([NS, D], F32); xn_s = A([NS, D], BF16)
    ss_s = A([NS, 2], F32); rs_s = A([NS, 2], F32)
    lbp = A([NS, 2, 512], F32); oml_t = A([NS, 512], F32)
    lbd_t = lbp[:, 0, :]; lb_t = lbp[:, 1, :]
    sel = A([NS, NS, 128], F32)
    rgn_col = A([128, 1], F32); asub_col = A([128, 1], F32)
    ones_f = A([128, 128], F32)
    pt_raw = A([128, 2 * NS], I32); subi = A([128, 1], I32); idx_t = A([128, 2 * NS], I32)
    hT_s = A([128, 8 * NS], BF16)
    m_wS = A.mark()
    wS = [A([128, 8, 512], BF16)] * 2
    p_tok = A([NS, 7, 512], F32)
    m_ft = A.mark()
    f_t = A([NS, 512], F32); kk_t = A([NS, 512], F32); q_t = A([NS, 512], F32); g_t = A([NS, 512], F32)
    sig_t = f_t
    featT = A([128, 4, 64], F32)
    m_Sin = A.mark()
    S_in = A([128, NS, 128], F32); S_out = S_in
    KKbd = A([NS, NS, 128], F32)
    orT = A([128, 64], F32); sq_s = A([128, 64], F32); rstd_bc = A([128, 64], F32); tmp64 = A([128, 64], F32)
    mixT_s = A([128, 128], BF16)
    prod_s = q_t; s_self = A([NS, 8], F32); p_self = A([NS, 8], F32); Pbd = A([NS, NS, 8], F32)
    qb_s = A([128, 512], F32)
    Kb = A([128, 4096], F32); Vb = A([128, 4096], F32)
    Kbs = [Kb, A.at(m_Sin, [128, 4096], F32)]
    Vbbs = [A.at(m_wS, [128, 4096], BF16), A.at(m_ft, [128, 4096], BF16)]
    Kkeys = [("Kb",), ("Kb1", "S_in", "KKbd")]
    Vkeys = [("Vbb0", ("wS", 0)), ("Vbb1", "f_t", "kk_t", "q_t", "g_t")]
    sc_s = A([128, 64], F32); pexp = A([128, 64], BF16); ps8 = A([128, 8], F32)
    OT = A([128, 128], F32); Lc = A([128, 128], F32); psb = A([128, 128], F32); tmpO = A([128, 128], F32); Rl = A([128, 128], F32)
    dif_s = A([128, 64], F32); sqa = A([128, 64], F32); rstd_a = A([128, 64], F32); tmpa = A([128, 64], F32)
    hs1 = A([NS, D], F32); hn_s = A([NS, D], BF16); junks = hn_s; h2T_s = A([128, 8 * NS], BF16)
    wdn = [A([128, 2, D], BF16) for _ in range(2)]
    sgs = A([128, NS], F32); aT_s = A([128, NFC * NS], BF16)
    ys_t = hs1

    P.dma("sp", lambda e: e.dma_start(out=xs_t[:], in_=xs_d), K("s_xs"), (), ("xs_t",))
    P.dma("sp", lambda e: e.dma_start(out=lbp[:], in_=lbp_d.rearrange("(o a) n -> o a n", o=1).to_broadcast([NS, 2, 512])), K("s_lbp"), (), ("lbp",))
    P.dma("sp", lambda e: e.dma_start(out=sel[:], in_=sel_d), K("s_sel"), (), ("sel",))
    P.dma("sp", lambda e: e.dma_start(out=rgn_col[:], in_=rgc_d), K("s_rgc"), (), ("rgn_col",))
    P.dma("sp", lambda e: e.dma_start(out=asub_col[:], in_=asc_d), K("s_asc"), (), ("asub_col",))
    P.dma("sp", lambda e: e.dma_start(out=subi[:], in_=sub_d), K("s_sub"), (), ("subi",))
    P.dma("sp", lambda e: e.dma_start(out=pt_raw[:], in_=bass.AP(tensor=ptl_d.tensor, offset=0, ap=[[2 * NS, 8], [0, 16], [1, 2 * NS]])), K("s_pt"), (), ("pt_raw",))
    C("dve", lambda e: e.memset(ones_f[:], 1.0), (), ("ones_f",))
    C("dve", lambda e: e.tensor_scalar(out=asub_col[:], in0=asub_col[:], scalar1=1.0 - LAM_INIT, scalar2=None, op0=ALU.mult), ("asub_col",), ("asub_col",))
    for cidx in range(2 * NS):
        pass
    C("dve", lambda e: e.scalar_tensor_tensor(out=idx_t[:], in0=pt_raw[:], scalar=16, in1=subi[:, 0:1].to_broadcast([128, 2 * NS]), op0=ALU.mult, op1=ALU.add), ("pt_raw", "subi"), ("idx_t",))
    C("dve", lambda e: e.tensor_tensor(out=lbd_t, in0=lbp[:, 0, :], in1=lbp[:, 1, :], op=ALU.subtract), ("lbp",), ("lbp",))
    C("act", lambda e: e.activation(out=oml_t[:], in_=lbd_t, func=AF.Sigmoid, scale=-1.0), ("lbp",), ("oml_t",))
    C("act", lambda e: e.activation(out=lb_t, in_=lbd_t, func=AF.Sigmoid), ("lbp", "oml_t"), ("lbp",))
    C("dve", lambda e: e.memset(PS[1][:], 0.0), (), ("psL",), (1,))
    C("dve", lambda e: e.memset(PS[2][:], 0.0), (), ("psO",), (2,))

    def rstd16(ssum, out, rkey, wkey, inv):
        C("act", lambda e: e.activation(out=out, in_=ssum, func=AF.Ln, bias=epsc[0:NS, 0:1], scale=inv), (rkey, "epsc"), (wkey,))
        C("act", lambda e: e.activation(out=out, in_=out, func=AF.Exp, scale=-0.5), (wkey,), (wkey,))

    C("act", lambda e: e.activation(out=junks[:], in_=xs_t[:], func=AF.Square, accum_out=ss_s[:, 0:1]), ("xs_t",), ("hn_s", "ss_s0"))
    rstd16(ss_s[:, 0:1], rs_s[:, 0:1], "ss_s0", "rs_s0", 1.0 / D)
    C("dve", lambda e: e.scalar_tensor_tensor(out=xn_s[:], in0=xs_t[:], scalar=rs_s[:, 0:1], in1=nmix_bc[0:NS, :], op0=ALU.mult, op1=ALU.mult), ("xs_t", "rs_s0", "nmix_bc"), ("xn_s",))
    for dc in range(8):
        C("pe", lambda e, dc=dc: e.matmul(out=PS[7][:, dc * NS:(dc + 1) * NS], lhsT=xn_s[:, dc * 128:(dc + 1) * 128], rhs=ident[0:NS, 0:NS], start=True, stop=True),
          ("xn_s", "ident"), ("ps7s",), (7,))
    C("act", lambda e: e.activation(out=hT_s[:], in_=PS[7][:, 0:8 * NS], func=AF.Copy), ("ps7s",), ("hT_s",), (7,))
    for g in range(7):
        ws = g % 2
        P.dma("pool", lambda e, g=g, ws=ws: e.dma_start(out=wS[ws][:], in_=w_in_d[:, g * 512:(g + 1) * 512].rearrange("(kc p) n -> p kc n", p=128)), K(("wS", 0)), (), (("wS", 0),))
        bk = 5 + g % 2
        for dc in range(8):
            C("pe", lambda e, dc=dc, ws=ws, bk=bk: e.matmul(out=PS[bk][0:NS, :], lhsT=hT_s[:, dc * NS:(dc + 1) * NS], rhs=wS[ws][:, dc, :], start=(dc == 0), stop=(dc == 7)),
              ("hT_s", ("wS", 0)), (("psp", bk),), (bk,))
        C("dve", lambda e, g=g, bk=bk: e.tensor_scalar(out=p_tok[:, g, :], in0=PS[bk][0:NS, :], scalar1=1.0, scalar2=None, op0=ALU.mult), (("psp", bk),), (("p_tok", g),), (bk,))
    P.dma("sp", lambda e: e.dma_start(out=k_s, in_=p_tok[:, 5, :]), K("s_ks"), (("p_tok", 5),), ())
    P.dma("sp", lambda e: e.dma_start(out=v_s, in_=p_tok[:, 6, :]), K("s_vs"), (("p_tok", 6),), ())
    C("act", lambda e: e.activation(out=sig_t[:], in_=p_tok[:, 1, :], func=AF.Sigmoid), (("p_tok", 1),), ("f_t",))
    C("dve", lambda e: e.tensor_tensor(out=f_t[:], in0=sig_t[:], in1=oml_t[:], op=ALU.mult), ("f_t", "oml_t"), ("f_t",))
    C("dve", lambda e: e.tensor_tensor(out=f_t[:], in0=f_t[:], in1=lb_t, op=ALU.add), ("f_t", "lbp"), ("f_t",))
    C("dve", lambda e: e.tensor_scalar(out=kk_t[:], in0=f_t[:], scalar1=-1.0, scalar2=1.0, op0=ALU.mult, op1=ALU.add), ("f_t",), ("kk_t",))
    C("act", lambda e: e.activation(out=q_t[:], in_=p_tok[:, 0, :], func=AF.Silu), (("p_tok", 0),), ("q_t",))
    C("act", lambda e: e.activation(out=g_t[:], in_=p_tok[:, 3, :], func=AF.Silu), (("p_tok", 3),), ("g_t",))
    srcs_fm = [(f_t, "f_t", None), (q_t, "q_t", None), (g_t, "g_t", None), (p_tok, ("p_tok", 6), 6)]
    for xi, (xt_, xk, gsel) in enumerate(srcs_fm):
        for h in range(4):
            src_ap = (xt_[:, h * 128:(h + 1) * 128] if gsel is None else xt_[:, gsel, h * 128:(h + 1) * 128])
            C("pe", lambda e, xi=xi, h=h, src_ap=src_ap: e.matmul(out=PS[7][:, 128 + xi * 64 + h * NS: 128 + xi * 64 + (h + 1) * NS], lhsT=src_ap, rhs=ident_f[0:NS, 0:NS], start=True, stop=True),
              (xk, "ident_f"), ("ps7f",), (7,))
    C("act", lambda e: e.activation(out=featT[:], in_=PS[7][:, 128:384].rearrange("p (a b) -> p a b", a=4), func=AF.Copy), ("ps7f",), ("featT",), (7,))
    fT, qT, gT, avT = featT[:, 0, :], featT[:, 1, :], featT[:, 2, :], featT[:, 3, :]
    for h in range(4):
        P.dma("sp", lambda e, h=h: e.dma_start(out=S_in[:], in_=st_d[:, h, :, :].rearrange("b k v -> k b v")), K("s_sin"), (), ("S_in",))
        C("dve", lambda e, h=h: e.tensor_tensor(out=KKbd[:], in0=kk_t[:, h * 128:(h + 1) * 128].unsqueeze(1).to_broadcast([NS, NS, 128]), in1=sel[:], op=ALU.mult), ("kk_t", "sel"), ("KKbd",))
        for b in range(NS):
            bk = 4 + b % 2
            col = h * NS + b
            C("pe", lambda e, b=b, h=h, bk=bk: e.matmul(out=PS[bk][:, 0:128], lhsT=KKbd[:, b, :], rhs=p_tok[:, 2, h * 128:(h + 1) * 128], start=True, stop=True),
              ("KKbd", ("p_tok", 2)), (("pso", bk),), (bk,))
            C("dve", lambda e, b=b, bk=bk, col=col: e.scalar_tensor_tensor(out=S_out[:, b, :], in0=S_in[:, b, :], scalar=fT[:, col:col + 1], in1=PS[bk][:, 0:128], op0=ALU.mult, op1=ALU.add),
              ("S_in", "featT", ("pso", bk)), ("S_in",), (bk,))
            C("pe", lambda e, b=b, col=col: e.matmul(out=PS[6][:, col:col + 1], lhsT=S_out[:, b, :], rhs=qT[:, col:col + 1], start=True, stop=True),
              ("S_in", "featT"), ("ps6o",), (6,))
        P.dma("sp", lambda e, h=h: e.dma_start(out=s_s[:, h, :, :].rearrange("b k v -> k b v"), in_=S_out[:]), K("s_sout"), ("S_in",), ())
    C("act", lambda e: e.activation(out=orT[:], in_=PS[6][:, 0:64], func=AF.Copy), ("ps6o",), ("orT",), (6,))
    C("dve", lambda e: e.tensor_tensor(out=sq_s[:], in0=orT[:], in1=orT[:], op=ALU.mult), ("orT",), ("sq_s",))
    C("pe", lambda e: e.matmul(out=PS[6][:, 64:128], lhsT=ones_f[:], rhs=sq_s[:], start=True, stop=True), ("ones_f", "sq_s"), ("ps6q",), (6,))
    C("act", lambda e: e.activation(out=rstd_bc[:], in_=PS[6][:, 64:128], func=AF.Ln, bias=epsc[:, 0:1], scale=1.0 / 128), ("ps6q", "epsc"), ("rstd_bc",), (6,))
    C("act", lambda e: e.activation(out=rstd_bc[:], in_=rstd_bc[:], func=AF.Exp, scale=-0.5), ("rstd_bc",), ("rstd_bc",))
    C("dve", lambda e: e.tensor_tensor(out=tmp64[:], in0=orT[:], in1=rstd_bc[:], op=ALU.mult), ("orT", "rstd_bc"), ("tmp64",))
    C("dve", lambda e: e.scalar_tensor_tensor(out=mixT_s[:, 0:64], in0=tmp64[:], scalar=rgn_col[:, 0:1], in1=gT, op0=ALU.mult, op1=ALU.mult), ("tmp64", "rgn_col", "featT"), ("mixT_r",))
    C("dve", lambda e: e.tensor_tensor(out=prod_s[:], in0=p_tok[:, 4, :], in1=p_tok[:, 5, :], op=ALU.mult), (("p_tok", 4), ("p_tok", 5)), ("q_t",))
    C("dve", lambda e: e.tensor_reduce(out=s_self[:], in_=prod_s[:].rearrange("p (a d) -> p a d", d=64), axis=AX.X, op=ALU.add), ("q_t",), ("s_self",))
    C("act", lambda e: e.activation(out=p_self[:], in_=s_self[:], func=AF.Exp, scale=0.125), ("s_self",), ("p_self",))
    C("dve", lambda e: e.tensor_tensor(out=Pbd[:], in0=p_self[:].unsqueeze(1).to_broadcast([NS, NS, 8]), in1=sel[:, :, 0:8], op=ALU.mult), ("p_self", "sel"), ("Pbd",))
    steps = [(b, half) for b in range(NS) for half in range(2)]

    def gathers(i):
        b, half = steps[i]
        ci = 2 * b + half
        kb_, kk_ = Kbs[i % 2], Kkeys[i % 2]
        P.dma("pool", lambda e: e.indirect_dma_start(out=kb_[:], out_offset=None, in_=ck_d, in_offset=bass.IndirectOffsetOnAxis(ap=idx_t[:, ci:ci + 1], axis=0)),
              K(("s_kb", i % 2)), ("idx_t",), kk_)
        P.dma("pool", lambda e: e.indirect_dma_start(out=Vb[:], out_offset=None, in_=cv_d, in_offset=bass.IndirectOffsetOnAxis(ap=idx_t[:, ci:ci + 1], axis=0)),
              K("s_vb"), ("idx_t",), ("Vb",))

    gathers(0)
    for i, (b, half) in enumerate(steps):
        kb_, kk_ = Kbs[i % 2], Kkeys[i % 2]
        vb_, vk_ = Vbbs[i % 2], Vkeys[i % 2]
        kb3 = kb_[:].rearrange("p (j c) -> p j c", j=8)
        vb3 = vb_[:].rearrange("p (j c) -> p j c", j=8)
        if half == 0:
            C("pe", lambda e, b=b: e.matmul(out=PS[0][:], lhsT=sel[:, b, :], rhs=p_tok[:, 4, :], start=True, stop=True), ("sel", ("p_tok", 4)), ("ps0q",), (0,))
            C("act", lambda e: e.activation(out=qb_s[:], in_=PS[0][:], func=AF.Copy), ("ps0q",), ("qb_s",), (0,))
        C("act", lambda e, vb_=vb_: e.activation(out=vb_[:], in_=Vb[:], func=AF.Copy), ("Vb",), vk_)
        if i + 1 < len(steps):
            gathers(i + 1)
        C("dve", lambda e, kb3=kb3: e.tensor_tensor(out=kb3, in0=kb3, in1=qb_s[:].unsqueeze(1).to_broadcast([128, 8, 512]), op=ALU.mult), kk_ + ("qb_s",), kk_)
        C("dve", lambda e, kb_=kb_: e.tensor_reduce(out=sc_s[:], in_=kb_[:].rearrange("p (a d) -> p a d", d=64), axis=AX.X, op=ALU.add), kk_, ("sc_s",))
        C("act", lambda e: e.activation(out=pexp[:], in_=sc_s[:], func=AF.Exp, scale=0.125), ("sc_s",), ("pexp",))
        C("dve", lambda e: e.tensor_reduce(out=ps8[:], in_=pexp[:].rearrange("p (j c) -> p c j", j=8), axis=AX.X, op=ALU.add), ("pexp",), ("ps8",))
        C("pe", lambda e, b=b, half=half: e.matmul(out=PS[1][:, b * 8:(b + 1) * 8], lhsT=ones_f[:], rhs=ps8[:], start=False, stop=(half == 1), skip_group_check=True),
          ("ones_f", "ps8", "psL"), ("psL",), (1,))
        for h in range(4):
            for jk in range(8):
                C("pe", lambda e, b=b, h=h, jk=jk, half=half, vb3=vb3: e.matmul(out=PS[2][:, b * 8 + 2 * h: b * 8 + 2 * h + 2], lhsT=vb3[:, jk, h * 128:(h + 1) * 128],
                                                                          rhs=pexp[:, jk * 8 + 2 * h: jk * 8 + 2 * h + 2], start=False, stop=(half == 1 and jk == 7), skip_group_check=True),
                  vk_ + ("pexp", "psO"), ("psO",), (2,))
    C("act", lambda e: e.activation(out=OT[:], in_=PS[2][:, 0:128], func=AF.Copy), ("psO",), ("OT",), (2,))
    C("act", lambda e: e.activation(out=Lc[:], in_=PS[1][:, 0:128], func=AF.Copy), ("psL",), ("Lc",), (1,))
    C("pe", lambda e: e.matmul(out=PS[3][:, 0:128], lhsT=ones_f[0:NS, :], rhs=Pbd[:].rearrange("p b c -> p (b c)"), start=True, stop=True), ("ones_f", "Pbd"), ("ps3p",), (3,))
    C("act", lambda e: e.activation(out=psb[:], in_=PS[3][:, 0:128], func=AF.Copy), ("ps3p",), ("psb",), (3,))
    for m_ in range(2):
        C("dve", lambda e, m_=m_: e.tensor_tensor(out=tmpO[:, m_:128:2].rearrange("p (b h) -> p b h", h=4), in0=psb[:, m_:128:2].rearrange("p (b h) -> p b h", h=4),
                                                 in1=avT.rearrange("p (h b) -> p b h", h=4), op=ALU.mult), ("psb", "featT"), (("tmpO", m_),))
    C("dve", lambda e: e.tensor_tensor(out=OT[:], in0=OT[:], in1=tmpO[:], op=ALU.add), ("OT", ("tmpO", 0), ("tmpO", 1)), ("OT",))
    C("dve", lambda e: e.tensor_tensor(out=Lc[:], in0=Lc[:], in1=psb[:], op=ALU.add), ("Lc", "psb"), ("Lc",))
    C("dve", lambda e: e.reciprocal(out=Rl[:], in_=Lc[:]), ("Lc",), ("Rl",))
    C("dve", lambda e: e.tensor_tensor(out=OT[:], in0=OT[:], in1=Rl[:], op=ALU.mult), ("OT", "Rl"), ("OT",))
    C("dve", lambda e: e.scalar_tensor_tensor(out=dif_s[:], in0=OT[:, 1:128:2], scalar=nlam[:, 0:1], in1=OT[:, 0:128:2], op0=ALU.mult, op1=ALU.add), ("OT", "nlam"), ("dif_s",))
    C("dve", lambda e: e.tensor_tensor(out=sqa[:], in0=dif_s[:], in1=dif_s[:], op=ALU.mult), ("dif_s",), ("sqa",))
    C("pe", lambda e: e.matmul(out=PS[3][:, 128:192], lhsT=ones_f[:], rhs=sqa[:], start=True, stop=True), ("ones_f", "sqa"), ("ps3q",), (3,))
    C("act", lambda e: e.activation(out=rstd_a[:], in_=PS[3][:, 128:192], func=AF.Ln, bias=epsc[:, 0:1], scale=1.0 / 128), ("ps3q", "epsc"), ("rstd_a",), (3,))
    C("act", lambda e: e.activation(out=rstd_a[:], in_=rstd_a[:], func=AF.Exp, scale=-0.5), ("rstd_a",), ("rstd_a",))
    C("dve", lambda e: e.tensor_tensor(out=tmpa[:], in0=dif_s[:], in1=rstd_a[:], op=ALU.mult), ("dif_s", "rstd_a"), ("tmpa",))
    C("dve", lambda e: e.tensor_scalar(out=mixT_s[:, 64:128].rearrange("p (h b) -> p b h", h=4), in0=tmpa[:].rearrange("p (b h) -> p b h", h=4), scalar1=asub_col[:, 0:1], scalar2=None, op0=ALU.mult),
      ("tmpa", "asub_col"), ("mixT_a",))
    for nh in range(2):
        for ec in range(8):
            C("pe", lambda e, nh=nh, ec=ec: e.matmul(out=PS[4 + nh][0:NS, :], lhsT=mixT_s[:, ec * NS:(ec + 1) * NS], rhs=wo[:, ec, nh * 512:(nh + 1) * 512], start=(ec == 0), stop=(ec == 7)),
              ("mixT_r", "mixT_a", "wo"), (("psy", nh),), (4 + nh,))
        C("dve", lambda e, nh=nh: e.tensor_tensor(out=hs1[:, nh * 512:(nh + 1) * 512], in0=xs_t[:, nh * 512:(nh + 1) * 512], in1=PS[4 + nh][0:NS, :], op=ALU.add), ("xs_t", ("psy", nh)), (("hs1", nh),), (4 + nh,))
    C("act", lambda e: e.activation(out=junks[:], in_=hs1[:], func=AF.Square, accum_out=ss_s[:, 1:2]), (("hs1", 0), ("hs1", 1)), ("hn_s", "ss_s1"))
    rstd16(ss_s[:, 1:2], rs_s[:, 1:2], "ss_s1", "rs_s1", 1.0 / D)
    C("dve", lambda e: e.scalar_tensor_tensor(out=hn_s[:], in0=hs1[:], scalar=rs_s[:, 1:2], in1=nffn_bc[0:NS, :], op0=ALU.mult, op1=ALU.mult), (("hs1", 0), ("hs1", 1), "rs_s1", "nffn_bc"), ("hn_s",))
    for dc in range(8):
        C("pe", lambda e, dc=dc: e.matmul(out=PS[7][:, dc * NS:(dc + 1) * NS], lhsT=hn_s[:, dc * 128:(dc + 1) * 128], rhs=ident[0:NS, 0:NS], start=True, stop=True),
          ("hn_s", "ident"), ("ps7s",), (7,))
    C("act", lambda e: e.activation(out=h2T_s[:], in_=PS[7][:, 0:8 * NS], func=AF.Copy), ("ps7s",), ("h2T_s",), (7,))
    for fg in range(11):
        ws = wctr[0] % 2; wctr[0] += 1
        P.dma("pool", lambda e, fg=fg, ws=ws: e.dma_start(out=wg[ws][:], in_=w_gate[fg]), K(("wg", ws)), (), (("wg", ws),))
        P.dma("pool", lambda e, fg=fg, ws=ws: e.dma_start(out=wu[ws][:], in_=w_up[fg]), K(("wu", ws)), (), (("wu", ws),))
        P.dma("pool", lambda e, fg=fg, ws=ws: e.dma_start(out=wdn[ws][:], in_=w_down[fg * 256:(fg + 1) * 256, :].rearrange("(kc p) n -> p kc n", p=128)), K(("wdn", ws)), (), (("wdn", ws),))
        for fl in range(2):
            fc = fg * 2 + fl
            for dc in range(8):
                C("pe", lambda e, dc=dc, fl=fl, ws=ws: e.matmul(out=PS[6][:, 0:NS], lhsT=wg[ws][:, dc, fl * 128:(fl + 1) * 128], rhs=h2T_s[:, dc * NS:(dc + 1) * NS], start=(dc == 0), stop=(dc == 7)),
                  ("h2T_s", ("wg", ws)), ("ps6g",), (6,))
            C("act", lambda e: e.activation(out=sgs[:], in_=PS[6][:, 0:NS], func=AF.Silu), ("ps6g",), ("sgs",), (6,))
            for dc in range(8):
                C("pe", lambda e, dc=dc, fl=fl, ws=ws: e.matmul(out=PS[6][:, NS:2 * NS], lhsT=wu[ws][:, dc, fl * 128:(fl + 1) * 128], rhs=h2T_s[:, dc * NS:(dc + 1) * NS], start=(dc == 0), stop=(dc == 7)),
                  ("h2T_s", ("wu", ws)), ("ps6u",), (6,))
            C("dve", lambda e, fc=fc: e.tensor_tensor(out=aT_s[:, fc * NS:(fc + 1) * NS], in0=sgs[:], in1=PS[6][:, NS:2 * NS], op=ALU.mult), ("sgs", "ps6u"), (("aT_s", fc),), (6,))
            for nh in range(2):
                C("pe", lambda e, fc=fc, fl=fl, nh=nh, ws=ws: e.matmul(out=PS[4 + nh][0:NS, :], lhsT=aT_s[:, fc * NS:(fc + 1) * NS], rhs=wdn[ws][:, fl, nh * 512:(nh + 1) * 512], start=(fc == 0), stop=(fc == NFC - 1)),
                  (("aT_s", fc), ("wdn", ws)), (("psy", nh),), (4 + nh,))
    for nh in range(2):
        C("dve", lambda e, nh=nh: e.tensor_tensor(out=hs1[:, nh * 512:(nh + 1) * 512], in0=hs1[:, nh * 512:(nh + 1) * 512], in1=PS[4 + nh][0:NS, :], op=ALU.add), (("hs1", nh), ("psy", nh)), (("hs1", nh),), (4 + nh,))
    C("act", lambda e: e.activation(out=junks[:], in_=hs1[:], func=AF.Square, accum_out=ss_s[:, 0:1]), (("hs1", 0), ("hs1", 1)), ("hn_s", "ss_s0"))
    rstd16(ss_s[:, 0:1], rs_s[:, 0:1], "ss_s0", "rs_s0", 1.0 / D)
    C("dve", lambda e: e.scalar_tensor_tensor(out=ys_t[:], in0=hs1[:], scalar=rs_s[:, 0:1], in1=nfin_bc[0:NS, :], op0=ALU.mult, op1=ALU.mult), (("hs1", 0), ("hs1", 1), "rs_s0", "nfin_bc"), (("hs1", 0), ("hs1", 1)))
    P.dma("sp", lambda e: e.dma_start(out=y_s, in_=ys_t[:]), K("s_ys"), (("hs1", 0), ("hs1", 1)), ())
    peak_s = A.mark()
    return nc, P, A, dma_keys, dict(peak_ab=peak_ab, peak_c=peak_c, peak_s=peak_s)


_CACHE = {}


def _get_prog():
    if "nc" in _CACHE:
        return _CACHE["nc"]
    nc, P, A, keys, info = build()
    import contextlib
    with contextlib.ExitStack() as st:
        sems = {e: st.enter_context(nc.semaphore("eng_" + e)) for e in Prog.ENGS}
        dsems = {k: st.enter_context(nc.semaphore("d%d" % i)) for i, k in enumerate(keys)}
        block = st.enter_context(nc.Block())
        P.emit(nc, block, sems, dsems)
    _CACHE["nc"] = nc
    _CACHE["info"] = (info, P.n_instr, len(keys))
    return nc


def kernel(x_prompt, x_sample, cache_k, cache_v, state_hgrn, page_table, w_in, w_out, lb_param, r_gnorm,
           lam_q1, lam_k1, lam_q2, lam_k2, a_subln, norm_mix, norm_ffn, w_gate, w_up, w_down, norm_final):
    f = lambda a: np.ascontiguousarray(np.asarray(a, dtype=np.float32))
    x_prompt = f(x_prompt); w_in0 = f(w_in)[0]
    nc = _get_prog()
    ident = np.eye(128, dtype=np.float32)
    tri = np.triu(np.ones((128, 128), np.float32))
    mT = np.tile(np.triu(np.ones((64, 64), np.float32)), (2, 1))
    scm = np.ones((128, 512), np.float32); scm[:, ::64] = 0.0
    lamv = np.stack([f(lam_q1)[0], f(lam_k1)[0], f(lam_q2)[0], f(lam_k2)[0]])
    in_maps = []
    relay = lambda w: np.ascontiguousarray(f(w)[0].reshape(8, 128, 11, 256).transpose(2, 1, 0, 3))
    wg_l, wu_l = relay(w_gate), relay(w_up)
    x_sample = f(x_sample); page_table = np.asarray(page_table, dtype=np.int32); state_hgrn = f(state_hgrn)
    ck2 = f(cache_k)[0].reshape(NPHYS * 16, 4096)
    cv2 = f(cache_v)[0].reshape(NPHYS * 16, 4096)
    sel = np.zeros((NS, NS, 128), np.float32)
    for b in range(NS):
        sel[b, b, :] = 1.0
    subi = (np.arange(128) % 16).astype(np.int32).reshape(128, 1)
    for c in range(NCORES):
        s, h = c // 4, c % 4
        col = lambda g: w_in0[:, g * 512 + h * 128: g * 512 + (h + 1) * 128]
        w_fm = np.ascontiguousarray(np.concatenate([col(0), col(1), col(4), col(5)], axis=1))
        w_tm = np.ascontiguousarray(np.concatenate([col(2), col(3), col(5), col(6)], axis=1))
        gidx = np.zeros((128, 64), np.int32)
        for tt in range(4):
            for tb in range(4):
                for r in range(4):
                    gidx[:, (tt * 4 + tb) * 4 + r] = r * 2048 + 512 * h + tb * 128 + np.arange(128)
        in_maps.append(dict(
            xp=x_prompt[s], w_fm=w_fm, w_tm=w_tm, w_out=f(w_out)[0], w_gate=wg_l, w_up=wu_l, w_down=f(w_down)[0],
            lbh=np.ascontiguousarray(f(lb_param)[:, h * 128:(h + 1) * 128].T), rgn=f(r_gnorm), lamv=lamv, asub=f(a_subln),
            nmix=f(norm_mix), nffn=f(norm_ffn), nfin=f(norm_final).reshape(1, D), ident=ident, tri=tri, mT=mT, scm=scm, gidx=gidx,
            xs=np.ascontiguousarray(x_sample[NS * c:NS * (c + 1), 0, :]), w_in_s=w_in0, lbp=f(lb_param), sel=sel,
            rgn_col=np.ascontiguousarray(f(r_gnorm)[0].reshape(128, 1)), asub_col=np.ascontiguousarray(f(a_subln)[0].reshape(128, 1)),
            ptl=np.ascontiguousarray(page_table[NS * c:NS * (c + 1)].reshape(NS, 2, 8).transpose(2, 0, 1).reshape(8, 2 * NS)),
            subi=subi, cache_k=ck2, cache_v=cv2, state=np.ascontiguousarray(state_hgrn[0, NS * c:NS * (c + 1)]),
            xres=np.ascontiguousarray(np.concatenate([x_prompt[s, 2048 * q + 512 * h: 2048 * q + 512 * (h + 1)] for q in range(4)], axis=0)),
        ))
    res = run_bass_kernel_spmd(nc, in_maps, core_ids=list(range(NCORES)))
    R = res.results
    y_prompt = np.zeros((2, T, D), np.float32)
    k_prompt = np.zeros((1, 2, T, 4, 2, 64), np.float32)
    v_prompt = np.zeros((1, 2, T, 4, 128), np.float32)
    s_prompt = np.zeros((1, 2, 4, 128, 128), np.float32)
    for c in range(NCORES):
        s, h = c // 4, c % 4
        for q in range(4):
            y_prompt[s, 2048 * q + 512 * h: 2048 * q + 512 * (h + 1)] = R[c]["y_p"][512 * q: 512 * (q + 1)]
        k_prompt[0, s, :, h] = R[c]["k_p"].reshape(T, 2, 64)
        v_prompt[0, s, :, h] = R[c]["v_p"]
        s_prompt[0, s, h] = R[c]["s_p"]
    y_sample = np.zeros((128, 1, D), np.float32)
    k_sample = np.zeros((1, 128, 1, 4, 2, 64), np.float32)
    v_sample = np.zeros((1, 128, 1, 4, 128), np.float32)
    s_sample = np.zeros((1, 128, 4, 128, 128), np.float32)
    for c in range(NCORES):
        if "y_s" not in R[c]:
            break
        y_sample[NS * c:NS * (c + 1), 0] = R[c]["y_s"]
        k_sample[0, NS * c:NS * (c + 1), 0] = R[c]["k_s"].reshape(NS, 4, 2, 64)
        v_sample[0, NS * c:NS * (c + 1), 0] = R[c]["v_s"].reshape(NS, 4, 128)
        s_sample[0, NS * c:NS * (c + 1)] = R[c]["s_s"]
    return (y_prompt, y_sample, k_prompt, v_prompt, s_prompt, k_sample, v_sample, s_sample)
```

```python
import numpy as np
import concourse.bass as bass
import concourse.mybir as mybir
from concourse.bass_utils import run_bass_kernel_spmd

F32 = mybir.dt.float32
BF16 = mybir.dt.bfloat16
I32 = mybir.dt.int32
ALU = mybir.AluOpType
AF = mybir.ActivationFunctionType
AX = mybir.AxisListType

NCORES = 8
D = 1024
T = 8192
DFF = 2816
NFC = DFF // 128
EPS = 1e-6
LAM_INIT = 0.8 - 0.6
TT = 512
NT = T // TT
NS = 16
NPG = 16
import os as _os
NPHYS = int(_os.environ.get('DBG_NPHYS', '2560'))


class Prog:
    ENGS = ("pe", "act", "dve", "pool", "sp")

    def __init__(self):
        self.ops = []
        self.last_write = {}
        self.readers = {}
        self.last_eng = {}
        self.last_dma = {}
        self.bank_last = {}

    def add(self, eng, fn, reads=(), writes=(), kind="c", key=None, banks=()):
        i = len(self.ops)
        deps = set()
        for b in banks:
            bl = self.bank_last.setdefault(b, {})
            for e2, j2 in bl.items():
                if e2 != eng:
                    deps.add(j2)
            bl[eng] = i
        for r in reads:
            w = self.last_write.get(r)
            if w is not None:
                deps.add(w)
        for w_ in writes:
            w = self.last_write.get(w_)
            if w is not None:
                deps.add(w)
            deps.update(self.readers.get(w_, ()))
        self.ops.append(dict(eng=eng, fn=fn, deps=deps, kind=kind, key=key, signal=False))
        for r in reads:
            self.readers.setdefault(r, []).append(i)
        for w_ in writes:
            self.last_write[w_] = i
            self.readers[w_] = []
        if kind == "c":
            self.last_eng[eng] = i
        else:
            self.last_dma[key] = i
        return i

    def dma(self, eng, fn, key, reads=(), writes=()):
        return self.add(eng, fn, reads, writes, kind="dma", key=key)

    def barrier(self):
        deps = set(self.last_eng.values()) | set(self.last_dma.values())
        for e in self.ENGS:
            self.ops.append(dict(eng=e, fn=None, deps=set(deps), kind="bar", key=None, signal=False))
        self.last_write = {}
        self.readers = {}

    def emit(self, nc, block, sems, dma_sems):
        ops = self.ops
        for op in ops:
            nd = set()
            for d in op["deps"]:
                p = ops[d]
                if p["kind"] == "c" and p["eng"] == op["eng"]:
                    if op["eng"] == "pe" and op["kind"] == "c":
                        continue
                nd.add(d)
            op["deps"] = nd
            for d in nd:
                if ops[d]["kind"] == "c":
                    ops[d]["signal"] = True
        cnt = {e: 0 for e in self.ENGS}
        dcnt = {}
        for op in ops:
            if op["kind"] == "c":
                if op["signal"]:
                    cnt[op["eng"]] += 1
                    op["done"] = ("eng_" + op["eng"], cnt[op["eng"]])
            elif op["kind"] in ("dma", "cc"):
                k = op["key"]
                dcnt[k] = dcnt.get(k, 0) + (16 if op["kind"] == "dma" else 1)
                op["done"] = (k, dcnt[k])
        allsem = dict(dma_sems)
        for e in self.ENGS:
            allsem["eng_" + e] = sems[e]
        streams = {e: [] for e in self.ENGS}
        seen = {e: {} for e in self.ENGS}
        for op in ops:
            e = op["eng"]
            waits = {}
            for d in op["deps"]:
                k, v = ops[d]["done"]
                if seen[e].get(k, 0) >= v:
                    continue
                if waits.get(k, 0) < v:
                    waits[k] = v
            seen[e].update(waits)
            streams[e].append((sorted(waits.items(), key=lambda kv: str(kv[0])), op))
        self.n_instr = {e: len(streams[e]) for e in self.ENGS}

        def run(engobj, lst, final=False):
            for waits, op in lst:
                for k, v in waits:
                    engobj.wait_ge(allsem[k], v)
                if op["kind"] == "bar":
                    continue
                ins = op["fn"](engobj)
                if op["kind"] == "dma":
                    ins.then_inc(allsem[op["done"][0]], 16)
                elif op["kind"] == "cc":
                    ins.then_inc(allsem[op["done"][0]])
                elif op["signal"]:
                    ins.then_inc(allsem[op["done"][0]], 1)
            if final:
                for k, v in dcnt.items():
                    engobj.wait_ge(allsem[k], v)

        block.sync(lambda e: run(e, streams["sp"], final=True))
        block.scalar(lambda e: run(e, streams["act"]))
        block.vector(lambda e: run(e, streams["dve"]))
        block.gpsimd(lambda e: run(e, streams["pool"]))
        block.tensor(lambda e: run(e, streams["pe"]))


class Alloc:
    def __init__(self, nc, limit=196608):
        self.nc = nc
        self.off = 16640
        self.limit = limit
        self.n = 0

    def __call__(self, shape, dtype, name=None):
        esz = 2 if dtype == BF16 else 4
        nbytes = esz
        for d in shape[1:]:
            nbytes *= d
        nbytes = (nbytes + 63) // 64 * 64
        self.n += 1
        h = self.nc.alloc_sbuf_tensor_at(name or f"t{self.n}", list(shape), dtype, offset=self.off)
        self.off += nbytes
        assert self.off <= self.limit, (self.off, name)
        return h

    def at(self, off, shape, dtype, name=None):
        self.n += 1
        return self.nc.alloc_sbuf_tensor_at(name or f"t{self.n}", list(shape), dtype, offset=off)

    def mark(self):
        return self.off

    def reset(self, m):
        self.off = m


import os
DBG_NT = int(os.environ.get('DBG_NT', '16'))
DBG_C = int(os.environ.get('DBG_C', '4'))
DBG_CC = int(os.environ.get('DBG_CC', '1'))
DBG_ST = int(os.environ.get('DBG_ST', '9'))
DBG_SUB = int(os.environ.get('DBG_SUB', '9'))
DBG_X = int(os.environ.get('DBG_X', '0'))
DBG_S = int(os.environ.get('DBG_S', '1'))


def build(with_sample=True):
    nc = bass.Bass("TRN2", target_bir_lowering=False)
    P = Prog()
    dti = lambda n, sh, dt=F32: nc.dram_tensor(n, sh, dt, kind="ExternalInput").ap()
    dto = lambda n, sh, dt=F32: nc.dram_tensor(n, sh, dt, kind="ExternalOutput").ap()
    xp = dti("xp", [T, D])
    w_fm = dti("w_fm", [D, 512])
    w_tm = dti("w_tm", [D, 512])
    w_out = dti("w_out", [D, D])
    w_gate = dti("w_gate", [11, 128, 8, 256])
    w_up = dti("w_up", [11, 128, 8, 256])
    w_down = dti("w_down", [DFF, D])
    lbh = dti("lbh", [128, 2])
    rgn = dti("rgn", [1, 128])
    lamv = dti("lamv", [4, 64])
    asub = dti("asub", [1, 128])
    nmix = dti("nmix", [1, D])
    nffn = dti("nffn", [1, D])
    nfin = dti("nfin", [1, D])
    ident_d = dti("ident", [128, 128])
    tri_d = dti("tri", [128, 128])
    mT_d = dti("mT", [128, 64])
    scm_d = dti("scm", [128, 512])
    gidx_d = dti("gidx", [128, 64], I32)
    y_p = dto("y_p", [2048, D])
    k_p = dto("k_p", [T, 128])
    v_p = dto("v_p", [T, 128])
    s_p = dto("s_p", [128, 128])
    srcs = [nc.dram_tensor(f"mix_src{q}", [2048, 256], BF16).ap() for q in range(4)]
    gaths = [nc.dram_tensor(f"mix_gath{q}", [4 * 2048, 256], BF16).ap() for q in range(4)]

    dma_keys = []
    def K(k):
        if k not in dma_keys:
            dma_keys.append(k)
        return k

    A = Alloc(nc)
    ident_f = A([128, 128], F32); ident = A([128, 128], BF16)
    tri_f = A([128, 128], F32); tri = A([128, 128], BF16)
    mT = A([128, 64], F32)
    scm = A([128, 512], F32)
    rgn_bc = A([128, 128], F32)
    asub_bc = A([128, 128], F32)
    nmix_bc = A([128, D], F32)
    lam_t = A([128, 4, 64], F32); lam_pr = A([128, 2, 64], F32); lam_s = A([128, 2], F32)
    lam_e = A([128, 2], F32); nlam = A([128, 1], F32)
    lbt = A([128, 2], F32); lbd = A([128, 1], F32); lb = A([128, 1], F32); oml = A([128, 1], F32)
    noml = A([128, 1], F32)
    epsc = A([128, 1], F32)
    PS = [nc.alloc_psum_tensor(f"ps{i}", [128, 512], F32) for i in range(8)]

    def C(eng, fn, r=(), w=(), b=()):
        return P.add(eng, fn, r, w, banks=b)

    P.dma("sp", lambda e: e.dma_start(out=ident_f[:], in_=ident_d), K("c_ident"), (), ("ident_f",))
    P.dma("sp", lambda e: e.dma_start(out=tri_f[:], in_=tri_d), K("c_tri"), (), ("tri_f",))
    P.dma("sp", lambda e: e.dma_start(out=mT[:], in_=mT_d), K("c_mT"), (), ("mT",))
    P.dma("sp", lambda e: e.dma_start(out=scm[:], in_=scm_d), K("c_scm"), (), ("scm",))
    P.dma("sp", lambda e: e.dma_start(out=rgn_bc[:], in_=rgn.to_broadcast([128, 128])), K("c_rgn"), (), ("rgn_bc",))
    P.dma("sp", lambda e: e.dma_start(out=asub_bc[:], in_=asub.to_broadcast([128, 128])), K("c_asub"), (), ("asub_bc",))
    P.dma("sp", lambda e: e.dma_start(out=nmix_bc[:], in_=nmix.to_broadcast([128, D])), K("c_nmix"), (), ("nmix_bc",))
    P.dma("sp", lambda e: e.dma_start(out=lam_t[:], in_=lamv.rearrange("(o a) b -> o a b", o=1).to_broadcast([128, 4, 64])), K("c_lam"), (), ("lam_t",))
    P.dma("sp", lambda e: e.dma_start(out=lbt[:], in_=lbh), K("c_lb"), (), ("lbt",))
    C("dve", lambda e: e.memset(epsc[:], EPS), (), ("epsc",))
    C("act", lambda e: e.activation(out=ident[:], in_=ident_f[:], func=AF.Copy), ("ident_f",), ("ident",))
    C("act", lambda e: e.activation(out=tri[:], in_=tri_f[:], func=AF.Copy), ("tri_f",), ("tri",))
    C("dve", lambda e: e.tensor_scalar(out=asub_bc[:], in0=asub_bc[:], scalar1=1.0 - LAM_INIT, scalar2=None, op0=ALU.mult), ("asub_bc",), ("asub_bc",))
    C("dve", lambda e: e.tensor_tensor(out=lam_pr[:], in0=lam_t[:, 0:4:2, :], in1=lam_t[:, 1:4:2, :], op=ALU.mult), ("lam_t",), ("lam_pr",))
    C("dve", lambda e: e.tensor_reduce(out=lam_s[:], in_=lam_pr[:], axis=AX.X, op=ALU.add), ("lam_pr",), ("lam_s",))
    C("act", lambda e: e.activation(out=lam_e[:], in_=lam_s[:], func=AF.Exp), ("lam_s",), ("lam_e",))
    C("dve", lambda e: e.tensor_tensor(out=nlam[:], in0=lam_e[:, 1:2], in1=lam_e[:, 0:1], op=ALU.subtract), ("lam_e",), ("nlam",))
    C("dve", lambda e: e.tensor_scalar(out=nlam[:], in0=nlam[:], scalar1=-LAM_INIT, scalar2=None, op0=ALU.add), ("nlam",), ("nlam",))
    C("dve", lambda e: e.tensor_tensor(out=lbd[:], in0=lbt[:, 0:1], in1=lbt[:, 1:2], op=ALU.subtract), ("lbt",), ("lbd",))
    C("act", lambda e: e.activation(out=lb[:], in_=lbd[:], func=AF.Sigmoid), ("lbd",), ("lb",))
    C("act", lambda e: e.activation(out=oml[:], in_=lbd[:], func=AF.Sigmoid, scale=-1.0), ("lbd",), ("oml",))
    C("dve", lambda e: e.tensor_scalar(out=noml[:], in0=oml[:], scalar1=-1.0, scalar2=None, op0=ALU.mult), ("oml",), ("noml",))

    mark0 = A.mark()
    wfm = A([128, 8, 512], BF16); wtm = A([128, 8, 512], BF16)
    aqT = A([128, T], BF16); akT = A([128, T], BF16)
    av = A([128, 64, 132], BF16)
    xt = [A([128, 4, D], F32)]
    xn = A([128, 4, D], BF16)
    hT = [A([128, 8, TT], BF16) for _ in range(2)]
    junk = A([128, D], BF16)
    ss = A([128, 4], F32); rs = A([128, 4], F32)
    sig = A([128, TT], F32); logf = A([128, TT], F32); bb = A([128, TT], F32)
    ebs = [A([128, TT], F32) for _ in range(2)]; enb = A([128, TT], F32); kk = A([128, TT], F32); qf = A([128, TT], F32)
    qTts = [A([128, TT], BF16) for _ in range(2)]; kTts = [A([128, TT], BF16) for _ in range(2)]; khT = A([128, TT], BF16)
    kh_toks = [A([128, 4, 128], BF16) for _ in range(2)]; v_toks = [A([128, 4, 128], BF16) for _ in range(2)]; g_toks = [A([128, 4, 128], F32) for _ in range(2)]
    kvst = [A([128, 4, 256], F32) for _ in range(2)]
    S = [A([128, 128], F32) for _ in range(2)]
    Sb = [A([128, 128], BF16) for _ in range(3)]
    ATm = [A([128, 64], BF16) for _ in range(2)]
    pT = [[A([128, TT], BF16) for _ in range(3)] for _ in range(2)]
    t1 = A([128, 128], F32); dif = A([128, 128], F32); hn1 = A([128, 128], F32)
    rl = A([128, 4], F32); sq1 = A([128, 2], F32); rs1 = A([128, 2], F32)
    o_st = [A([128, 4, 256], BF16) for _ in range(2)]

    P.dma("pool", lambda e: e.dma_start(out=wfm[:], in_=w_fm.rearrange("(kc p) n -> p kc n", p=128)), K("w_fm"), (), ("wfm",))
    P.dma("pool", lambda e: e.dma_start(out=wtm[:], in_=w_tm.rearrange("(kc p) n -> p kc n", p=128)), K("w_tm"), (), ("wtm",))
    C("pool", lambda e: e.memset(av[:, :, 128:132], 1.0), (), ("av_ones",))
    C("dve", lambda e: e.memset(S[0][:], 0.0), (), (("S", 0),))
    C("pool", lambda e: e.memset(Sb[0][:], 0.0), (), (("Sb", 0),))

    def rstd_ops(ssum, out, n, rkey, wkey, inv):
        rk = tuple(rkey) if isinstance(rkey, (tuple, list)) and rkey and isinstance(rkey[0], tuple) else (rkey,)
        C("act", lambda e: e.activation(out=out, in_=ssum, func=AF.Ln, bias=epsc[:, 0:1], scale=inv), rk + ("epsc",), (wkey,))
        C("act", lambda e: e.activation(out=out, in_=out, func=AF.Exp, scale=-0.5), (wkey,), (wkey,))

    chunk_ctr = [0]
    cur_free = [0, 1]

    def phase_a(j):
        sl = j % 2
        eb, qTt, kTt, kh_tok, v_tok, g_tok = ebs[sl], qTts[sl], kTts[sl], kh_toks[sl], v_toks[sl], g_toks[sl]
        P.dma("sp", lambda e: e.dma_start(out=xt[0][:], in_=xp[j * TT:(j + 1) * TT, :].rearrange("(tb p) d -> p tb d", p=128)),
              K(("xt", 0)), (), (("xt", 0),))
        for tb in range(4):
            C("act", lambda e, tb=tb: e.activation(out=junk[:], in_=xt[0][:, tb, :], func=AF.Square, accum_out=ss[:, tb:tb + 1]),
              (("xt", 0),), ("junk", ("ss", tb)))
        rstd_ops(ss[:], rs[:], 4, [("ss", tb) for tb in range(4)], "rs", 1.0 / D)
        for tb in range(4):
            C("dve", lambda e, tb=tb: e.scalar_tensor_tensor(out=xn[:, tb, :], in0=xt[0][:, tb, :], scalar=rs[:, tb:tb + 1],
                                                             in1=nmix_bc[:], op0=ALU.mult, op1=ALU.mult),
              (("xt", 0), "rs", "nmix_bc"), (("xn", tb),))
        yield
        for dc in range(8):
            bk = cur_free[dc % 2]
            pb = PS[bk]
            for tb in range(4):
                C("pe", lambda e, dc=dc, tb=tb, pb=pb: e.matmul(out=pb[:, tb * 128:(tb + 1) * 128], lhsT=xn[:, tb, dc * 128:(dc + 1) * 128], rhs=ident[:], start=True, stop=True),
                  (("xn", tb), "ident"), (("ps", bk),), (bk,))
            eng = "act" if dc % 2 == 0 else "dve"
            if eng == "act":
                C("act", lambda e, dc=dc, pb=pb: e.activation(out=hT[sl][:, dc, :], in_=pb[:, 0:TT], func=AF.Copy), (("ps", bk),), (("hT", sl, dc),), (bk,))
            else:
                C("dve", lambda e, dc=dc, pb=pb: e.tensor_copy(out=hT[sl][:, dc, :], in_=pb[:, 0:TT]), (("ps", bk),), (("hT", sl, dc),), (bk,))
            if dc % 2 == 1:
                yield
        hkeys = tuple(("hT", sl, dc) for dc in range(8))
        cs = slice(j * TT, (j + 1) * TT)
        for g in range(4):
            bk = cur_free[g % 2]
            ps = PS[bk]
            for dc in range(8):
                C("pe", lambda e, g=g, dc=dc, ps=ps: e.matmul(out=ps[:], lhsT=wfm[:, dc, g * 128:(g + 1) * 128], rhs=hT[sl][:, dc, :], start=(dc == 0), stop=(dc == 7)),
                  hkeys + ("wfm",), (("ps", bk),), (bk,))
            if g == 0:
                C("act", lambda e, ps=ps: e.activation(out=qf[:], in_=ps[:], func=AF.Silu), (("ps", bk),), ("qf",), (bk,))
            elif g == 1:
                C("act", lambda e, ps=ps: e.activation(out=sig[:], in_=ps[:], func=AF.Sigmoid), (("ps", bk),), ("sig",), (bk,))
            elif g == 2:
                C("dve", lambda e, ps=ps: e.tensor_copy(out=aqT[:, cs], in_=ps[:]), (("ps", bk),), (("aqT", j),), (bk,))
            else:
                C("dve", lambda e, ps=ps: e.tensor_copy(out=akT[:, cs], in_=ps[:]), (("ps", bk),), (("akT", j),), (bk,))
            yield
        ks = j % 2
        for tb in range(4):
            bk = cur_free[tb % 2]
            ps = PS[bk]
            for dc in range(8):
                C("pe", lambda e, tb=tb, dc=dc, ps=ps: e.matmul(out=ps[:], lhsT=hT[sl][:, dc, tb * 128:(tb + 1) * 128], rhs=wtm[:, dc, :], start=(dc == 0), stop=(dc == 7)),
                  hkeys + ("wtm",), (("ps", bk),), (bk,))
            if DBG_X not in (3, 6):
                C("dve", lambda e, tb=tb, ps=ps: e.tensor_copy(out=v_tok[:, tb, :], in_=ps[:, 0:128]), (("ps", bk),), (("v_tok", sl, tb),), (bk,))
            if DBG_X not in (4, 6):
                C("act", lambda e, tb=tb, ps=ps: e.activation(out=g_tok[:, tb, :], in_=ps[:, 128:256], func=AF.Silu), (("ps", bk),), (("g_tok", sl, tb),), (bk,))
            if DBG_X not in (5, 6):
                C("dve", lambda e, tb=tb, ps=ps: e.tensor_scalar(out=kvst[ks][:, tb, :], in0=ps[:, 256:512], scalar1=1.0, scalar2=None, op0=ALU.mult), (("ps", bk),), (("kvst", ks, tb),), (bk,))
            if DBG_X != 1:
                C("pool", lambda e, tb=tb: e.tensor_copy(out=av[:, 4 * j + tb, 0:128], in_=kvst[ks][:, tb, 128:256]), (("kvst", ks, tb),), (("av", 4 * j + tb),))
            yield
        kvk = tuple(("kvst", ks, tb) for tb in range(4))
        if DBG_X != 2:
            P.dma("sp", lambda e: e.dma_start(out=k_p[j * TT:(j + 1) * TT, :].rearrange("(tb p) c -> p tb c", p=128), in_=kvst[ks][:, :, 0:128]), K(("kst", ks)), kvk, ())
            P.dma("sp", lambda e: e.dma_start(out=v_p[j * TT:(j + 1) * TT, :].rearrange("(tb p) c -> p tb c", p=128), in_=kvst[ks][:, :, 128:256]), K(("vst", ks)), kvk, ())
        yield
        C("act", lambda e: e.activation(out=logf[:], in_=sig[:], func=AF.Ln, bias=lb[:, 0:1], scale=oml[:, 0:1]), ("sig", "lb", "oml"), ("logf",))
        C("dve", lambda e: e.tensor_scalar(out=kk[:], in0=sig[:], scalar1=noml[:, 0:1], scalar2=oml[:, 0:1], op0=ALU.mult, op1=ALU.add), ("sig", "noml", "oml"), ("kk",))
        C("dve", lambda e: e.tensor_tensor_scan(out=bb[:], data0=scm[:], data1=logf[:], initial=0.0, op0=ALU.mult, op1=ALU.add), ("scm", "logf"), ("bb",))
        C("act", lambda e: e.activation(out=eb[:], in_=bb[:], func=AF.Exp), ("bb",), (("eb", sl),))
        C("act", lambda e: e.activation(out=enb[:], in_=bb[:], func=AF.Exp, scale=-1.0), ("bb",), ("enb",))
        C("dve", lambda e: e.tensor_tensor(out=qTt[:], in0=qf[:], in1=eb[:], op=ALU.mult), ("qf", ("eb", sl)), (("qTt", sl),))
        C("dve", lambda e: e.tensor_tensor(out=kTt[:], in0=kk[:], in1=enb[:], op=ALU.mult), ("kk", "enb"), (("kTt", sl),))
        for c in range(8):
            C("dve", lambda e, c=c: e.scalar_tensor_tensor(out=khT[:, c * 64:(c + 1) * 64], in0=enb[:, c * 64:(c + 1) * 64], scalar=eb[:, c * 64 + 63:c * 64 + 64],
                                                           in1=kk[:, c * 64:(c + 1) * 64], op0=ALU.mult, op1=ALU.mult), ("enb", ("eb", sl), "kk"), (("khT", c // 2),))
        yield
        pk = PS[7]
        for tb in range(4):
            C("pe", lambda e, tb=tb: e.matmul(out=pk[:, 0:128], lhsT=khT[:, tb * 128:(tb + 1) * 128], rhs=ident[:], start=True, stop=True),
              (("khT", tb), "ident"), ("ps7k",), (7,))
            C("dve", lambda e, tb=tb: e.tensor_copy(out=kh_tok[:, tb, :], in_=pk[:, 0:128]), ("ps7k",), (("kh_tok", sl, tb),), (7,))

    def hgrn_parts(j, c):
        sl = j % 2
        eb, qTt, kTt, kh_tok, v_tok, g_tok = ebs[sl], qTts[sl], kTts[sl], kh_toks[sl], v_toks[sl], g_toks[sl]
        tb, hf = c // 2, c % 2
        pr = slice(64 * hf, 64 * hf + 64)
        cc = slice(c * 64, (c + 1) * 64)
        am = ATm[hf]
        st = {}

        def part1():
            n = chunk_ctr[0]; chunk_ctr[0] += 1
            cur, nxt = n % 2, (n + 1) % 2
            st["n"] = n
            C("pe", lambda e: e.matmul(out=PS[7][pr, 256:320], lhsT=kTt[:, cc], rhs=qTt[:, cc], start=True, stop=True), (("kTt", sl), ("qTt", sl)), (("psAT", hf),), (7,))
            C("dve", lambda e: e.tensor_tensor(out=am[pr, :], in0=PS[7][pr, 256:320], in1=mT[pr, :], op=ALU.mult), (("psAT", hf), "mT"), (("ATm", hf),), (7,))
            C("pe", lambda e: e.matmul(out=PS[7][:, 320:448], lhsT=kh_tok[pr, tb, :], rhs=v_tok[pr, tb, :], start=True, stop=True), (("kh_tok", sl, tb), ("v_tok", sl, tb)), ("psU",), (7,))
            C("dve", lambda e: e.scalar_tensor_tensor(out=S[nxt][:], in0=S[cur][:], scalar=eb[:, c * 64 + 63:c * 64 + 64], in1=PS[7][:, 320:448], op0=ALU.mult, op1=ALU.add),
              (("S", cur), ("eb", sl), "psU"), (("S", nxt),), (7,))
            C("act", lambda e: e.activation(out=Sb[(n + 1) % 3][:], in_=S[nxt][:], func=AF.Copy), (("S", nxt),), (("Sb", (n + 1) % 3),))

        def part2():
            cur = st["n"] % 3
            ops_ = PS[7][pr, 128:256]
            C("pe", lambda e: e.matmul(out=ops_, lhsT=qTt[:, cc], rhs=Sb[cur][:], start=True, stop=False), (("qTt", sl), ("Sb", cur)), (("pso", hf),), (7,))
            C("pe", lambda e: e.matmul(out=ops_, lhsT=am[pr, :], rhs=v_tok[pr, tb, :], start=False, stop=True), (("ATm", hf), ("v_tok", sl, tb)), (("pso", hf),), (7,))
            if hf == 1:
                osl = o_st[j % 2]
                full = PS[7][:, 128:256]
                C("act", lambda e: e.activation(out=junk[:, 0:128], in_=full, func=AF.Square, accum_out=sq1[:, 0:1]), (("pso", 0), ("pso", 1)), ("junk", "sq1h"), (7,))
                rstd_ops(sq1[:, 0:1], rs1[:, 0:1], 1, "sq1h", "rs1h", 1.0 / 128)
                C("dve", lambda e: e.scalar_tensor_tensor(out=hn1[:], in0=full, scalar=rs1[:, 0:1], in1=g_tok[:, tb, :], op0=ALU.mult, op1=ALU.mult),
                  (("pso", 0), ("pso", 1), "rs1h", ("g_tok", sl, tb)), ("hn1",), (7,))
                C("pool", lambda e: e.tensor_tensor(out=osl[:, tb, 0:128], in0=hn1[:], in1=rgn_bc[:], op=ALU.mult), ("hn1", "rgn_bc"), (("o_st", j % 2, "r", tb),))
        return part1, part2

    def attn_pairs(j):
        nkb = 4 * j + 4
        qs_ = slice(j * TT, (j + 1) * TT)
        osl = o_st[j % 2]

        def acc(m, qs):
            i = m * 4 + qs
            return PS[4 + i // 3], (i % 3) * 129, i // 3

        def scores(kb):
            sl = kb % 2
            r = kb - 4 * j
            q0 = 128 * r if r > 0 else 0
            for m in range(2):
                ps = PS[2 * m + sl]
                C("pe", lambda e, m=m, ps=ps, q0=q0: e.matmul(out=ps[:, q0:TT], lhsT=akT[64 * m:64 * m + 64, kb * 128:(kb + 1) * 128],
                                                             rhs=aqT[64 * m:64 * m + 64, j * TT + q0:(j + 1) * TT], start=True, stop=True),
                  (("akT", kb // 4), ("aqT", j)), (("ps", 2 * m + sl),), (2 * m + sl,))

        def exps(kb):
            sl = kb % 2
            s3 = kb % 3
            r = kb - 4 * j
            q0 = 128 * r if r > 0 else 0
            for m in range(2):
                ps = PS[2 * m + sl]
                C("act", lambda e, m=m, ps=ps, q0=q0: e.activation(out=pT[m][s3][:, q0:TT], in_=ps[:, q0:TT], func=AF.Exp, scale=0.125),
                  (("ps", 2 * m + sl),), (("pT", m, s3),), (2 * m + sl,))
                if r >= 0:
                    C("dve", lambda e, m=m, q0=q0: e.tensor_tensor(out=pT[m][s3][:, q0:q0 + 128], in0=pT[m][s3][:, q0:q0 + 128], in1=tri[:], op=ALU.mult),
                      (("pT", m, s3), "tri"), (("pT", m, s3),))

        def pv(kb):
            sl = kb % 3
            r = kb - 4 * j
            for m in range(2):
                for qs in range(max(r, 0), 4):
                    ps, c0, bank = acc(m, qs)
                    C("pe", lambda e, m=m, qs=qs, ps=ps, c0=c0: e.matmul(out=ps[:, c0:c0 + 129], lhsT=pT[m][sl][:, qs * 128:(qs + 1) * 128], rhs=av[:, kb, 0:129],
                                                                        start=(kb == 0 and c0 == 0), stop=(kb == 4 * j + qs), skip_group_check=True),
                      (("pT", m, sl), ("av", kb), "av_ones"), (("acc", m, qs),), (4 + bank,))
            if r >= 0:
                qs = r
                p0, c00, b0_ = acc(0, qs)
                p1, c01, b1_ = acc(1, qs)
                C("dve", lambda e: e.reciprocal(out=rl[:, 0:1], in_=p0[:, c00 + 128:c00 + 129]), (("acc", 0, qs),), ("rl0",), (4 + b0_,))
                C("dve", lambda e: e.reciprocal(out=rl[:, 1:2], in_=p1[:, c01 + 128:c01 + 129]), (("acc", 1, qs),), ("rl1",), (4 + b1_,))
                C("dve", lambda e: e.tensor_tensor(out=rl[:, 2:3], in0=rl[:, 1:2], in1=nlam[:], op=ALU.mult), ("rl1", "nlam"), ("rl2",))
                C("dve", lambda e: e.tensor_scalar(out=t1[:], in0=p1[:, c01:c01 + 128], scalar1=rl[:, 2:3], scalar2=None, op0=ALU.mult), (("acc", 1, qs), "rl2"), ("t1",), (4 + b1_,))
                C("dve", lambda e: e.scalar_tensor_tensor(out=dif[:], in0=p0[:, c00:c00 + 128], scalar=rl[:, 0:1], in1=t1[:], op0=ALU.mult, op1=ALU.add),
                  (("acc", 0, qs), "rl0", "t1"), ("dif",), (4 + b0_,))
                C("act", lambda e: e.activation(out=junk[:, 128:256], in_=dif[:], func=AF.Square, accum_out=sq1[:, 1:2]), ("dif",), ("junk2", "sq1a"))
                rstd_ops(sq1[:, 1:2], rs1[:, 1:2], 1, "sq1a", "rs1a", 1.0 / 128)
                C("dve", lambda e: e.scalar_tensor_tensor(out=osl[:, qs, 128:256], in0=dif[:], scalar=rs1[:, 1:2], in1=asub_bc[:], op0=ALU.mult, op1=ALU.mult),
                  ("dif", "rs1a", "asub_bc"), (("o_st", j % 2, "a", qs),))

        scores(0)
        for kb in range(nkb):
            cur_free[:] = [(kb + 1) % 2, 2 + (kb + 1) % 2]
            yield
            if kb + 1 < nkb:
                scores(kb + 1)
            exps(kb)
            if kb >= 1:
                pv(kb - 1)
        pv(nkb - 1)

    def side_items(j):
        parts = [hgrn_parts(j, c) for c in range(8)]
        hg_items = []
        for c in range(8):
            hg_items.append(parts[c][0])
            if c >= 1:
                hg_items.append(parts[c - 1][1])
        hg_items.append(parts[7][1])
        pa_gen = phase_a(j + 1) if j + 1 < DBG_NT else iter(())
        def pa_step():
            next(pa_gen, None)
        items = []
        pa_done = [False]
        for k in range(max(len(hg_items), 14)):
            if k < len(hg_items):
                items.append(hg_items[k])
            if k < 14:
                items.append(pa_step)
        def drain():
            for _ in pa_gen:
                pass
        items.append(drain)
        return items

    for _ in phase_a(0):
        pass
    for j in range(DBG_NT):
        items = side_items(j)
        nsteps = 4 * j + 4
        k = 0
        for si, _ in enumerate(attn_pairs(j)):
            rem_steps = nsteps - si
            take = -(-(len(items) - k) // rem_steps)
            for _t in range(take):
                items[k](); k += 1
        while k < len(items):
            items[k](); k += 1
        okeys = tuple(("o_st", j % 2, "r", tb) for tb in range(4)) + tuple(("o_st", j % 2, "a", q) for q in range(4))
        P.dma("sp", lambda e, j=j: e.dma_start(out=srcs[j // 4][(j % 4) * TT:(j % 4 + 1) * TT, :].rearrange("(tb p) c -> p tb c", p=128), in_=o_st[j % 2][:]), K(("ost", j % 2)), okeys, (("src", j),))
        if j % 4 == 3 and DBG_CC:
            q = j // 4
            P.add("pool", lambda e, q=q: e.collective_compute("AllGather", ALU.bypass, replica_groups=[[0, 1, 2, 3], [4, 5, 6, 7]], ins=[srcs[q].opt()], outs=[gaths[q].opt()]),
                  tuple(("src", jj) for jj in range(4 * q, 4 * q + 4)), (("gath", q),), kind="cc", key=K(("cc", q)))
    nfin_chunks = chunk_ctr[0]
    P.dma("sp", lambda e: e.dma_start(out=s_p, in_=S[nfin_chunks % 2][:]), K("s_p"), (("S", nfin_chunks % 2),), ())
    P.barrier()
    peak_ab = A.mark()
    A.reset(mark0)
    xres = dti("xres", [2048, D])
    wo = A([128, 8, D], BF16)
    nffn_bc = A([128, D], F32); nfin_bc = A([128, D], F32)
    wg = [A([128, 8, 256], BF16) for _ in range(2)]
    wu = [A([128, 8, 256], BF16) for _ in range(2)]
    mark_s = A.mark()
    gidx = A([128, 64], I32)
    ogh = A([128, 4096], BF16)
    og = ogh[:].rearrange("p (a b c) -> p a b c", a=4, b=4)
    oT = A([128, 8, TT], BF16)
    wd = A([128, NFC, D], BF16)
    hp1 = A([128, 4, D], F32)
    hn = ogh[:].rearrange("p (a d) -> p a d", a=4)
    h2T = oT
    aT = A([128, NFC, TT], BF16)
    sg = [A([128, TT], F32) for _ in range(2)]
    yout = [A([128, D], F32)] * 2
    junkc = A([128, D], BF16)
    ssc = A([128, 4], F32); rsc = A([128, 4], F32)
    ssd = A([128, 4], F32); rsd = A([128, 4], F32)

    P.dma("sp", lambda e: e.dma_start(out=gidx[:], in_=gidx_d), K("c_gidx"), (), ("gidx",))
    P.dma("sp", lambda e: e.dma_start(out=nffn_bc[:], in_=nffn.to_broadcast([128, D])), K("c_nffn"), (), ("nffn_bc",))
    P.dma("sp", lambda e: e.dma_start(out=nfin_bc[:], in_=nfin.to_broadcast([128, D])), K("c_nfin"), (), ("nfin_bc",))
    P.dma("pool", lambda e: e.dma_start(out=wo[:], in_=w_out.rearrange("(kc p) n -> p kc n", p=128)), K("w_o"), (), ("wo",))
    for q4 in range(2):
        P.dma("pool", lambda e, q4=q4: e.dma_start(out=wd[:, q4 * 11:(q4 + 1) * 11, :], in_=w_down[q4 * 1408:(q4 + 1) * 1408, :].rearrange("(kc p) n -> p kc n", p=128)),
              K(("w_d", q4)), (), (("wd", q4),))
    wdk = (("wd", 0), ("wd", 1))
    wctr = [0]

    for tt in range(DBG_C):
        for tb in range(4):
            for r in range(4):
                col = (tt * 4 + tb) * 4 + r
                P.dma("pool", lambda e, tb=tb, r=r, col=col, tt=tt: e.indirect_dma_start(out=og[:, tb, r, :], out_offset=None, in_=gaths[tt],
                                                                                    in_offset=bass.IndirectOffsetOnAxis(ap=gidx[:, col:col + 1], axis=0)),
                      K(("og", tb)), (("gath", tt), "gidx"), (("og", tb, r), ("hn", tb)))
        P.dma("sp", lambda e, tt=tt: e.dma_start(out=hp1[:], in_=xres[tt * TT:(tt + 1) * TT, :].rearrange("(tb p) d -> p tb d", p=128)), K("hp1"), (), tuple(("hp1", tb) for tb in range(4)))
        for ec in range(8):
            half, r = ec // 4, ec % 4
            pb = PS[ec % 4]
            for tb in range(4):
                C("pe", lambda e, tb=tb, r=r, half=half, pb=pb: e.matmul(out=pb[:, tb * 128:(tb + 1) * 128], lhsT=og[:, tb, r, half * 128:(half + 1) * 128], rhs=ident[:], start=True, stop=True),
                  (("og", tb, r), "ident"), (("ps", ec % 4),), (ec % 4,))
            if ec % 2 == 0:
                C("act", lambda e, ec=ec, pb=pb: e.activation(out=oT[:, ec, :], in_=pb[:, 0:TT], func=AF.Copy), (("ps", ec % 4),), (("oT", ec),), (ec % 4,))
            else:
                C("dve", lambda e, ec=ec, pb=pb: e.tensor_copy(out=oT[:, ec, :], in_=pb[:, 0:TT]), (("ps", ec % 4),), (("oT", ec),), (ec % 4,))
        otk = tuple(("oT", ec) for ec in range(8))
        for tb in range(4):
            for nh in range(2):
                ps = PS[(tb * 2 + nh) % 4]
                pk_ = ("ps", (tb * 2 + nh) % 4)
                for ec in range(8):
                    C("pe", lambda e, tb=tb, nh=nh, ec=ec, ps=ps: e.matmul(out=ps[:], lhsT=oT[:, ec, tb * 128:(tb + 1) * 128], rhs=wo[:, ec, nh * 512:(nh + 1) * 512], start=(ec == 0), stop=(ec == 7)),
                      otk + ("wo",), (pk_,), (pk_[1],))
                C("dve", lambda e, tb=tb, nh=nh, ps=ps: e.tensor_tensor(out=hp1[:, tb, nh * 512:(nh + 1) * 512], in0=hp1[:, tb, nh * 512:(nh + 1) * 512], in1=ps[:], op=ALU.add),
                  (pk_, ("hp1", tb)), (("hp1", tb),), (pk_[1],))
        for tb in range(4):
            C("act", lambda e, tb=tb: e.activation(out=junkc[:], in_=hp1[:, tb, :], func=AF.Square, accum_out=ssc[:, tb:tb + 1]), (("hp1", tb),), ("junkc", ("ssc", tb)))
        rstd_ops(ssc[:], rsc[:], 4, [("ssc", tb) for tb in range(4)], "rsc", 1.0 / D)
        for tb in range(4):
            C("dve", lambda e, tb=tb: e.scalar_tensor_tensor(out=hn[:, tb, :], in0=hp1[:, tb, :], scalar=rsc[:, tb:tb + 1], in1=nffn_bc[:], op0=ALU.mult, op1=ALU.mult),
              (("hp1", tb), "rsc", "nffn_bc"), (("hn", tb),))
        for dc in range(8):
            pb = PS[dc % 4]
            for tb in range(4):
                C("pe", lambda e, dc=dc, tb=tb, pb=pb: e.matmul(out=pb[:, tb * 128:(tb + 1) * 128], lhsT=hn[:, tb, dc * 128:(dc + 1) * 128], rhs=ident[:], start=True, stop=True),
                  (("hn", tb), "ident"), (("ps", dc % 4),), (dc % 4,))
            if dc % 2 == 0:
                C("act", lambda e, dc=dc, pb=pb: e.activation(out=h2T[:, dc, :], in_=pb[:, 0:TT], func=AF.Copy), (("ps", dc % 4),), (("oT", dc),), (dc % 4,))
            else:
                C("dve", lambda e, dc=dc, pb=pb: e.tensor_copy(out=h2T[:, dc, :], in_=pb[:, 0:TT]), (("ps", dc % 4),), (("oT", dc),), (dc % 4,))
        h2k = tuple(("oT", dc) for dc in range(8))
        for fg in range(11):
            ws = wctr[0] % 2; wctr[0] += 1
            P.dma("pool", lambda e, fg=fg, ws=ws: e.dma_start(out=wg[ws][:], in_=w_gate[fg]), K(("wg", ws)), (), (("wg", ws),))
            P.dma("pool", lambda e, fg=fg, ws=ws: e.dma_start(out=wu[ws][:], in_=w_up[fg]), K(("wu", ws)), (), (("wu", ws),))
            for fl in range(2):
                fc = fg * 2 + fl
                pg, pu = PS[4 + fl * 2], PS[5 + fl * 2]
                for dc in range(8):
                    C("pe", lambda e, dc=dc, fl=fl, ws=ws, pg=pg: e.matmul(out=pg[:], lhsT=wg[ws][:, dc, fl * 128:(fl + 1) * 128], rhs=h2T[:, dc, :], start=(dc == 0), stop=(dc == 7)),
                      h2k + (("wg", ws),), (("ps", 4 + fl * 2),), (4 + fl * 2,))
                for dc in range(8):
                    C("pe", lambda e, dc=dc, fl=fl, ws=ws, pu=pu: e.matmul(out=pu[:], lhsT=wu[ws][:, dc, fl * 128:(fl + 1) * 128], rhs=h2T[:, dc, :], start=(dc == 0), stop=(dc == 7)),
                      h2k + (("wu", ws),), (("ps", 5 + fl * 2),), (5 + fl * 2,))
                C("act", lambda e, fl=fl, pg=pg: e.activation(out=sg[fl][:], in_=pg[:], func=AF.Silu), (("ps", 4 + fl * 2),), (("sg", fl),), (4 + fl * 2,))
                C("dve", lambda e, fl=fl, fc=fc, pu=pu: e.tensor_tensor(out=aT[:, fc, :], in0=sg[fl][:], in1=pu[:], op=ALU.mult), (("sg", fl), ("ps", 5 + fl * 2)), (("aT", fc),), (5 + fl * 2,))
        atk = tuple(("aT", fc) for fc in range(NFC))
        for tb in range(4):
            for nh in range(2):
                ps = PS[(tb * 2 + nh) % 4]
                pk_ = ("ps", (tb * 2 + nh) % 4)
                for fc in range(NFC):
                    C("pe", lambda e, tb=tb, nh=nh, fc=fc, ps=ps: e.matmul(out=ps[:], lhsT=aT[:, fc, tb * 128:(tb + 1) * 128], rhs=wd[:, fc, nh * 512:(nh + 1) * 512], start=(fc == 0), stop=(fc == NFC - 1)),
                      atk + wdk, (pk_,), (pk_[1],))
                C("dve", lambda e, tb=tb, nh=nh, ps=ps: e.tensor_tensor(out=hp1[:, tb, nh * 512:(nh + 1) * 512], in0=hp1[:, tb, nh * 512:(nh + 1) * 512], in1=ps[:], op=ALU.add),
                  (pk_, ("hp1", tb)), (("hp1", tb),), (pk_[1],))
        for tb in range(4):
            C("act", lambda e, tb=tb: e.activation(out=junkc[:], in_=hp1[:, tb, :], func=AF.Square, accum_out=ssd[:, tb:tb + 1]), (("hp1", tb),), ("junkc", ("ssd", tb)))
        rstd_ops(ssd[:], rsd[:], 4, [("ssd", tb) for tb in range(4)], "rsd", 1.0 / D)
        for tb in range(4):
            ys = tb % 2
            C("dve", lambda e, tb=tb, ys=ys: e.scalar_tensor_tensor(out=yout[ys][:], in0=hp1[:, tb, :], scalar=rsd[:, tb:tb + 1], in1=nfin_bc[:], op0=ALU.mult, op1=ALU.mult),
              (("hp1", tb), "rsd", "nfin_bc"), (("yout", 0),))
            P.dma("sp", lambda e, tt=tt, tb=tb, ys=ys: e.dma_start(out=y_p[tt * TT + tb * 128: tt * TT + (tb + 1) * 128, :], in_=yout[ys][:]), K(("yout", 0)), (("yout", 0),), ())
    peak_c = A.mark()
    if not with_sample or not DBG_S:
        return nc, P, A, dma_keys, dict(peak_ab=peak_ab, peak_c=peak_c)
    P.barrier()
    A.reset(mark_s)
    xs_d = dti("xs", [NS, D])
    w_in_d = dti("w_in_s", [D, 3584])
    lbp_d = dti("lbp", [2, 512])
    sel_d = dti("sel", [NS, NS, 128])
    rgc_d = dti("rgn_col", [128, 1]); asc_d = dti("asub_col", [128, 1])
    ptl_d = dti("ptl", [8, 2 * NS], I32)
    sub_d = dti("subi", [128, 1], I32)
    ck_d = dti("cache_k", [NPHYS * 16, 4096])
    cv_d = dti("cache_v", [NPHYS * 16, 4096])
    st_d = dti("state", [NS, 4, 128, 128])
    y_s = dto("y_s", [NS, D]); k_s = dto("k_s", [NS, 512]); v_s = dto("v_s", [NS, 512])
    s_s = dto("s_s", [NS, 4, 128, 128])

    xs_t = A([NS, D], F32); xn_s = A([NS, D], BF16)
    ss_s = A([NS, 2], F32); rs_s = A([NS, 2], F32)
    lbp = A([NS, 2, 512], F32); oml_t = A([NS, 512], F32)
    lbd_t = lbp[:, 0, :]; lb_t = lbp[:, 1, :]
    sel = A([NS, NS, 128], F32)
    rgn_col = A([128, 1], F32); asub_col = A([128, 1], F32)
    ones_f = A([128, 128], F32)
    pt_raw = A([128, 2 * NS], I32); subi = A([128, 1], I32); idx_t = A([128, 2 * NS], I32)
    hT_s = A([128, 8 * NS], BF16)
    m_wS = A.mark()
    wS = [A([128, 8, 512], BF16)] * 2
    p_tok = A([NS, 7, 512], F32)
    m_ft = A.mark()
    f_t = A([NS, 512], F32); kk_t = A([NS, 512], F32); q_t = A([NS, 512], F32); g_t = A([NS, 512], F32)
    sig_t = f_t
    featT = A([128, 4, 64], F32)
    m_Sin = A.mark()
    S_in = A([128, NS, 128], F32); S_out = S_in
    KKbd = A([NS, NS, 128], F32)
    orT = A([128, 64], F32); sq_s = A([128, 64], F32); rstd_bc = A([128, 64], F32); tmp64 = A([128, 64], F32)
    mixT_s = A([128, 128], BF16)
    prod_s = q_t; s_self = A([NS, 8], F32); p_self = A([NS, 8], F32); Pbd = A([NS, NS, 8], F32)
    qb_s = A([128, 512], F32)
    Kb = A([128, 4096], F32); Vb = A([128, 4096], F32)
    Kbs = [Kb, A.at(m_Sin, [128, 4096], F32)]
    Vbbs = [A.at(m_wS, [128, 4096], BF16), A.at(m_ft, [128, 4096], BF16)]
    Kkeys = [("Kb",), ("Kb1", "S_in", "KKbd")]
    Vkeys = [("Vbb0", ("wS", 0)), ("Vbb1", "f_t", "kk_t", "q_t", "g_t")]
    sc_s = A([128, 64], F32); pexp = A([128, 64], BF16); ps8 = A([128, 8], F32)
    OT = A([128, 128], F32); Lc = A([128, 128], F32); psb = A([128, 128], F32); tmpO = A([128, 128], F32); Rl = A([128, 128], F32)
    dif_s = A([128, 64], F32); sqa = A([128, 64], F32); rstd_a = A([128, 64], F32); tmpa = A([128, 64], F32)
    hs1 = A([NS, D], F32); hn_s = A([NS, D], BF16); junks = hn_s; h2T_s = A([128, 8 * NS], BF16)
    wdn = [A([128, 2, D], BF16) for _ in range(2)]
    sgs = A([128, NS], F32); aT_s = A([128, NFC * NS], BF16)
    ys_t = hs1

    P.dma("sp", lambda e: e.dma_start(out=xs_t[:], in_=xs_d), K("s_xs"), (), ("xs_t",))
    P.dma("sp", lambda e: e.dma_start(out=lbp[:], in_=lbp_d.rearrange("(o a) n -> o a n", o=1).to_broadcast([NS, 2, 512])), K("s_lbp"), (), ("lbp",))
    P.dma("sp", lambda e: e.dma_start(out=sel[:], in_=sel_d), K("s_sel"), (), ("sel",))
    P.dma("sp", lambda e: e.dma_start(out=rgn_col[:], in_=rgc_d), K("s_rgc"), (), ("rgn_col",))
    P.dma("sp", lambda e: e.dma_start(out=asub_col[:], in_=asc_d), K("s_asc"), (), ("asub_col",))
    P.dma("sp", lambda e: e.dma_start(out=subi[:], in_=sub_d), K("s_sub"), (), ("subi",))
    P.dma("sp", lambda e: e.dma_start(out=pt_raw[:], in_=bass.AP(tensor=ptl_d.tensor, offset=0, ap=[[2 * NS, 8], [0, 16], [1, 2 * NS]])), K("s_pt"), (), ("pt_raw",))
    C("dve", lambda e: e.memset(ones_f[:], 1.0), (), ("ones_f",))
    C("dve", lambda e: e.tensor_scalar(out=asub_col[:], in0=asub_col[:], scalar1=1.0 - LAM_INIT, scalar2=None, op0=ALU.mult), ("asub_col",), ("asub_col",))
    for cidx in range(2 * NS):
        pass
    C("dve", lambda e: e.scalar_tensor_tensor(out=idx_t[:], in0=pt_raw[:], scalar=16, in1=subi[:, 0:1].to_broadcast([128, 2 * NS]), op0=ALU.mult, op1=ALU.add), ("pt_raw", "subi"), ("idx_t",))
    C("dve", lambda e: e.tensor_tensor(out=lbd_t, in0=lbp[:, 0, :], in1=lbp[:, 1, :], op=ALU.subtract), ("lbp",), ("lbp",))
    C("act", lambda e: e.activation(out=oml_t[:], in_=lbd_t, func=AF.Sigmoid, scale=-1.0), ("lbp",), ("oml_t",))
    C("act", lambda e: e.activation(out=lb_t, in_=lbd_t, func=AF.Sigmoid), ("lbp", "oml_t"), ("lbp",))
    C("dve", lambda e: e.memset(PS[1][:], 0.0), (), ("psL",), (1,))
    C("dve", lambda e: e.memset(PS[2][:], 0.0), (), ("psO",), (2,))

    def rstd16(ssum, out, rkey, wkey, inv):
        C("act", lambda e: e.activation(out=out, in_=ssum, func=AF.Ln, bias=epsc[0:NS, 0:1], scale=inv), (rkey, "epsc"), (wkey,))
        C("act", lambda e: e.activation(out=out, in_=out, func=AF.Exp, scale=-0.5), (wkey,), (wkey,))

    C("act", lambda e: e.activation(out=junks[:], in_=xs_t[:], func=AF.Square, accum_out=ss_s[:, 0:1]), ("xs_t",), ("hn_s", "ss_s0"))
    rstd16(ss_s[:, 0:1], rs_s[:, 0:1], "ss_s0", "rs_s0", 1.0 / D)
    C("dve", lambda e: e.scalar_tensor_tensor(out=xn_s[:], in0=xs_t[:], scalar=rs_s[:, 0:1], in1=nmix_bc[0:NS, :], op0=ALU.mult, op1=ALU.mult), ("xs_t", "rs_s0", "nmix_bc"), ("xn_s",))
    for dc in range(8):
        C("pe", lambda e, dc=dc: e.matmul(out=PS[7][:, dc * NS:(dc + 1) * NS], lhsT=xn_s[:, dc * 128:(dc + 1) * 128], rhs=ident[0:NS, 0:NS], start=True, stop=True),
          ("xn_s", "ident"), ("ps7s",), (7,))
    C("act", lambda e: e.activation(out=hT_s[:], in_=PS[7][:, 0:8 * NS], func=AF.Copy), ("ps7s",), ("hT_s",), (7,))
    for g in range(7):
        ws = g % 2
        P.dma("pool", lambda e, g=g, ws=ws: e.dma_start(out=wS[ws][:], in_=w_in_d[:, g * 512:(g + 1) * 512].rearrange("(kc p) n -> p kc n", p=128)), K(("wS", 0)), (), (("wS", 0),))
        bk = 5 + g % 2
        for dc in range(8):
            C("pe", lambda e, dc=dc, ws=ws, bk=bk: e.matmul(out=PS[bk][0:NS, :], lhsT=hT_s[:, dc * NS:(dc + 1) * NS], rhs=wS[ws][:, dc, :], start=(dc == 0), stop=(dc == 7)),
              ("hT_s", ("wS", 0)), (("psp", bk),), (bk,))
        C("dve", lambda e, g=g, bk=bk: e.tensor_scalar(out=p_tok[:, g, :], in0=PS[bk][0:NS, :], scalar1=1.0, scalar2=None, op0=ALU.mult), (("psp", bk),), (("p_tok", g),), (bk,))
    P.dma("sp", lambda e: e.dma_start(out=k_s, in_=p_tok[:, 5, :]), K("s_ks"), (("p_tok", 5),), ())
    P.dma("sp", lambda e: e.dma_start(out=v_s, in_=p_tok[:, 6, :]), K("s_vs"), (("p_tok", 6),), ())
    C("act", lambda e: e.activation(out=sig_t[:], in_=p_tok[:, 1, :], func=AF.Sigmoid), (("p_tok", 1),), ("f_t",))
    C("dve", lambda e: e.tensor_tensor(out=f_t[:], in0=sig_t[:], in1=oml_t[:], op=ALU.mult), ("f_t", "oml_t"), ("f_t",))
    C("dve", lambda e: e.tensor_tensor(out=f_t[:], in0=f_t[:], in1=lb_t, op=ALU.add), ("f_t", "lbp"), ("f_t",))
    C("dve", lambda e: e.tensor_scalar(out=kk_t[:], in0=f_t[:], scalar1=-1.0, scalar2=1.0, op0=ALU.mult, op1=ALU.add), ("f_t",), ("kk_t",))
    C("act", lambda e: e.activation(out=q_t[:], in_=p_tok[:, 0, :], func=AF.Silu), (("p_tok", 0),), ("q_t",))
    C("act", lambda e: e.activation(out=g_t[:], in_=p_tok[:, 3, :], func=AF.Silu), (("p_tok", 3),), ("g_t",))
    srcs_fm = [(f_t, "f_t", None), (q_t, "q_t", None), (g_t, "g_t", None), (p_tok, ("p_tok", 6), 6)]
    for xi, (xt_, xk, gsel) in enumerate(srcs_fm):
        for h in range(4):
            src_ap = (xt_[:, h * 128:(h + 1) * 128] if gsel is None else xt_[:, gsel, h * 128:(h + 1) * 128])
            C("pe", lambda e, xi=xi, h=h, src_ap=src_ap: e.matmul(out=PS[7][:, 128 + xi * 64 + h * NS: 128 + xi * 64 + (h + 1) * NS], lhsT=src_ap, rhs=ident_f[0:NS, 0:NS], start=True, stop=True),
              (xk, "ident_f"), ("ps7f",), (7,))
    C("act", lambda e: e.activation(out=featT[:], in_=PS[7][:, 128:384].rearrange("p (a b) -> p a b", a=4), func=AF.Copy), ("ps7f",), ("featT",), (7,))
    fT, qT, gT, avT = featT[:, 0, :], featT[:, 1, :], featT[:, 2, :], featT[:, 3, :]
    for h in range(4):
        P.dma("sp", lambda e, h=h: e.dma_start(out=S_in[:], in_=st_d[:, h, :, :].rearrange("b k v -> k b v")), K("s_sin"), (), ("S_in",))
        C("dve", lambda e, h=h: e.tensor_tensor(out=KKbd[:], in0=kk_t[:, h * 128:(h + 1) * 128].unsqueeze(1).to_broadcast([NS, NS, 128]), in1=sel[:], op=ALU.mult), ("kk_t", "sel"), ("KKbd",))
        for b in range(NS):
            bk = 4 + b % 2
            col = h * NS + b
            C("pe", lambda e, b=b, h=h, bk=bk: e.matmul(out=PS[bk][:, 0:128], lhsT=KKbd[:, b, :], rhs=p_tok[:, 2, h * 128:(h + 1) * 128], start=True, stop=True),
              ("KKbd", ("p_tok", 2)), (("pso", bk),), (bk,))
            C("dve", lambda e, b=b, bk=bk, col=col: e.scalar_tensor_tensor(out=S_out[:, b, :], in0=S_in[:, b, :], scalar=fT[:, col:col + 1], in1=PS[bk][:, 0:128], op0=ALU.mult, op1=ALU.add),
              ("S_in", "featT", ("pso", bk)), ("S_in",), (bk,))
            C("pe", lambda e, b=b, col=col: e.matmul(out=PS[6][:, col:col + 1], lhsT=S_out[:, b, :], rhs=qT[:, col:col + 1], start=True, stop=True),
              ("S_in", "featT"), ("ps6o",), (6,))
        P.dma("sp", lambda e, h=h: e.dma_start(out=s_s[:, h, :, :].rearrange("b k v -> k b v"), in_=S_out[:]), K("s_sout"), ("S_in",), ())
    C("act", lambda e: e.activation(out=orT[:], in_=PS[6][:, 0:64], func=AF.Copy), ("ps6o",), ("orT",), (6,))
    C("dve", lambda e: e.tensor_tensor(out=sq_s[:], in0=orT[:], in1=orT[:], op=ALU.mult), ("orT",), ("sq_s",))
    C("pe", lambda e: e.matmul(out=PS[6][:, 64:128], lhsT=ones_f[:], rhs=sq_s[:], start=True, stop=True), ("ones_f", "sq_s"), ("ps6q",), (6,))
    C("act", lambda e: e.activation(out=rstd_bc[:], in_=PS[6][:, 64:128], func=AF.Ln, bias=epsc[:, 0:1], scale=1.0 / 128), ("ps6q", "epsc"), ("rstd_bc",), (6,))
    C("act", lambda e: e.activation(out=rstd_bc[:], in_=rstd_bc[:], func=AF.Exp, scale=-0.5), ("rstd_bc",), ("rstd_bc",))
    C("dve", lambda e: e.tensor_tensor(out=tmp64[:], in0=orT[:], in1=rstd_bc[:], op=ALU.mult), ("orT", "rstd_bc"), ("tmp64",))
    C("dve", lambda e: e.scalar_tensor_tensor(out=mixT_s[:, 0:64], in0=tmp64[:], scalar=rgn_col[:, 0:1], in1=gT, op0=ALU.mult, op1=ALU.mult), ("tmp64", "rgn_col", "featT"), ("mixT_r",))
    C("dve", lambda e: e.tensor_tensor(out=prod_s[:], in0=p_tok[:, 4, :], in1=p_tok[:, 5, :], op=ALU.mult), (("p_tok", 4), ("p_tok", 5)), ("q_t",))
    C("dve", lambda e: e.tensor_reduce(out=s_self[:], in_=prod_s[:].rearrange("p (a d) -> p a d", d=64), axis=AX.X, op=ALU.add), ("q_t",), ("s_self",))
    C("act", lambda e: e.activation(out=p_self[:], in_=s_self[:], func=AF.Exp, scale=0.125), ("s_self",), ("p_self",))
    C("dve", lambda e: e.tensor_tensor(out=Pbd[:], in0=p_self[:].unsqueeze(1).to_broadcast([NS, NS, 8]), in1=sel[:, :, 0:8], op=ALU.mult), ("p_self", "sel"), ("Pbd",))
    steps = [(b, half) for b in range(NS) for half in range(2)]

    def gathers(i):
        b, half = steps[i]
        ci = 2 * b + half
        kb_, kk_ = Kbs[i % 2], Kkeys[i % 2]
        P.dma("pool", lambda e: e.indirect_dma_start(out=kb_[:], out_offset=None, in_=ck_d, in_offset=bass.IndirectOffsetOnAxis(ap=idx_t[:, ci:ci + 1], axis=0)),
              K(("s_kb", i % 2)), ("idx_t",), kk_)
        P.dma("pool", lambda e: e.indirect_dma_start(out=Vb[:], out_offset=None, in_=cv_d, in_offset=bass.IndirectOffsetOnAxis(ap=idx_t[:, ci:ci + 1], axis=0)),
              K("s_vb"), ("idx_t",), ("Vb",))

    gathers(0)
    for i, (b, half) in enumerate(steps):
        kb_, kk_ = Kbs[i % 2], Kkeys[i % 2]
        vb_, vk_ = Vbbs[i % 2], Vkeys[i % 2]
        kb3 = kb_[:].rearrange("p (j c) -> p j c", j=8)
        vb3 = vb_[:].rearrange("p (j c) -> p j c", j=8)
        if half == 0:
            C("pe", lambda e, b=b: e.matmul(out=PS[0][:], lhsT=sel[:, b, :], rhs=p_tok[:, 4, :], start=True, stop=True), ("sel", ("p_tok", 4)), ("ps0q",), (0,))
            C("act", lambda e: e.activation(out=qb_s[:], in_=PS[0][:], func=AF.Copy), ("ps0q",), ("qb_s",), (0,))
        C("act", lambda e, vb_=vb_: e.activation(out=vb_[:], in_=Vb[:], func=AF.Copy), ("Vb",), vk_)
        if i + 1 < len(steps):
            gathers(i + 1)
        C("dve", lambda e, kb3=kb3: e.tensor_tensor(out=kb3, in0=kb3, in1=qb_s[:].unsqueeze(1).to_broadcast([128, 8, 512]), op=ALU.mult), kk_ + ("qb_s",), kk_)
        C("dve", lambda e, kb_=kb_: e.tensor_reduce(out=sc_s[:], in_=kb_[:].rearrange("p (a d) -> p a d", d=64), axis=AX.X, op=ALU.add), kk_, ("sc_s",))
        C("act", lambda e: e.activation(out=pexp[:], in_=sc_s[:], func=AF.Exp, scale=0.125), ("sc_s",), ("pexp",))
        C("dve", lambda e: e.tensor_reduce(out=ps8[:], in_=pexp[:].rearrange("p (j c) -> p c j", j=8), axis=AX.X, op=ALU.add), ("pexp",), ("ps8",))
        C("pe", lambda e, b=b, half=half: e.matmul(out=PS[1][:, b * 8:(b + 1) * 8], lhsT=ones_f[:], rhs=ps8[:], start=False, stop=(half == 1), skip_group_check=True),
          ("ones_f", "ps8", "psL"), ("psL",), (1,))
        for h in range(4):
            for jk in range(8):
                C("pe", lambda e, b=b, h=h, jk=jk, half=half, vb3=vb3: e.matmul(out=PS[2][:, b * 8 + 2 * h: b * 8 + 2 * h + 2], lhsT=vb3[:, jk, h * 128:(h + 1) * 128],
                                                                          rhs=pexp[:, jk * 8 + 2 * h: jk * 8 + 2 * h + 2], start=False, stop=(half == 1 and jk == 7), skip_group_check=True),
                  vk_ + ("pexp", "psO"), ("psO",), (2,))
    C("act", lambda e: e.activation(out=OT[:], in_=PS[2][:, 0:128], func=AF.Copy), ("psO",), ("OT",), (2,))
    C("act", lambda e: e.activation(out=Lc[:], in_=PS[1][:, 0:128], func=AF.Copy), ("psL",), ("Lc",), (1,))
    C("pe", lambda e: e.matmul(out=PS[3][:, 0:128], lhsT=ones_f[0:NS, :], rhs=Pbd[:].rearrange("p b c -> p (b c)"), start=True, stop=True), ("ones_f", "Pbd"), ("ps3p",), (3,))
    C("act", lambda e: e.activation(out=psb[:], in_=PS[3][:, 0:128], func=AF.Copy), ("ps3p",), ("psb",), (3,))
    for m_ in range(2):
        C("dve", lambda e, m_=m_: e.tensor_tensor(out=tmpO[:, m_:128:2].rearrange("p (b h) -> p b h", h=4), in0=psb[:, m_:128:2].rearrange("p (b h) -> p b h", h=4),
                                                 in1=avT.rearrange("p (h b) -> p b h", h=4), op=ALU.mult), ("psb", "featT"), (("tmpO", m_),))
    C("dve", lambda e: e.tensor_tensor(out=OT[:], in0=OT[:], in1=tmpO[:], op=ALU.add), ("OT", ("tmpO", 0), ("tmpO", 1)), ("OT",))
    C("dve", lambda e: e.tensor_tensor(out=Lc[:], in0=Lc[:], in1=psb[:], op=ALU.add), ("Lc", "psb"), ("Lc",))
    C("dve", lambda e: e.reciprocal(out=Rl[:], in_=Lc[:]), ("Lc",), ("Rl",))
    C("dve", lambda e: e.tensor_tensor(out=OT[:], in0=OT[:], in1=Rl[:], op=ALU.mult), ("OT", "Rl"), ("OT",))
    C("dve", lambda e: e.scalar_tensor_tensor(out=dif_s[:], in0=OT[:, 1:128:2], scalar=nlam[:, 0:1], in1=OT[:, 0:128:2], op0=ALU.mult, op1=ALU.add), ("OT", "nlam"), ("dif_s",))
    C("dve", lambda e: e.tensor_tensor(out=sqa[:], in0=dif_s[:], in1=dif_s[:], op=ALU.mult), ("dif_s",), ("sqa",))
    C("pe", lambda e: e.matmul(out=PS[3][:, 128:192], lhsT=ones_f[:], rhs=sqa[:], start=True, stop=True), ("ones_f", "sqa"), ("ps3q",), (3,))
    C("act", lambda e: e.activation(out=rstd_a[:], in_=PS[3][:, 128:192], func=AF.Ln, bias=epsc[:, 0:1], scale=1.0 / 128), ("ps3q", "epsc"), ("rstd_a",), (3,))
    C("act", lambda e: e.activation(out=rstd_a[:], in_=rstd_a[:], func=AF.Exp, scale=-0.5), ("rstd_a",), ("rstd_a",))
    C("dve", lambda e: e.tensor_tensor(out=tmpa[:], in0=dif_s[:], in1=rstd_a[:], op=ALU.mult), ("dif_s", "rstd_a"), ("tmpa",))
    C("dve", lambda e: e.tensor_scalar(out=mixT_s[:, 64:128].rearrange("p (h b) -> p b h", h=4), in0=tmpa[:].rearrange("p (b h) -> p b h", h=4), scalar1=asub_col[:, 0:1], scalar2=None, op0=ALU.mult),
      ("tmpa", "asub_col"), ("mixT_a",))
    for nh in range(2):
        for ec in range(8):
            C("pe", lambda e, nh=nh, ec=ec: e.matmul(out=PS[4 + nh][0:NS, :], lhsT=mixT_s[:, ec * NS:(ec + 1) * NS], rhs=wo[:, ec, nh * 512:(nh + 1) * 512], start=(ec == 0), stop=(ec == 7)),
              ("mixT_r", "mixT_a", "wo"), (("psy", nh),), (4 + nh,))
        C("dve", lambda e, nh=nh: e.tensor_tensor(out=hs1[:, nh * 512:(nh + 1) * 512], in0=xs_t[:, nh * 512:(nh + 1) * 512], in1=PS[4 + nh][0:NS, :], op=ALU.add), ("xs_t", ("psy", nh)), (("hs1", nh),), (4 + nh,))
    C("act", lambda e: e.activation(out=junks[:], in_=hs1[:], func=AF.Square, accum_out=ss_s[:, 1:2]), (("hs1", 0), ("hs1", 1)), ("hn_s", "ss_s1"))
    rstd16(ss_s[:, 1:2], rs_s[:, 1:2], "ss_s1", "rs_s1", 1.0 / D)
    C("dve", lambda e: e.scalar_tensor_tensor(out=hn_s[:], in0=hs1[:], scalar=rs_s[:, 1:2], in1=nffn_bc[0:NS, :], op0=ALU.mult, op1=ALU.mult), (("hs1", 0), ("hs1", 1), "rs_s1", "nffn_bc"), ("hn_s",))
    for dc in range(8):
        C("pe", lambda e, dc=dc: e.matmul(out=PS[7][:, dc * NS:(dc + 1) * NS], lhsT=hn_s[:, dc * 128:(dc + 1) * 128], rhs=ident[0:NS, 0:NS], start=True, stop=True),
          ("hn_s", "ident"), ("ps7s",), (7,))
    C("act", lambda e: e.activation(out=h2T_s[:], in_=PS[7][:, 0:8 * NS], func=AF.Copy), ("ps7s",), ("h2T_s",), (7,))
    for fg in range(11):
        ws = wctr[0] % 2; wctr[0] += 1
        P.dma("pool", lambda e, fg=fg, ws=ws: e.dma_start(out=wg[ws][:], in_=w_gate[fg]), K(("wg", ws)), (), (("wg", ws),))
        P.dma("pool", lambda e, fg=fg, ws=ws: e.dma_start(out=wu[ws][:], in_=w_up[fg]), K(("wu", ws)), (), (("wu", ws),))
        P.dma("pool", lambda e, fg=fg, ws=ws: e.dma_start(out=wdn[ws][:], in_=w_down[fg * 256:(fg + 1) * 256, :].rearrange("(kc p) n -> p kc n", p=128)), K(("wdn", ws)), (), (("wdn", ws),))
        for fl in range(2):
            fc = fg * 2 + fl
            for dc in range(8):
                C("pe", lambda e, dc=dc, fl=fl, ws=ws: e.matmul(out=PS[6][:, 0:NS], lhsT=wg[ws][:, dc, fl * 128:(fl + 1) * 128], rhs=h2T_s[:, dc * NS:(dc + 1) * NS], start=(dc == 0), stop=(dc == 7)),
                  ("h2T_s", ("wg", ws)), ("ps6g",), (6,))
            C("act", lambda e: e.activation(out=sgs[:], in_=PS[6][:, 0:NS], func=AF.Silu), ("ps6g",), ("sgs",), (6,))
            for dc in range(8):
                C("pe", lambda e, dc=dc, fl=fl, ws=ws: e.matmul(out=PS[6][:, NS:2 * NS], lhsT=wu[ws][:, dc, fl * 128:(fl + 1) * 128], rhs=h2T_s[:, dc * NS:(dc + 1) * NS], start=(dc == 0), stop=(dc == 7)),
                  ("h2T_s", ("wu", ws)), ("ps6u",), (6,))
            C("dve", lambda e, fc=fc: e.tensor_tensor(out=aT_s[:, fc * NS:(fc + 1) * NS], in0=sgs[:], in1=PS[6][:, NS:2 * NS], op=ALU.mult), ("sgs", "ps6u"), (("aT_s", fc),), (6,))
            for nh in range(2):
                C("pe", lambda e, fc=fc, fl=fl, nh=nh, ws=ws: e.matmul(out=PS[4 + nh][0:NS, :], lhsT=aT_s[:, fc * NS:(fc + 1) * NS], rhs=wdn[ws][:, fl, nh * 512:(nh + 1) * 512], start=(fc == 0), stop=(fc == NFC - 1)),
                  (("aT_s", fc), ("wdn", ws)), (("psy", nh),), (4 + nh,))
    for nh in range(2):
        C("dve", lambda e, nh=nh: e.tensor_tensor(out=hs1[:, nh * 512:(nh + 1) * 512], in0=hs1[:, nh * 512:(nh + 1) * 512], in1=PS[4 + nh][0:NS, :], op=ALU.add), (("hs1", nh), ("psy", nh)), (("hs1", nh),), (4 + nh,))
    C("act", lambda e: e.activation(out=junks[:], in_=hs1[:], func=AF.Square, accum_out=ss_s[:, 0:1]), (("hs1", 0), ("hs1", 1)), ("hn_s", "ss_s0"))
    rstd16(ss_s[:, 0:1], rs_s[:, 0:1], "ss_s0", "rs_s0", 1.0 / D)
    C("dve", lambda e: e.scalar_tensor_tensor(out=ys_t[:], in0=hs1[:], scalar=rs_s[:, 0:1], in1=nfin_bc[0:NS, :], op0=ALU.mult, op1=ALU.mult), (("hs1", 0), ("hs1", 1), "rs_s0", "nfin_bc"), (("hs1", 0), ("hs1", 1)))
    P.dma("sp", lambda e: e.dma_start(out=y_s, in_=ys_t[:]), K("s_ys"), (("hs1", 0), ("hs1", 1)), ())
    peak_s = A.mark()
    return nc, P, A, dma_keys, dict(peak_ab=peak_ab, peak_c=peak_c, peak_s=peak_s)


_CACHE = {}


def _get_prog():
    if "nc" in _CACHE:
        return _CACHE["nc"]
    nc, P, A, keys, info = build()
    import contextlib
    with contextlib.ExitStack() as st:
        sems = {e: st.enter_context(nc.semaphore("eng_" + e)) for e in Prog.ENGS}
        dsems = {k: st.enter_context(nc.semaphore("d%d" % i)) for i, k in enumerate(keys)}
        block = st.enter_context(nc.Block())
        P.emit(nc, block, sems, dsems)
    _CACHE["nc"] = nc
    _CACHE["info"] = (info, P.n_instr, len(keys))
    return nc


def kernel(x_prompt, x_sample, cache_k, cache_v, state_hgrn, page_table, w_in, w_out, lb_param, r_gnorm,
           lam_q1, lam_k1, lam_q2, lam_k2, a_subln, norm_mix, norm_ffn, w_gate, w_up, w_down, norm_final):
    f = lambda a: np.ascontiguousarray(np.asarray(a, dtype=np.float32))
    x_prompt = f(x_prompt); w_in0 = f(w_in)[0]
    nc = _get_prog()
    ident = np.eye(128, dtype=np.float32)
    tri = np.triu(np.ones((128, 128), np.float32))
    mT = np.tile(np.triu(np.ones((64, 64), np.float32)), (2, 1))
    scm = np.ones((128, 512), np.float32); scm[:, ::64] = 0.0
    lamv = np.stack([f(lam_q1)[0], f(lam_k1)[0], f(lam_q2)[0], f(lam_k2)[0]])
    in_maps = []
    relay = lambda w: np.ascontiguousarray(f(w)[0].reshape(8, 128, 11, 256).transpose(2, 1, 0, 3))
    wg_l, wu_l = relay(w_gate), relay(w_up)
    x_sample = f(x_sample); page_table = np.asarray(page_table, dtype=np.int32); state_hgrn = f(state_hgrn)
    ck2 = f(cache_k)[0].reshape(NPHYS * 16, 4096)
    cv2 = f(cache_v)[0].reshape(NPHYS * 16, 4096)
    sel = np.zeros((NS, NS, 128), np.float32)
    for b in range(NS):
        sel[b, b, :] = 1.0
    subi = (np.arange(128) % 16).astype(np.int32).reshape(128, 1)
    for c in range(NCORES):
        s, h = c // 4, c % 4
        col = lambda g: w_in0[:, g * 512 + h * 128: g * 512 + (h + 1) * 128]
        w_fm = np.ascontiguousarray(np.concatenate([col(0), col(1), col(4), col(5)], axis=1))
        w_tm = np.ascontiguousarray(np.concatenate([col(2), col(3), col(5), col(6)], axis=1))
        gidx = np.zeros((128, 64), np.int32)
        for tt in range(4):
            for tb in range(4):
                for r in range(4):
                    gidx[:, (tt * 4 + tb) * 4 + r] = r * 2048 + 512 * h + tb * 128 + np.arange(128)
        in_maps.append(dict(
            xp=x_prompt[s], w_fm=w_fm, w_tm=w_tm, w_out=f(w_out)[0], w_gate=wg_l, w_up=wu_l, w_down=f(w_down)[0],
            lbh=np.ascontiguousarray(f(lb_param)[:, h * 128:(h + 1) * 128].T), rgn=f(r_gnorm), lamv=lamv, asub=f(a_subln),
            nmix=f(norm_mix), nffn=f(norm_ffn), nfin=f(norm_final).reshape(1, D), ident=ident, tri=tri, mT=mT, scm=scm, gidx=gidx,
            xs=np.ascontiguousarray(x_sample[NS * c:NS * (c + 1), 0, :]), w_in_s=w_in0, lbp=f(lb_param), sel=sel,
            rgn_col=np.ascontiguousarray(f(r_gnorm)[0].reshape(128, 1)), asub_col=np.ascontiguousarray(f(a_subln)[0].reshape(128, 1)),
            ptl=np.ascontiguousarray(page_table[NS * c:NS * (c + 1)].reshape(NS, 2, 8).transpose(2, 0, 1).reshape(8, 2 * NS)),
            subi=subi, cache_k=ck2, cache_v=cv2, state=np.ascontiguousarray(state_hgrn[0, NS * c:NS * (c + 1)]),
            xres=np.ascontiguousarray(np.concatenate([x_prompt[s, 2048 * q + 512 * h: 2048 * q + 512 * (h + 1)] for q in range(4)], axis=0)),
        ))
    res = run_bass_kernel_spmd(nc, in_maps, core_ids=list(range(NCORES)))
    R = res.results
    y_prompt = np.zeros((2, T, D), np.float32)
    k_prompt = np.zeros((1, 2, T, 4, 2, 64), np.float32)
    v_prompt = np.zeros((1, 2, T, 4, 128), np.float32)
    s_prompt = np.zeros((1, 2, 4, 128, 128), np.float32)
    for c in range(NCORES):
        s, h = c // 4, c % 4
        for q in range(4):
            y_prompt[s, 2048 * q + 512 * h: 2048 * q + 512 * (h + 1)] = R[c]["y_p"][512 * q: 512 * (q + 1)]
        k_prompt[0, s, :, h] = R[c]["k_p"].reshape(T, 2, 64)
        v_prompt[0, s, :, h] = R[c]["v_p"]
        s_prompt[0, s, h] = R[c]["s_p"]
    y_sample = np.zeros((128, 1, D), np.float32)
    k_sample = np.zeros((1, 128, 1, 4, 2, 64), np.float32)
    v_sample = np.zeros((1, 128, 1, 4, 128), np.float32)
    s_sample = np.zeros((1, 128, 4, 128, 128), np.float32)
    for c in range(NCORES):
        if "y_s" not in R[c]:
            break
        y_sample[NS * c:NS * (c + 1), 0] = R[c]["y_s"]
        k_sample[0, NS * c:NS * (c + 1), 0] = R[c]["k_s"].reshape(NS, 4, 2, 64)
        v_sample[0, NS * c:NS * (c + 1), 0] = R[c]["v_s"].reshape(NS, 4, 128)
        s_sample[0, NS * c:NS * (c + 1)] = R[c]["s_s"]
    return (y_prompt, y_sample, k_prompt, v_prompt, s_prompt, k_sample, v_sample, s_sample)
```
